# Optimizing a Trainium2 kernel written in Bass

```python
import jax, jax.numpy as jnp
from jax import lax
import numpy as np

D_MODEL = 1024
BATCH = 16
SEQ = 256
DEPTH = 4
DEC_BATCH = 8
DEC_SEQ = 1024
PAST_LEN = 512

GRID_W = 64
N_REC = (DEPTH + 1) // 2
N_ATT = DEPTH // 2
POOL_WIDTH = D_MODEL // 2
POOL_WINDOWS = (2, 4, 8, 16)
POOL_GROUPS = len(POOL_WINDOWS)
POOL_GROUP_DIM = POOL_WIDTH // POOL_GROUPS
REC_WIDTH = D_MODEL // 2
REC_HEAD_DIM = 128
REC_HEADS = REC_WIDTH // REC_HEAD_DIM
CHUNK = 16
REC_IN_WIDTH = 2 * POOL_WIDTH + 5 * REC_WIDTH
REC_OUT_IN = POOL_WIDTH + REC_WIDTH
ATT_HEAD_DIM = 128
ATT_HEADS = D_MODEL // ATT_HEAD_DIM
ATT_KV_HEADS = 2
ATT_WIDTH = ATT_HEADS * ATT_HEAD_DIM
KV_WIDTH = ATT_KV_HEADS * ATT_HEAD_DIM
ATT_IN_WIDTH = 2 * ATT_WIDTH + 2 * KV_WIDTH
AXIS_DIM = ATT_HEAD_DIM // 2
Q_BLOCK = 128
ROPE_THETA = 10000.0
EPS = 1e-6
F_MIN = 1e-6

kernel_name = "hybrid_pool_hgrn2_gqa_prefix_diffusion_step"


def rmsnorm(x, g):
    xf = x.astype(jnp.float32)
    y = xf * lax.rsqrt(jnp.mean(xf * xf, axis=-1, keepdims=True) + EPS)
    return (y * g.astype(jnp.float32)).astype(x.dtype)


def modulation(cvec, w, b):
    m = jax.nn.silu(cvec) @ w + b
    shift, scale, gate = jnp.split(m, 3, axis=-1)
    return shift[:, None, :], scale[:, None, :], gate[:, None, :]


def axial_rope(n_tokens):
    rows = n_tokens // GRID_W
    t = jnp.arange(rows * GRID_W)
    row = (t // GRID_W).astype(jnp.float32)
    col = (t % GRID_W).astype(jnp.float32)
    inv = ROPE_THETA ** (-jnp.arange(0, AXIS_DIM, 2, dtype=jnp.float32) / AXIS_DIM)
    ang = jnp.concatenate([row[:, None] * inv[None, :], col[:, None] * inv[None, :]], axis=-1)
    return jnp.cos(ang)[None, :, None, :], jnp.sin(ang)[None, :, None, :]


def apply_rope(x, cos, sin):
    xf = x.astype(jnp.float32)
    x1, x2 = xf[..., 0::2], xf[..., 1::2]
    y = jnp.stack([x1 * cos - x2 * sin, x1 * sin + x2 * cos], axis=-1).reshape(x.shape)
    return y.astype(x.dtype)


def pool_mixer(u, w, scale):
    B, L, _ = u.shape
    uf = u.astype(jnp.float32)
    cs = jnp.concatenate([jnp.zeros((B, 1, POOL_WIDTH), jnp.float32), jnp.cumsum(uf, axis=1)], axis=1)
    t = jnp.arange(L)
    outs = []
    for g, win in enumerate(POOL_WINDOWS):
        sl = slice(g * POOL_GROUP_DIM, (g + 1) * POOL_GROUP_DIM)
        lo = jnp.clip(t - win // 2, 0, L)
        hi = jnp.clip(t + win // 2, 0, L)
        cnt = (hi - lo).astype(jnp.float32)
        mean = (cs[:, hi, sl] - cs[:, lo, sl]) / cnt[None, :, None]
        outs.append(mean - uf[..., sl])
    d = jnp.stack(outs, axis=2)
    y = jnp.einsum('blgc,gcd->blgd', d, w.astype(jnp.float32)).reshape(B, L, POOL_WIDTH)
    return (y * scale.astype(jnp.float32)).astype(u.dtype)


def hgrn2_scan(q, k, v, log_f, s0):
    B, L, H, DK = q.shape
    DV = v.shape[-1]
    n = L // CHUNK

    def chunks(a):
        return a.reshape(B, n, CHUNK, H, a.shape[-1]).transpose(1, 0, 3, 2, 4)

    qc, kc, vc, gc = chunks(q), chunks(k), chunks(v), chunks(log_f)
    b = jnp.cumsum(gc, axis=3)
    mask = jnp.tril(jnp.ones((CHUNK, CHUNK), dtype=bool))[:, :, None]
    diff = b[..., :, None, :] - b[..., None, :, :]
    decay = jnp.where(mask, jnp.exp(jnp.where(mask, diff, 0.0)), 0.0)
    a_intra = jnp.einsum('nbhtd,nbhsd,nbhtsd->nbhts', qc, kc, decay)
    o_intra = jnp.einsum('nbhts,nbhsv->nbhtv', a_intra, vc)
    q_in = qc * jnp.exp(b)
    k_out = kc * jnp.exp(b[..., -1:, :] - b)
    a_last = jnp.exp(b[..., -1, :])

    def step(S, inp):
        qi, ko, vi, al = inp
        o = jnp.einsum('bhtd,bhdv->bhtv', qi, S)
        S = al[..., None] * S + jnp.einsum('bhsd,bhsv->bhdv', ko, vi)
        return S, o

    s_final, o_inter = lax.scan(step, s0.astype(jnp.float32), (q_in, k_out, vc, a_last))
    o = (o_intra + o_inter).transpose(1, 0, 3, 2, 4).reshape(B, L, H, DV)
    return o, s_final


def rec_mixer(h, s0, w_in, lb, head_norm, p_w, p_scale, w_out):
    B, L, _ = h.shape
    P, R = POOL_WIDTH, REC_WIDTH
    z = h @ w_in
    u_pool, g_pool, q, f_fw, f_bw, i_v, g_rec = jnp.split(
        z, [P, 2 * P, 2 * P + R, 2 * P + 2 * R, 2 * P + 3 * R, 2 * P + 4 * R], axis=-1)
    y_pool = pool_mixer(u_pool, p_w, p_scale) * jax.nn.silu(g_pool)

    qf = jax.nn.silu(q.astype(jnp.float32)).reshape(B, L, REC_HEADS, REC_HEAD_DIM)
    vf = i_v.astype(jnp.float32).reshape(B, L, REC_HEADS, REC_HEAD_DIM)
    lbf = lb.astype(jnp.float32)

    def forget(zf, lower):
        f = lower + (1.0 - lower) * jax.nn.sigmoid(zf.astype(jnp.float32))
        f = jnp.clip(f, F_MIN, 1.0).reshape(B, L, REC_HEADS, REC_HEAD_DIM)
        return jnp.log(f), 1.0 - f

    lf_f, k_f = forget(f_fw, lbf[0])
    lf_b, k_b = forget(f_bw, lbf[1])
    o_f, s_f = hgrn2_scan(qf, k_f, vf, lf_f, s0[:, 0])
    flip = lambda a: jnp.flip(a, axis=1)
    o_b, s_b = hgrn2_scan(flip(qf), flip(k_b), flip(vf), flip(lf_b), s0[:, 1])
    o = o_f + flip(o_b)
    o = rmsnorm(o, head_norm).reshape(B, L, REC_WIDTH).astype(h.dtype) * jax.nn.silu(g_rec)
    y = jnp.concatenate([y_pool, o], axis=-1) @ w_out
    return y, jnp.stack([s_f, s_b], axis=1)


def att_project(h, w_in, q_norm, k_norm):
    B, L, _ = h.shape
    z = h @ w_in
    q, k, v, g = jnp.split(z, [ATT_WIDTH, ATT_WIDTH + KV_WIDTH, ATT_WIDTH + 2 * KV_WIDTH], axis=-1)
    q = rmsnorm(q.reshape(B, L, ATT_HEADS, ATT_HEAD_DIM), q_norm)
    k = rmsnorm(k.reshape(B, L, ATT_KV_HEADS, ATT_HEAD_DIM), k_norm)
    v = v.reshape(B, L, ATT_KV_HEADS, ATT_HEAD_DIM)
    return q, k, v, g


def attention(q, k, v):
    B, Lq, H, D = q.shape
    rep = H // ATT_KV_HEADS
    scale = D ** -0.5
    qb = q.reshape(B, Lq // Q_BLOCK, Q_BLOCK, ATT_KV_HEADS, rep, D).transpose(1, 0, 2, 3, 4, 5)
    kf = k.astype(jnp.float32)
    vf = v.astype(jnp.float32)

    def one_block(qblk):
        s = jnp.einsum('bqgrd,bkgd->bgrqk', qblk.astype(jnp.float32), kf) * scale
        p = jax.nn.softmax(s, axis=-1)
        return jnp.einsum('bgrqk,bkgd->bqgrd', p, vf)

    o = lax.map(one_block, qb)
    return o.transpose(1, 0, 2, 3, 4, 5).reshape(B, Lq, H * D).astype(q.dtype)


def setup_inputs(seed: int = 0) -> dict:
    key = jax.random.key(seed)
    ks = jax.random.split(key, 21)
    f32 = jnp.float32

    def nrm(k, shape, scale):
        return jax.random.normal(k, shape, f32) * scale

    D = D_MODEL
    return {
        "x_prompt": nrm(ks[0], (BATCH, SEQ, D), 1.0),
        "x_sample": nrm(ks[1], (DEC_BATCH, DEC_SEQ, D), 1.0),
        "c": nrm(ks[2], (DEC_BATCH, D), 1.0),
        "state_hgrn": nrm(ks[3], (DEC_BATCH, N_REC, 2, REC_HEADS, REC_HEAD_DIM, REC_HEAD_DIM), 0.5),
        "cache_k": nrm(ks[4], (DEC_BATCH, N_ATT, PAST_LEN, ATT_KV_HEADS, ATT_HEAD_DIM), 1.0),
        "cache_v": nrm(ks[5], (DEC_BATCH, N_ATT, PAST_LEN, ATT_KV_HEADS, ATT_HEAD_DIM), 1.0),
        "c_ctx": nrm(ks[6], (D,), 1.0),
        "ada_w": nrm(ks[7], (DEPTH, D, 3 * D), 0.5 * D ** -0.5),
        "ada_b": nrm(ks[8], (DEPTH, 3 * D), 0.02),
        "norm_pre": 1.0 + nrm(ks[9], (DEPTH, D), 0.05),
        "norm_post": 1.0 + nrm(ks[10], (DEPTH, D), 0.05),
        "rec_w_in": nrm(ks[11], (N_REC, D, REC_IN_WIDTH), D ** -0.5),
        "rec_lb_logits": nrm(ks[12], (N_REC, 2, REC_WIDTH), 0.1),
        "rec_head_norm": 1.0 + nrm(ks[13], (N_REC, REC_HEAD_DIM), 0.05),
        "pool_w": nrm(ks[14], (N_REC, POOL_GROUPS, POOL_GROUP_DIM, POOL_GROUP_DIM), POOL_GROUP_DIM ** -0.5),
        "pool_scale": 1.0 + nrm(ks[15], (N_REC, POOL_WIDTH), 0.05),
        "rec_w_out": nrm(ks[16], (N_REC, REC_OUT_IN, D), REC_OUT_IN ** -0.5),
        "att_w_in": nrm(ks[17], (N_ATT, D, ATT_IN_WIDTH), D ** -0.5),
        "att_q_norm": 1.0 + nrm(ks[18], (N_ATT, ATT_HEAD_DIM), 0.05),
        "att_k_norm": 1.0 + nrm(ks[19], (N_ATT, ATT_HEAD_DIM), 0.05),
        "att_w_out": nrm(ks[20], (N_ATT, ATT_WIDTH, D), ATT_WIDTH ** -0.5),
    }


def reference(x_prompt, x_sample, c, state_hgrn, cache_k, cache_v, c_ctx, ada_w, ada_b,
              norm_pre, norm_post, rec_w_in, rec_lb_logits, rec_head_norm, pool_w, pool_scale,
              rec_w_out, att_w_in, att_q_norm, att_k_norm, att_w_out):
    lb_p = jax.nn.softmax(rec_lb_logits.astype(jnp.float32), axis=0)
    lower_bounds = jnp.clip(jnp.cumsum(lb_p, axis=0) - lb_p[0], 0.0, 1.0)
    cos, sin = axial_rope(x_sample.shape[1])

    xc, xl = x_prompt, x_sample
    new_states, new_k, new_v = [], [], []
    for i in range(DEPTH):
        j = i // 2
        sh_c, sc_c, gt_c = modulation(c_ctx[None, :], ada_w[i], ada_b[i])
        sh_l, sc_l, gt_l = modulation(c, ada_w[i], ada_b[i])
        hc = rmsnorm(xc, norm_pre[i]) * (1.0 + sc_c) + sh_c
        hl = rmsnorm(xl, norm_pre[i]) * (1.0 + sc_l) + sh_l
        if i % 2 == 0:
            s_zero = jnp.zeros((hc.shape[0], 2, REC_HEADS, REC_HEAD_DIM, REC_HEAD_DIM), jnp.float32)
            oc, s_ctx = rec_mixer(hc, s_zero, rec_w_in[j], lower_bounds[j], rec_head_norm[j],
                                  pool_w[j], pool_scale[j], rec_w_out[j])
            new_states.append(s_ctx.astype(x_prompt.dtype))
            ol, _ = rec_mixer(hl, state_hgrn[:, j], rec_w_in[j], lower_bounds[j], rec_head_norm[j],
                              pool_w[j], pool_scale[j], rec_w_out[j])
        else:
            qc_, kc_, vc_, gc_ = att_project(hc, att_w_in[j], att_q_norm[j], att_k_norm[j])
            oc = (attention(qc_, kc_, vc_) * jax.nn.silu(gc_)) @ att_w_out[j]
            new_k.append(kc_)
            new_v.append(vc_)
            ql, kl, vl, gl = att_project(hl, att_w_in[j], att_q_norm[j], att_k_norm[j])
            ql = apply_rope(ql, cos, sin)
            kl = apply_rope(kl, cos, sin)
            keys = jnp.concatenate([cache_k[:, j].astype(kl.dtype), kl], axis=1)
            vals = jnp.concatenate([cache_v[:, j].astype(vl.dtype), vl], axis=1)
            ol = (attention(ql, keys, vals) * jax.nn.silu(gl)) @ att_w_out[j]
        xc = xc + gt_c * rmsnorm(oc, norm_post[i])
        xl = xl + gt_l * rmsnorm(ol, norm_post[i])

    return (xc, xl, jnp.stack(new_states, axis=1), jnp.stack(new_k, axis=1), jnp.stack(new_v, axis=1))
```

```python
import os
import numpy as np
from contextlib import ExitStack
import concourse.bass as bass
import concourse.mybir as mybir
from concourse.bass_utils import run_bass_kernel_spmd

F32 = mybir.dt.float32
BF16 = mybir.dt.bfloat16
AF = mybir.ActivationFunctionType
ALU = mybir.AluOpType
AX = mybir.AxisListType

ENGS = ["pe", "act", "dve", "pool", "sp"]
EPS = 1e-6
F_MIN = 1e-6
NCORES = 8
CH = 64
CSH = 20.0
NRING = 3


class Sched:
    def __init__(self, nc, n_dma_sems=16):
        self.nc = nc
        self.ops = {e: [] for e in ENGS}
        self.last_w = {}
        self.readers = {}
        self.n_dma_sems = n_dma_sems
        self.dma_slot_val = {}
        self.dma_rr = {"sp": 0, "pool": 0}
        self.all_dma_tokens = []
        self.pending = {e: set() for e in ENGS}
        self.trace = None

    def _deps(self, reads, writes):
        deps = set()
        for k in reads:
            t = self.last_w.get(k)
            if t is not None:
                deps.add(t)
        for k in writes:
            t = self.last_w.get(k)
            if t is not None:
                deps.add(t)
            for r in self.readers.get(k, ()):
                deps.add(r)
        return deps

    def _commit(self, tok, reads, writes):
        for k in writes:
            self.last_w[k] = tok
            self.readers[k] = []
        for k in reads:
            self.readers.setdefault(k, []).append(tok)

    def op(self, eng, emit, reads=(), writes=(), signal=True):
        deps = self._deps(reads, writes)
        deps |= self.pending[eng]
        self.pending[eng] = set()
        idx = len(self.ops[eng])
        tok = ("e", eng, idx)
        if eng == "pe":
            deps = {d for d in deps if not (d[0] == "e" and d[1] == "pe")}
        self.ops[eng].append(dict(emit=emit, deps=deps, signal=signal, tok=tok, dma=None))
        self._commit(tok, reads, writes)
        return tok

    def dma(self, q, emit, reads=(), writes=()):
        deps = self._deps(reads, writes)
        deps |= self.pending[q]
        self.pending[q] = set()
        slot = self.dma_rr[q]
        self.dma_rr[q] = (slot + 1) % self.n_dma_sems
        prev = self.dma_slot_val.get((q, slot), 0)
        if prev > 0:
            deps.add(("d", (q, slot), prev))
        val = prev + 16
        self.dma_slot_val[(q, slot)] = val
        tok = ("d", (q, slot), val)
        self.ops[q].append(dict(emit=emit, deps=deps, signal=False, tok=tok, dma=(q, slot)))
        self._commit(tok, reads, writes)
        self.all_dma_tokens.append(tok)
        return tok

    def barrier(self):
        toks = set()
        for e in ENGS:
            for i in range(len(self.ops[e]) - 1, -1, -1):
                if self.ops[e][i]["dma"] is None:
                    toks.add(self.ops[e][i]["tok"])
                    break
        for k, v in self.dma_slot_val.items():
            toks.add(("d", k, v))
        for e in ENGS:
            self.pending[e] |= toks

    def emit_all(self):
        nc = self.nc
        with ExitStack() as es:
            esem = {e: es.enter_context(nc.semaphore("sem_" + e)) for e in ENGS}
            dsem = {}
            for k in self.dma_slot_val:
                dsem[k] = es.enter_context(nc.semaphore("dsem_%s_%d" % k))
            counts = {}
            for e in ENGS:
                c = 0
                arr = []
                for o in self.ops[e]:
                    if o["signal"] and o["dma"] is None:
                        c += 1
                    arr.append(c)
                res = [None] * len(arr)
                nxt = None
                for i in range(len(arr) - 1, -1, -1):
                    o = self.ops[e][i]
                    if o["signal"] and o["dma"] is None:
                        nxt = arr[i]
                    res[i] = nxt
                counts[e] = res

            def resolve(tok):
                if tok[0] == "e":
                    v = counts[tok[1]][tok[2]]
                    assert v is not None, ("dep on op with no later signal", tok)
                    return ("e", tok[1]), esem[tok[1]], v
                return tok[1], dsem[tok[1]], tok[2]

            block = es.enter_context(nc.Block())

            def run(ename, eobj):
                seen = {}
                for o in self.ops[ename]:
                    waits = {}
                    for d in o["deps"]:
                        key, sem, v = resolve(d)
                        if v > waits.get(key, (None, 0))[1]:
                            waits[key] = (sem, v)
                    wl = []
                    for key, (sem, v) in waits.items():
                        if seen.get(key, 0) >= v:
                            continue
                        eobj.wait_ge(sem, v)
                        seen[key] = v
                        wl.append((key, v))
                    if self.trace is not None:
                        self.trace.append((ename, o["tok"], wl, o["signal"], o.get("tag")))
                    ins = o["emit"](eobj)
                    if o["dma"] is not None:
                        ins.then_inc(dsem[o["dma"]], 16)
                    elif o["signal"]:
                        ins.then_inc(esem[ename], 1)
                if ename == "sp":
                    for k, v in self.dma_slot_val.items():
                        eobj.wait_ge(dsem[k], v)

            @block.sync
            def _(e):
                run("sp", e)

            @block.tensor
            def _(e):
                run("pe", e)

            @block.scalar
            def _(e):
                run("act", e)

            @block.vector
            def _(e):
                run("dve", e)

            @block.gpsimd
            def _(e):
                run("pool", e)


def build_program(nlayers=4, dbg_names=(), stop=99):
    wl = _build(nlayers, (), stop, None)[2]
    nc, dbg_out, _ = _build(nlayers, dbg_names, stop, wl)
    return nc, dbg_out


def _build(nlayers, dbg_names, stop, wlist_in):
    nc = bass.Bass("TRN2", target_bir_lowering=False)
    es = ExitStack()

    def din(name, shape):
        return nc.dram_tensor(name, list(shape), F32, kind="ExternalInput").ap()

    def dout(name, shape):
        return nc.dram_tensor(name, list(shape), F32, kind="ExternalOutput").ap()

    x_in = din("x_in", [12, 128, 1024])
    cT_in = din("cT", [128, 8, 2])
    st_in = din("st_in", [2, 128, 8, 128])
    ck_in = din("ck_in", [2, 512, 256])
    cv_in = din("cv_in", [2, 512, 256])
    ada_w = din("ada_w", [4, 1024, 3072])
    ada_bT = din("ada_bT", [128, 4, 24])
    ada_bg = din("ada_bg", [4, 1024])
    npreT = din("npreT", [128, 4, 8])
    npost = din("npost", [4, 1024])
    rec_w_in = din("rec_w_in", [2, 1024, 3584])
    rec_w_out = din("rec_w_out", [2, 1024, 1024])
    att_w_in = din("att_w_in", [2, 1024, 2560])
    att_w_out = din("att_w_out", [2, 1024, 1024])
    lbT_in = din("lbT", [128, 2, 2, 4])
    hnT_in = din("hnT", [128, 2])
    pool_w = din("pool_w", [2, 4, 128, 128])
    pscT_in = din("pscT", [128, 2, 4])
    qn_row = din("qn_row", [2, 128])
    kn_row = din("kn_row", [2, 128])
    ident_in = din("ident", [128, 128])
    ones_in = din("ones", [128, 128])
    mk_in = din("mk", [64, 2, 256])
    rmask_in = din("rmask", [128, 512])
    icnt_s = din("icnt_s", [4, 1024])
    icnt_p = din("icnt_p", [4, 256])
    cos_in = din("cos_t", [8, 128, 128])
    sin_in = din("sin_t", [8, 128, 128])

    y_out = dout("y_out", [12, 128, 1024])
    st_out = dout("st_out", [2, 2, 2, 4, 128, 128])
    ck_out = dout("ck_out", [2, 2, 256, 256])
    cv_out = dout("cv_out", [2, 2, 256, 256])
    dbg_out = {}

    def sb(name, shape, dt, stack=None):
        return (stack or es).enter_context(nc.sbuf_tensor("sb_" + name, list(shape), dt))

    S = Sched(nc)

    X = sb("X", [128, 12, 1024], F32)
    ring = sb("ring", [128, NRING, 8, 512], BF16)
    hT = sb("hT", [128, 8, 1024], BF16)
    GT = sb("GT", [128, 2, 1024], F32)
    SC = sb("SC", [128, 8, 520], F32)
    idb = sb("idb", [128, 128], BF16)
    onesb = sb("onesb", [128, 128], BF16)
    mk = sb("mk", [64, 2, 256], BF16)
    rmask = sb("rmask", [128, 512], F32)
    mkf = sb("mkf", [64, 2, 256], F32)
    cT = sb("cTs", [128, 8, 2], F32)
    scT = sb("scT", [128, 8, 2], BF16)
    sc_rep = sb("sc_rep", [128, 8, 2, 128], BF16)
    adabT = sb("adabT", [128, 4, 24], F32)
    npre = sb("npre", [128, 4, 8], F32)
    lbl = sb("lbl", [128, 2, 2, 4], F32)
    lb = sb("lb", [128, 2, 2, 4], F32)
    oml = sb("oml", [128, 2, 2, 4], F32)
    hn = sb("hn", [128, 2], F32)
    psc = sb("psc", [128, 2, 4], F32)
    poolw = sb("poolw", [128, 8, 128], BF16)
    modT2 = sb("modT", [128, 2, 16, 2], F32)
    Gcol2 = sb("Gcol", [128, 2, 8, 2], F32)
    small = sb("small", [128, 64], F32)
    PS = es.enter_context(nc.psum_tensor("PS", [128, 8, 512], F32))
    ARENA_BYTES = 80 * 1024
    arena = sb("arena", [128, ARENA_BYTES // 4], F32)
    arena_off = [0]

    def aalloc(shape, dt):
        esz = 2 if dt == BF16 else 4
        n = int(np.prod(shape[1:])) * esz
        n = (n + 63) // 64 * 64
        off = arena_off[0]
        assert off + n <= ARENA_BYTES, ("arena overflow", off, n)
        arena_off[0] = off + n
        ap = arena[0:shape[0], off // 4:(off + n) // 4]
        if dt == BF16:
            ap = ap.bitcast(BF16)
        ap = ap[:, 0:int(np.prod(shape[1:]))]
        if len(shape) == 3:
            ap = ap.rearrange("p (a b) -> p a b", b=shape[2])
        elif len(shape) == 4:
            ap = ap.rearrange("p (a b c) -> p a b c", b=shape[2], c=shape[3])
        return ap

    def psb(b):
        return PS[:, b, :].bitcast(BF16)

    def sc(i, n=512):
        return SC[:, i, 0:n]

    def scb(i, n=1024):
        return SC[:, i, :].bitcast(BF16)[:, 0:n]

    def sc2(i):
        return SC[:, 2 * i:2 * i + 2, :].rearrange("p a b -> p (a b)")

    def act(out, in_, func, reads, writes, **kw):
        S.op("act", lambda e: e.activation(out=out, in_=in_, func=func, **kw), reads, writes)

    def tt(eng, out, a, b, op, reads, writes):
        S.op(eng, lambda e: e.tensor_tensor(out, a, b, op), reads, writes)

    def ts(eng, out, a, s1, s2, op0, op1, reads, writes):
        if s2 is None:
            S.op(eng, lambda e: e.tensor_scalar(out, a, s1, None, op0), reads, writes)
        else:
            S.op(eng, lambda e: e.tensor_scalar(out, a, s1, s2, op0, op1), reads, writes)

    def stt(eng, out, in0, scalar, in1, op0, op1, reads, writes):
        S.op(eng, lambda e: e.scalar_tensor_tensor(out, in0, scalar, in1, op0, op1), reads, writes)

    def cp(eng, out, in_, reads, writes):
        if eng == "act":
            S.op("act", lambda e: e.activation(out=out, in_=in_, func=AF.Copy), reads, writes)
        else:
            S.op(eng, lambda e: e.tensor_copy(out, in_), reads, writes)

    def mm(out, lhsT, rhs, start, stop, reads, writes, signal):
        S.op("pe", lambda e: e.matmul(out, lhsT, rhs, start=start, stop=stop), reads, writes, signal=signal)

    def tr(out, in_, reads, writes, signal):
        n = in_.shape[0]
        S.op("pe", lambda e: e.transpose(out, in_, idb[0:n, 0:n]), list(reads) + ["idb"], writes, signal=signal)

    def dbg(name, ap, shape, reads, dt=F32):
        if name not in dbg_names:
            return
        d = nc.dram_tensor("dbg_" + name, list(shape), dt, kind="ExternalOutput").ap()
        dbg_out[name] = d
        S.dma("sp", lambda e: e.dma_start(out=d, in_=ap), reads=reads)

    bank_rr = {"gen": 0, "tr": 0}

    def gen_bank():
        b = bank_rr["gen"]
        bank_rr["gen"] ^= 1
        return b

    def tr_bank():
        b = 2 + bank_rr["tr"]
        bank_rr["tr"] ^= 1
        return b

    wlist = list(wlist_in) if wlist_in is not None else []
    wcollect = []
    wstate = {"issued": 0, "next": 0}

    def w_issue_upto(k):
        while wstate["issued"] <= k and wstate["issued"] < len(wlist):
            i = wstate["issued"]
            wap, c0, ncol = wlist[i]
            slot = i % NRING
            src = wap[:, c0:c0 + ncol].rearrange("(c p) n -> p c n", p=128)
            S.dma("pool", lambda e, slot=slot, src=src, ncol=ncol: e.dma_start(out=ring[:, slot, :, 0:ncol], in_=src),
                  writes=[("ring", slot)])
            wstate["issued"] += 1

    def w_get(wkey, c0, ahead=NRING - 1):
        k = wstate["next"]
        wstate["next"] += 1
        wcollect.append((wkey, c0))
        if wlist_in is not None:
            assert wlist_in[k][3] == (wkey, c0), ("weight stream order mismatch", k, wlist_in[k][3], (wkey, c0))
        w_issue_upto(k + ahead)
        return k % NRING

    WSRC = {}
    for l_ in range(4):
        WSRC[("ada", l_)] = ada_w[l_]
    for j_ in range(2):
        WSRC[("rin", j_)] = rec_w_in[j_]
        WSRC[("rout", j_)] = rec_w_out[j_]
        WSRC[("ain", j_)] = att_w_in[j_]
        WSRC[("aout", j_)] = att_w_out[j_]
    if wlist_in is not None:
        wlist = [(WSRC[wk], c0, 512) for (wk, c0) in wlist_in]
        wlist_in = [(WSRC[wk], c0, 512, (wk, c0)) for (wk, c0) in wlist_in]

    for t in range(12):
        S.dma("sp", lambda e, t=t: e.dma_start(out=X[:, t, :], in_=x_in[t]), writes=[("X", t)])
    S.dma("pool", lambda e: e.dma_start(out=idb[:], in_=ident_in), writes=["idb"])
    S.dma("pool", lambda e: e.dma_start(out=onesb[:], in_=ones_in), writes=["onesb"])
    S.dma("pool", lambda e: e.dma_start(out=mk[:], in_=mk_in), writes=["mk"])
    S.dma("pool", lambda e: e.dma_start(out=poolw[:], in_=pool_w.rearrange("j g c d -> c (j g) d")), writes=["poolw"])
    S.dma("sp", lambda e: e.dma_start(out=rmask[:], in_=rmask_in), writes=["rmask"])
    S.dma("sp", lambda e: e.dma_start(out=mkf[:], in_=mk_in), writes=["mkf"])
    S.dma("sp", lambda e: e.dma_start(out=cT[:], in_=cT_in), writes=["cT"])
    S.dma("sp", lambda e: e.dma_start(out=adabT[:], in_=ada_bT), writes=["adabT"])
    S.dma("sp", lambda e: e.dma_start(out=npre[:], in_=npreT), writes=["npre"])
    S.dma("sp", lambda e: e.dma_start(out=lbl[:], in_=lbT_in), writes=["lbl"])
    S.dma("sp", lambda e: e.dma_start(out=hn[:], in_=hnT_in), writes=["hn"])
    S.dma("sp", lambda e: e.dma_start(out=psc[:], in_=pscT_in), writes=["psc"])
    w_issue_upto(NRING - 2)

    act(scT[:], cT[:], AF.Silu, ["cT"], ["scT"])
    cp("dve", sc_rep[:], scT[:].unsqueeze(3).to_broadcast([128, 8, 2, 128]), ["scT"], ["sc_rep"])
    S.op("dve", lambda e: e.memset(lb[:, 0, :, :], 0.0), [], ["lb0"])
    tt("dve", small[:, 0:8], lbl[:, 1, :, :].rearrange("p a b -> p (a b)"),
       lbl[:, 0, :, :].rearrange("p a b -> p (a b)"), ALU.subtract, ["lbl"], ["small"])
    act(lb[:, 1, :, :].rearrange("p a b -> p (a b)"), small[:, 0:8], AF.Sigmoid, ["small"], ["lb1"])
    ts("dve", oml[:].rearrange("p j a b -> p (j a b)"), lb[:].rearrange("p j a b -> p (j a b)"),
       -1.0, 1.0, ALU.mult, ALU.add, ["lb0", "lb1"], ["oml"])
    LBK = ["lb0", "lb1", "oml"]

    deferred = []

    def run_deferred(bk, n=1):
        for _ in range(n):
            if deferred:
                deferred.pop(0)(bk)

    def queue_mod_ss(l):
        modT = modT2[:, l % 2]
        Gcol = Gcol2[:, l % 2]

        def blk(b):
            def f(bk):
                slot = w_get(("ada", l), b * 512)
                for n in range(4):
                    for kc in range(8):
                        mm(PS[:, bk, n * 2:n * 2 + 2], ring[:, slot, kc, n * 128:(n + 1) * 128], scT[:, kc, :],
                           kc == 0, kc == 7, [("ring", slot), "scT"], [("ps", bk)], signal=(kc == 7 and n == 3))
                tt("dve", modT[:, 4 * b:4 * b + 4, :], PS[:, bk, 0:8].rearrange("p (c v) -> p c v", v=2),
                   adabT[:, l, 4 * b:4 * b + 4].unsqueeze(2).to_broadcast([128, 4, 2]), ALU.add,
                   [("ps", bk), "adabT"], [("modT", l % 2, b)])
                if b == 3:
                    stt("dve", Gcol, modT[:, 8:16, :], 1.0, npre[:, l, :].unsqueeze(2).to_broadcast([128, 8, 2]),
                        ALU.add, ALU.mult, [("modT", l % 2, 2), ("modT", l % 2, 3), "npre"], [("Gcol", l % 2)])
            return f
        for b in range(4):
            deferred.append(blk(b))

    def queue_mod_gate(l, v):
        def blk(b):
            def f(bk):
                if b == 0:
                    S.dma("sp", lambda e: e.dma_start(out=sc2(2)[:, 0:1024],
                                                      in_=ada_bg[l:l + 1, :].partition_broadcast(128)),
                          writes=[("sc", 4), ("sc", 5)])
                    S.dma("sp", lambda e: e.dma_start(out=sc2(3)[:, 0:1024],
                                                      in_=npost[l:l + 1, :].partition_broadcast(128)),
                          writes=[("sc", 6), ("sc", 7)])
                slot = w_get(("ada", l), (4 + b) * 512)
                for kc in range(8):
                    mm(PS[:, bk, :], sc_rep[:, kc, v, :], ring[:, slot, kc, :], kc == 0, kc == 7,
                       [("ring", slot), "sc_rep"], [("ps", bk)], signal=(kc == 7))
                tt("dve", GT[:, v, b * 512:(b + 1) * 512], PS[:, bk, :], sc2(2)[:, b * 512:(b + 1) * 512], ALU.add,
                   [("ps", bk), ("sc", 4), ("sc", 5)], [("GT", v, b)])
                tt("dve", GT[:, v, b * 512:(b + 1) * 512], GT[:, v, b * 512:(b + 1) * 512],
                   sc2(3)[:, b * 512:(b + 1) * 512], ALU.mult,
                   [("GT", v, b), ("sc", 6), ("sc", 7)], [("GT", v, b)])
            return f
        for b in range(2):
            deferred.append(blk(b))

    def queue_pass_mod(l, v):
        if v == 1 and l + 1 < nlayers:
            queue_mod_ss(l + 1)
        queue_mod_gate(l, v)

    def prenorm(l, tile0, ngroups, v):
        modT = modT2[:, l % 2]
        Gcol = Gcol2[:, l % 2]
        MK = [("Gcol", l % 2), ("modT", l % 2, 0), ("modT", l % 2, 1)]
        for g in range(ngroups):
            for jt in range(4):
                t = tile0 + g * 4 + jt
                ssq = small[:, 16 + jt:17 + jt]
                rs = small[:, 20 + jt:21 + jt]
                import os
                PN = int(os.environ.get("PN", "9"))
                act(scb(4 + jt), X[:, t, :], AF.Square, [("X", t)], [("sc", 4 + jt), ("ssq", jt)], accum_out=ssq)
                if PN <= 1:
                    continue
                act(rs, ssq, AF.Sqrt, [("ssq", jt)], [("rs", jt)], scale=1.0 / 1024, bias=EPS)
                S.op("dve", lambda e, rs=rs: e.reciprocal(rs, rs), [("rs", jt)], [("rs", jt)])
                if PN <= 2:
                    continue
                ts("dve", scb(jt), X[:, t, :], rs, None, ALU.mult, None, [("X", t), ("rs", jt)], [("sc", jt)])
            if PN <= 3:
                continue
            for cpair in range(4):
                bk = tr_bank()
                if os.environ.get("BK4"):
                    bk = 2 + cpair
                for cc in range(2):
                    c = 2 * cpair + cc
                    for jt in range(4):
                        tr(psb(bk)[:, cc * 512 + jt * 128: cc * 512 + (jt + 1) * 128],
                           scb(jt)[:, c * 128:(c + 1) * 128], [("sc", jt)], [("ps", bk)],
                           signal=(cc == 1 and jt == 3))
                if PN <= 4:
                    continue
                for cc in range(2):
                    c = 2 * cpair + cc
                    dst = hT[:, c, g * 512:(g + 1) * 512]
                    if os.environ.get("DSTX"):
                        dst = ring[:, 0, c, :]
                    src = psb(bk)[:, cc * 512:(cc + 1) * 512]
                    if PN == 5:
                        cp("act" if cc == 0 else "dve", dst, src, [("ps", bk)], [("hT", g, c)])
                    elif PN == 7:
                        cp("dve", dst, src, [("ps", bk)], [("hT", g, c)])
                    elif PN == 8:
                        cp("act", dst, src, [("ps", bk)], [("hT", g, c)])
                    elif PN == 6:
                        ts("dve", dst, src, Gcol[:, c, v:v + 1], modT[:, c, v:v + 1], ALU.mult, ALU.add,
                           [("ps", bk)] + MK, [("hT", g, c)])
                    elif cpair % 2 == 0:
                        act(dst, src, AF.Identity, [("ps", bk)] + MK, [("hT", g, c)],
                            scale=Gcol[:, c, v:v + 1], bias=modT[:, c, v:v + 1])
                    else:
                        ts("dve", dst, src, Gcol[:, c, v:v + 1], modT[:, c, v:v + 1], ALU.mult, ALU.add,
                           [("ps", bk)] + MK, [("hT", g, c)])

    def hT_keys(g):
        return [("hT", g, c) for c in range(8)]

    def proj_fm(slot, n, g, bk):
        for kc in range(8):
            mm(PS[:, bk, :], ring[:, slot, kc, n * 128:(n + 1) * 128], hT[:, kc, g * 512:(g + 1) * 512],
               kc == 0, kc == 7, [("ring", slot)] + hT_keys(g), [("ps", bk)], signal=(kc == 7))

    def proj_tm(slot, tok0, ntok, bk):
        g = tok0 // 512
        for kc in range(8):
            mm(PS[0:ntok, bk, :], hT[:, kc, tok0:tok0 + ntok], ring[:, slot, kc, :],
               kc == 0, kc == 7, [("ring", slot)] + hT_keys(g), [("ps", bk)], signal=(kc == 7))

    def out_proj_and_residual(l, tile0, ntiles, v, zin_fn, zin_keys_fn, wkey):
        while deferred:
            run_deferred(gen_bank())
        s0 = w_get(wkey, 0)
        s1 = w_get(wkey, 512, NRING - 2)
        slots = (s0, s1)
        for it in range(ntiles):
            t = tile0 + it
            b0 = 4 + 2 * (it % 2)
            for nb in range(2):
                for kc in range(8):
                    mm(PS[:, b0 + nb, :], zin_fn(kc, it), ring[:, slots[nb], kc, :], kc == 0, kc == 7,
                       [("ring", slots[nb])] + zin_keys_fn(kc, it), [("ps", b0 + nb)], signal=(kc == 7))
            po = PS[:, b0:b0 + 2, :].rearrange("p a b -> p (a b)")
            pk = [("ps", b0), ("ps", b0 + 1)]
            ssq = small[:, 24 + (it % 2):25 + (it % 2)]
            rs = small[:, 26 + (it % 2):27 + (it % 2)]
            jk = 6 + (it % 2)
            act(scb(jk), po, AF.Square, pk, [("sc", jk), ("ssq2", it % 2)], accum_out=ssq)
            act(rs, ssq, AF.Sqrt, [("ssq2", it % 2)], [("rs2", it % 2)], scale=1.0 / 1024, bias=EPS)
            S.op("dve", lambda e, rs=rs: e.reciprocal(rs, rs), [("rs2", it % 2)], [("rs2", it % 2)])
            tmpk = 2 * (it % 2)
            tmp = sc2(it % 2)[:, 0:1024]
            stt("dve", tmp, po, rs, GT[:, v, :], ALU.mult, ALU.mult,
                pk + [("rs2", it % 2), ("GT", v, 0), ("GT", v, 1)], [("sc", tmpk), ("sc", tmpk + 1)])
            tt("pool", X[:, t, :], X[:, t, :], tmp, ALU.add,
               [("X", t), ("sc", tmpk), ("sc", tmpk + 1)], [("X", t)])

    def rec_pass(l, j, tile0, ntiles, v, seqs, is_sample):
        L = ntiles * 128
        G = L // 512
        nch = L // CH
        arena_off[0] = 0
        qo = aalloc([128, 4, L], BF16)
        qp = [aalloc([128, 4, L], BF16) for r in range(2)]
        kp = [aalloc([128, 4, L], BF16) for r in range(2)]
        v_c = aalloc([64, nch, 512], BF16)
        ypool = aalloc([128, 4, L], BF16)
        tabS = aalloc([128, 2, 4, nch], F32)
        tabG = aalloc([128, 2, 4, nch], F32)
        tabA = aalloc([128, 2, 4, nch], F32)
        St = aalloc([128, 8, 128], F32)
        Stb = aalloc([128, 8, 128], BF16)
        kTt = aalloc([64, 8, 128], BF16)
        ATm = aalloc([64, 8, 64], BF16)
        Lseq = seqs[0][1]
        icnt = aalloc([128, Lseq], F32)

        if stop <= 1:
            return
        prenorm(l, tile0, G, v)
        src_ic = icnt_s if is_sample else icnt_p
        if stop <= 2:
            return

        queue_pass_mod(l, v)
        slot = w_get(("rin", j), 1024)
        for n in range(4):
            for g in range(G):
                bk = gen_bank()
                proj_fm(slot, n, g, bk)
                act(qo[:, n, g * 512:(g + 1) * 512], PS[:, bk, :], AF.Silu, [("ps", bk)], [("qo", n, g)])

        f_items = [(r, n, g) for r in range(2) for n in range(4) for g in range(G)]
        f_slots = {}

        def f_ctx(idx):
            r, n, g = f_items[idx]
            o4 = 4 * (idx % 2)
            Ts = [sc(o4 + i) for i in range(4)]
            Ks = [("sc", o4 + i) for i in range(4)]
            return r, n, g, Ts, Ks

        def f_s0(idx):
            r, n, g, (T1, T2, T3, T4), (K1, K2, K3, K4) = f_ctx(idx)
            if r not in f_slots:
                f_slots[r] = w_get(("rin", j), 1536 + 512 * r)
            bk = gen_bank()
            proj_fm(f_slots[r], n, g, bk)
            act(T1, PS[:, bk, :], AF.Exp, [("ps", bk)], [K1], scale=-1.0)

        def f_s1(idx):
            r, n, g, (T1, T2, T3, T4), (K1, K2, K3, K4) = f_ctx(idx)
            act(T1, T1, AF.Ln, [K1], [K1], bias=1.0)
            act(T1, T1, AF.Exp, [K1], [K1], scale=-1.0)
            ts("dve", T1, T1, oml[:, j, r, n:n + 1], lb[:, j, r, n:n + 1], ALU.mult, ALU.add, [K1] + LBK, [K1])
            ts("dve", T1, T1, F_MIN, 1.0, ALU.max, ALU.min, [K1], [K1])
            act(T2, T1, AF.Ln, [K1], [K2])
            act(T3, T1, AF.Identity, [K1], [K3], scale=-1.0, bias=1.0)

        def f_s2(idx):
            r, n, g, (T1, T2, T3, T4), (K1, K2, K3, K4) = f_ctx(idx)
            S.op("dve", lambda e: e.tensor_tensor_scan(T4, rmask[:], T2, 0.0, ALU.mult, ALU.add),
                 [K2, "rmask"], [K4])
            b3 = T4.rearrange("p (c t) -> p c t", t=CH)
            c0 = g * 8
            tk = ("tab", r, n, g)
            smt = small[:, 32 + 8 * (idx % 2):40 + 8 * (idx % 2)]
            smk = ("smt", idx % 2)
            tt("dve", smt, b3[:, :, CH - 1], b3[:, :, CH // 2 - 1], ALU.subtract, [K4], [smk])
            e_mid = tabS if r == 0 else tabG
            e_dif = tabG if r == 0 else tabS
            act(e_mid[:, r, n, c0:c0 + 8], b3[:, :, CH // 2 - 1], AF.Exp, [K4], [tk + (0,)],
                bias=(-CSH if r == 0 else CSH))
            act(e_dif[:, r, n, c0:c0 + 8], smt, AF.Exp, [smk], [tk + (1,)], bias=(CSH if r == 0 else -CSH))
            act(tabA[:, r, n, c0:c0 + 8], b3[:, :, CH - 1], AF.Exp, [K4], [tk + (2,)])
            tt("pool", T1.rearrange("p (c t) -> p c t", t=CH), b3,
               b3[:, :, CH // 2 - 1:CH // 2].to_broadcast([128, 8, CH]), ALU.subtract, [K4, K1], [K1])
            if r == 1:
                tt("dve", T1, T1, T2, ALU.subtract, [K1, K2], [K1])

        def f_s3(idx):
            r, n, g, (T1, T2, T3, T4), (K1, K2, K3, K4) = f_ctx(idx)
            sg = 1.0 if r == 0 else -1.0
            act(T2, T1, AF.Exp, [K1], [K2], scale=sg, bias=-CSH)
            act(T4, T1, AF.Exp, [K1], [K4], scale=-sg, bias=-CSH)
            tt("dve", qp[r][:, n, g * 512:(g + 1) * 512], qo[:, n, g * 512:(g + 1) * 512], T2, ALU.mult,
               [("qo", n, g), K2], [("qp", r, n, g)])
            tt("pool", kp[r][:, n, g * 512:(g + 1) * 512], T3, T4, ALU.mult, [K3, K4], [("kp", r, n, g)])

        NF = len(f_items)
        for idx in range(NF + 1):
            if idx < NF:
                f_s0(idx)
            if idx >= 1:
                f_s2(idx - 1)
            if idx < NF:
                f_s1(idx)
            if idx >= 1:
                f_s3(idx - 1)

        tg = "_%d_%d" % (l, int(is_sample))
        allk = lambda nm, *pre: [(nm,) + pre + (n, g) for n in range(4) for g in range(G)]
        dbg("hT" + tg, hT[:, :, 0:L], [128, 8, L], [("hT", g, c) for g in range(G) for c in range(8)], BF16)
        dbg("qp0" + tg, qp[0], [128, 4, L], allk("qp", 0), BF16)
        dbg("kp0" + tg, kp[0], [128, 4, L], allk("kp", 0), BF16)
        dbg("qp1" + tg, qp[1], [128, 4, L], allk("qp", 1), BF16)
        dbg("kp1" + tg, kp[1], [128, 4, L], allk("kp", 1), BF16)
        dbg("tabS" + tg, tabS, [128, 2, 4, nch], [("tab", r, n, g, k) for r in range(2) for n in range(4) for g in range(G) for k in range(3)])
        dbg("tabG" + tg, tabG, [128, 2, 4, nch], [("tab", r, n, g, k) for r in range(2) for n in range(4) for g in range(G) for k in range(3)])
        dbg("tabA" + tg, tabA, [128, 2, 4, nch], [("tab", r, n, g, k) for r in range(2) for n in range(4) for g in range(G) for k in range(3)])
        if stop <= 3:
            return
        slot = w_get(("rin", j), 2560)
        for c in range(nch):
            bk = gen_bank()
            proj_tm(slot, c * CH, CH, bk)
            cp("act" if c % 2 == 0 else "dve", v_c[:, c, :], PS[0:64, bk, :], [("ps", bk)], [("v_c", c)])

        slot = w_get(("rin", j), 512)
        for n in range(4):
            for g in range(G):
                bk = gen_bank()
                proj_fm(slot, n, g, bk)
                act(ypool[:, n, g * 512:(g + 1) * 512], PS[:, bk, :], AF.Silu, [("ps", bk)], [("yp", n, g)])

        slot = w_get(("rin", j), 0)
        nseq = len(seqs)
        Wd = nseq * (Lseq + 16)
        for n in range(4):
            win = (2, 4, 8, 16)[n]
            UB, WA, WB = sc2(0), sc2(1), sc2(2)
            DB = sc2(3).bitcast(BF16)
            KU, KA, KB, KD = [[("sc", 2 * i), ("sc", 2 * i + 1)] for i in range(4)]
            S.dma("sp", lambda e, n=n: e.dma_start(out=icnt, in_=src_ic[n:n + 1, :].partition_broadcast(128)),
                  writes=["icnt"])
            S.op("pool", lambda e, UB=UB: e.memset(UB[:, 0:Wd], 0.0), [], KU)
            for g in range(G):
                bk = gen_bank()
                proj_fm(slot, n, g, bk)
                for k, (s0, Ls) in enumerate(seqs):
                    a = max(s0, g * 512)
                    b = min(s0 + Ls, (g + 1) * 512)
                    if a >= b:
                        continue
                    dst0 = k * (Ls + 16) + 8 + (a - s0)
                    cp("act", UB[:, dst0:dst0 + (b - a)], PS[:, bk, a - g * 512:b - g * 512], [("ps", bk)], KU)
            tt("dve", WA[:, 1:Wd], UB[:, 0:Wd - 1], UB[:, 1:Wd], ALU.add, KU, KA)
            cur, curk, oth, othk = WA, KA, WB, KB
            lo, hi = 1, Wd
            if win >= 4:
                tt("dve", oth[:, lo + 1:hi - 1], cur[:, lo:hi - 2], cur[:, lo + 2:hi], ALU.add, curk, othk)
                cur, curk, oth, othk = oth, othk, cur, curk
                lo, hi = lo + 1, hi - 1
            if win >= 8:
                tt("dve", oth[:, lo + 2:hi - 2], cur[:, lo:hi - 4], cur[:, lo + 4:hi], ALU.add, curk, othk)
                cur, curk, oth, othk = oth, othk, cur, curk
                lo, hi = lo + 2, hi - 2
            if win >= 16:
                tt("dve", oth[:, lo + 4:hi - 4], cur[:, lo:hi - 8], cur[:, lo + 8:hi], ALU.add, curk, othk)
                cur, curk, oth, othk = oth, othk, cur, curk
                lo, hi = lo + 4, hi - 4
            assert lo <= 8 and hi >= Wd - 8
            for k, (s0, Ls) in enumerate(seqs):
                base = k * (Ls + 16) + 8
                tt("dve", oth[:, base:base + Ls], cur[:, base:base + Ls], icnt, ALU.mult, curk + ["icnt"], othk)
                tt("dve", DB[:, s0:s0 + Ls], oth[:, base:base + Ls], UB[:, base:base + Ls], ALU.subtract,
                   othk + KU, KD)
            for g in range(G):
                bk = gen_bank()
                mm(PS[:, bk, :], poolw[:, j * 4 + n, :], DB[:, g * 512:(g + 1) * 512], True, True,
                   KD + ["poolw"], [("ps", bk)], signal=True)
                yv = ypool[:, n, g * 512:(g + 1) * 512]
                stt("dve", yv, PS[:, bk, :], psc[:, j, n:n + 1], yv, ALU.mult, ALU.mult,
                    [("ps", bk), "psc", ("yp", n, g)], [("yp", n, g)])

        dbg("vc" + tg, v_c, [64, nch, 512], [("v_c", c) for c in range(nch)], BF16)
        dbg("yp" + tg, ypool, [128, 4, L], allk("yp"), BF16)
        if stop <= 4:
            return
        if is_sample:
            ic_bf = icnt.bitcast(BF16)
            kT2 = [kTt, ic_bf[0:64, 0:1024].rearrange("p (a b) -> p a b", b=128)]
            AT2 = [ATm, ic_bf[0:64, 1024:1536].rearrange("p (a b) -> p a b", b=64)]
            alias_k = ["icnt"]
        else:
            kT2 = [kTt, aalloc([64, 8, 128], BF16)]
            AT2 = [ATm, aalloc([64, 8, 64], BF16)]
            alias_k = []
        for si, (s0, Ls) in enumerate(seqs):
            nst = Ls // CH
            cb = s0 // CH
            for bsel in range(2):
                S.op("dve", lambda e, bsel=bsel: e.memset(AT2[bsel], 0.0), [],
                     [("ATm", bsel, 0), ("ATm", bsel, 1)] + alias_k)
            if is_sample:
                S.dma("sp", lambda e: e.dma_start(out=St, in_=st_in[j]), writes=["St"])
            else:
                S.op("dve", lambda e: e.memset(St, 0.0), [], ["St"])

            def geo(i):
                cr = (cb + i, cb + nst - 1 - i)
                t0 = (cr[0] * CH, cr[1] * CH)
                gq = (t0[0] // 512, t0[1] // 512)
                return cr, t0, gq

            def scan_p(i):
                cr, t0, gq = geo(i)
                bsel = i % 2
                kTb, ATb = kT2[bsel], AT2[bsel]
                bT = 2 + bsel
                for r in range(2):
                    for n in range(4):
                        tr(psb(bT)[0:64, (4 * r + n) * 128:(4 * r + n + 1) * 128], kp[r][:, n, t0[r]:t0[r] + CH],
                           [("kp", r, n, gq[r])], [("ps", bT)], signal=(r == 1 and n == 3))
                cp("act", kTb.rearrange("p a b -> p (a b)"), psb(bT)[0:64, :], [("ps", bT)],
                   [("kTt", bsel)] + alias_k)
                bA = (0, 5)[bsel]
                for r in range(2):
                    for n in range(4):
                        mm(PS[0:64, bA, (4 * r + n) * 64:(4 * r + n + 1) * 64], kp[r][:, n, t0[r]:t0[r] + CH],
                           qp[r][:, n, t0[r]:t0[r] + CH], True, True,
                           [("kp", r, n, gq[r]), ("qp", r, n, gq[r])], [("ps", bA)], signal=(r == 1 and n == 3))
                for r in range(2):
                    mku = mkf[:, r, :].bitcast(mybir.dt.uint32)
                    S.op("dve", lambda e, r=r, mku=mku, bA=bA, ATb=ATb: e.copy_predicated(
                        ATb[:, 4 * r:4 * r + 4, :].rearrange("p a b -> p (a b)"), mku,
                        PS[0:64, bA, 256 * r:256 * r + 256]),
                        [("ps", bA), "mkf"], [("ATm", bsel, r)] + alias_k)

            def scan_q(i):
                cr, t0, gq = geo(i)
                bsel = i % 2
                kTb, ATb = kT2[bsel], AT2[bsel]
                first = i < nst // 2
                tabk = lambda r, kind: [("tab", r, n, gq[r], kind) for n in range(4)]
                if deferred:
                    run_deferred(1)
                for r in range(2):
                    for n in range(4):
                        mm(PS[:, 6 + r, n * 128:(n + 1) * 128], kTb[:, 4 * r + n, :],
                           v_c[:, cr[r], n * 128:(n + 1) * 128], True, True,
                           [("kTt", bsel), ("v_c", cr[r])], [("ps", 6 + r)], signal=(n == 3))
                tmpKV = SC[:, 0:2, 0:512].rearrange("p a (h v) -> p a h v", v=128)
                for r in range(2):
                    for n in range(4):
                        act(tmpKV[:, r, n, :], PS[:, 6 + r, n * 128:(n + 1) * 128], AF.Identity,
                            [("ps", 6 + r)] + tabk(r, 0) + tabk(r, 1), [("sc", r, n)],
                            scale=tabG[:, r, n, cr[r]:cr[r] + 1])
                for r in range(2):
                    tt("dve", Stb[:, 4 * r:4 * r + 4, :], St[:, 4 * r:4 * r + 4, :],
                       tabS[:, r, :, cr[r]:cr[r] + 1].to_broadcast([128, 4, 128]), ALU.mult,
                       ["St"] + tabk(r, 0) + tabk(r, 1), [("Stb", r)])
                for r in range(2):
                    tt("dve", St[:, 4 * r:4 * r + 4, :], St[:, 4 * r:4 * r + 4, :],
                       tabA[:, r, :, cr[r]:cr[r] + 1].to_broadcast([128, 4, 128]), ALU.mult,
                       ["St"] + tabk(r, 2), ["St"])
                St4 = St.rearrange("p (a h) v -> p a h v", a=2)
                tt("dve", St4, St4, tmpKV, ALU.add,
                   ["St"] + [("sc", r, n) for r in range(2) for n in range(4)], ["St", ("sc", 0), ("sc", 1)])
                bO = 4
                for r in range(2):
                    for n in range(4):
                        o_ap = PS[:, bO, (4 * r + n) * 64:(4 * r + n + 1) * 64]
                        mm(o_ap, v_c[:, cr[r], n * 128:(n + 1) * 128], ATb[:, 4 * r + n, :], True, False,
                           [("v_c", cr[r]), ("ATm", bsel, r)], [("ps", bO)], signal=False)
                        mm(o_ap, Stb[:, 4 * r + n, :], qp[r][:, n, t0[r]:t0[r] + CH], False, True,
                           [("Stb", r), ("qp", r, n, gq[r])], [("ps", bO)], signal=(r == 1 and n == 3))
                for r in range(2):
                    dst = qo[:, :, t0[r]:t0[r] + CH]
                    src = PS[:, bO, 256 * r:256 * r + 256].rearrange("p (a b) -> p a b", b=64)
                    ok = [("qo", n, gq[r]) for n in range(4)]
                    E2 = float(np.exp(2.0 * CSH))
                    if first:
                        act(dst, src, AF.Identity, [("ps", bO)], ok, scale=E2)
                    else:
                        stt("dve", dst, src, E2, dst, ALU.mult, ALU.add, [("ps", bO)] + ok, ok)

            S.op("dve", lambda e: e.memset(SC[:, 0:2, 0:8], 0.0), [], [("sc", 0), ("sc", 1)] +
                 [("sc", r, n) for r in range(2) for n in range(4)])
            scan_p(0)
            for i in range(nst):
                if i + 1 < nst:
                    scan_p(i + 1)
                scan_q(i)
            if not is_sample:
                dst = st_out[si, j].rearrange("r h d v -> d (r h) v")
                S.dma("sp", lambda e, dst=dst: e.dma_start(out=dst, in_=St), reads=["St"])

        dbg("o" + tg, qo, [128, 4, L], allk("qo"), BF16)
        if stop <= 5:
            return
        ctr = 0
        for n in range(4):
            for g in range(G):
                o4 = 4 * (ctr % 2)
                ctr += 1
                T1, T2 = sc(o4), sc(o4 + 1)
                K1, K2 = ("sc", o4), ("sc", o4 + 1)
                ov = qo[:, n, g * 512:(g + 1) * 512]
                sqb = scb(o4 + 2, 512)
                act(sqb, ov, AF.Square, [("qo", n, g)], [("sc", o4 + 2)])
                bk = gen_bank()
                mm(PS[:, bk, :], onesb[:], sqb, True, True, [("sc", o4 + 2), "onesb"], [("ps", bk)], signal=True)
                act(T1, PS[:, bk, :], AF.Sqrt, [("ps", bk)], [K1], scale=1.0 / 128, bias=EPS)
                S.op("dve", lambda e, T1=T1: e.reciprocal(T1, T1), [K1], [K1])
                stt("dve", qp[0][:, n, g * 512:(g + 1) * 512], ov, hn[:, j:j + 1], T1, ALU.mult, ALU.mult,
                    [("qo", n, g), "hn", K1, ("qp", 0, n, g)], [("qp", 0, n, g)])

        slot = w_get(("rin", j), 3072)
        ctr = 0
        for n in range(4):
            for g in range(G):
                bk = gen_bank()
                proj_fm(slot, n, g, bk)
                o4 = 4 * (ctr % 2) + 3
                ctr += 1
                sgt = scb(o4, 512)
                act(sgt, PS[:, bk, :], AF.Silu, [("ps", bk)], [("sc", o4)])
                zv = qp[0][:, n, g * 512:(g + 1) * 512]
                tt("dve", zv, zv, sgt, ALU.mult, [("qp", 0, n, g), ("sc", o4)], [("qp", 0, n, g)])
        dbg("z" + tg, qp[0], [128, 4, L], allk("qp", 0), BF16)
        dbg("zin%d_%d" % (l, int(is_sample)), ypool, [128, 4, L], [("yp", n, g) for n in range(4) for g in range(G)])

        def zin_fn(kc, it):
            buf = ypool if kc < 4 else qp[0]
            return buf[:, kc % 4, it * 128:(it + 1) * 128]

        def zin_keys(kc, it):
            g = (it * 128) // 512
            return [("yp", kc, g)] if kc < 4 else [("qp", 0, kc - 4, g)]

        out_proj_and_residual(l, tile0, ntiles, v, zin_fn, zin_keys, ("rout", j))
        S.barrier()

    def att_pass(l, j, tile0, ntiles, v, seqs, is_sample):
        L = ntiles * 128
        G = L // 512
        nkt_cache = 4 if is_sample else 0
        arena_off[0] = 0
        QT = aalloc([128, 8, L], BF16)
        KT = aalloc([128, 2, 512 + L], BF16)
        Vt = aalloc([128, nkt_cache + ntiles, 256], BF16)
        gT = aalloc([128, 8, L], BF16)
        pT = aalloc([128, 3, 512], BF16)
        gq = aalloc([128, 128], F32)
        gk = aalloc([128, 128], F32)
        stg = aalloc([128, 2, 512], F32)
        cosT = aalloc([128, 8, 128], F32)
        sinT = aalloc([128, 8, 128], F32)
        ckt = aalloc([128, 4, 256], BF16)
        if is_sample:
            CGq = aalloc([128, 8, 128], F32)
            SGq = aalloc([128, 8, 128], F32)
            CGk = aalloc([128, 8, 128], F32)
            SGk = aalloc([128, 8, 128], F32)

        S.dma("sp", lambda e: e.dma_start(out=gq, in_=qn_row[j:j + 1, :].partition_broadcast(128)), writes=["gq"])
        S.dma("sp", lambda e: e.dma_start(out=gk, in_=kn_row[j:j + 1, :].partition_broadcast(128)), writes=["gk"])
        if is_sample:
            S.dma("sp", lambda e: e.dma_start(out=cosT, in_=cos_in.rearrange("t p f -> p t f")), writes=["cosT"])
            S.dma("sp", lambda e: e.dma_start(out=sinT, in_=sin_in.rearrange("t p f -> p t f")), writes=["sinT"])
            S.dma("pool", lambda e: e.dma_start(out=ckt, in_=ck_in[j].rearrange("(t p) f -> p t f", p=128)),
                  writes=["ckt"])
            S.dma("pool", lambda e: e.dma_start(out=Vt[:, 0:4, :], in_=cv_in[j].rearrange("(t p) f -> p t f", p=128)),
                  writes=[("Vt", t) for t in range(4)])
        prenorm(l, tile0, G, v)
        if is_sample:
            for t in range(4):
                bk = tr_bank()
                for h in range(2):
                    tr(psb(bk)[:, h * 128:(h + 1) * 128], ckt[:, t, h * 128:(h + 1) * 128], ["ckt"], [("ps", bk)],
                       signal=(h == 1))
                cp("act", KT[:, :, t * 128:(t + 1) * 128], psb(bk)[:, 0:256].rearrange("p (h t) -> p h t", t=128),
                   [("ps", bk)], [("KT", t)])

        def normrope(src_ps, pskeys, nh, gain, it, dst_bf, dkeys, ctr):
            o4 = 4 * (ctr % 2)
            A, B, C = sc(o4, nh * 128), sc(o4 + 1, nh * 128), sc(o4 + 2, nh * 128)
            KA, KB, KC = ("sc", o4), ("sc", o4 + 1), ("sc", o4 + 2)
            ssq = small[:, 48 + 4 * (ctr % 2):48 + 4 * (ctr % 2) + nh]
            sk = ("ssq3", ctr % 2)
            v3 = lambda ap: ap.rearrange("p (h d) -> p h d", d=128)
            act(A, src_ps, AF.Square, pskeys, [KA])
            S.op("dve", lambda e: e.tensor_reduce(out=ssq, in_=v3(A), axis=AX.X, op=ALU.add), [KA], [sk])
            act(ssq, ssq, AF.Sqrt, [sk], [sk], scale=1.0 / 128, bias=EPS)
            S.op("dve", lambda e: e.reciprocal(ssq, ssq), [sk], [sk])
            tt("dve", v3(B), v3(src_ps), ssq.unsqueeze(2).to_broadcast([128, nh, 128]), ALU.mult,
               pskeys + [sk], [KB])
            gainb = gain.unsqueeze(1).to_broadcast([128, nh, 128])
            if not is_sample:
                tt("dve", v3(dst_bf), v3(B), gainb, ALU.mult, [KB, "gq", "gk"], dkeys)
                return B
            CG, SG = (CGq, SGq) if nh == 4 else (CGk, SGk)
            c2 = CG[:, it, :].unsqueeze(1).to_broadcast([128, nh, 128])
            tt("dve", v3(A), v3(B), c2, ALU.mult, [KB, "ropetab"], [KA])
            B4 = B.rearrange("p (h i two) -> p h i two", i=64, two=2)
            C4 = C.rearrange("p (h i two) -> p h i two", i=64, two=2)
            s4 = SG[:, it, :].rearrange("p (i two) -> p i two", two=2)
            for e_ in range(2):
                tt("pool" if e_ == 0 else "dve", C4[:, :, :, e_], B4[:, :, :, 1 - e_],
                   s4[:, :, e_].unsqueeze(1).to_broadcast([128, nh, 64]), ALU.mult, [KB, "ropetab", KC], [(KC[0], KC[1], e_)])
            tt("pool", dst_bf, A, C, ALU.add, [KA, (KC[0], KC[1], 0), (KC[0], KC[1], 1)], dkeys + [KC])
            return B

        if is_sample:
            for (CG, SG, gain) in ((CGq, SGq, gq), (CGk, SGk, gk)):
                gb8 = gain.unsqueeze(1).to_broadcast([128, 8, 128])
                tt("dve", CG, cosT, gb8, ALU.mult, ["cosT", "gq", "gk"], ["ropetab"])
                g2 = gain.rearrange("p (i two) -> p i two", two=2)
                S4 = SG.rearrange("p t (i two) -> p t i two", two=2)
                s8 = sinT.rearrange("p t (i two) -> p t i two", two=2)
                for e_ in range(2):
                    tt("dve", S4[:, :, :, e_], s8[:, :, :, e_],
                       g2[:, :, 1 - e_].unsqueeze(1).to_broadcast([128, 8, 64]), ALU.mult,
                       ["sinT", "gq", "gk"], ["ropetab"])
        queue_pass_mod(l, v)
        if stop == 11:
            return
        items = []
        for qb in range(2):
            for it in range(ntiles):
                items.append(("q", qb, it))
        for it in range(ntiles):
            items.append(("kv", 0, it))
        slots = {}
        pend = []

        def stage_a(ctr, item):
            kind, qb, it = item
            key = (kind, qb)
            if key not in slots:
                slots[key] = w_get(("ain", j), qb * 512 if kind == "q" else 1024)
            slot = slots[key]
            bk = gen_bank()
            proj_tm(slot, it * 128, 128, bk)
            pk = [("ps", bk)]
            dk = [("sc", 4 * (ctr % 2) + 3)]
            if kind == "q":
                qr = scb(4 * (ctr % 2) + 3, 512)
                normrope(PS[:, bk, :], pk, 4, gq, it, qr, dk, ctr)
                return (kind, qb, it, qr, dk, bk, None)
            kr = scb(4 * (ctr % 2) + 3, 256)
            Bn = normrope(PS[:, bk, 0:256], pk, 2, gk, it, kr, dk, ctr)
            kt_idx = nkt_cache + it
            cp("dve", Vt[:, kt_idx, :], PS[:, bk, 256:512], pk, [("Vt", kt_idx)])
            if not is_sample:
                sq = it % 2
                tt("dve", stg[:, sq, 0:256].rearrange("p (h d) -> p h d", d=128),
                   Bn.rearrange("p (h d) -> p h d", d=128), gk.unsqueeze(1).to_broadcast([128, 2, 128]), ALU.mult,
                   [("sc", 4 * (ctr % 2) + 1), "gk"], [("stg", sq, 0)])
                cp("dve", stg[:, sq, 256:512], PS[:, bk, 256:512], pk, [("stg", sq, 1)])
                si, tt0 = divmod(it * 128, 256)
                S.dma("sp", lambda e, si=si, tt0=tt0, sq=sq: e.dma_start(out=ck_out[si, j, tt0:tt0 + 128, :],
                                                                         in_=stg[:, sq, 0:256]), reads=[("stg", sq, 0)])
                S.dma("sp", lambda e, si=si, tt0=tt0, sq=sq: e.dma_start(out=cv_out[si, j, tt0:tt0 + 128, :],
                                                                         in_=stg[:, sq, 256:512]), reads=[("stg", sq, 1)])
            return (kind, qb, it, kr, dk, bk, kt_idx)

        def stage_b(st):
            kind, qb, it, rr, dk, bk, kt_idx = st
            bt = tr_bank()
            if kind == "q":
                for h in range(4):
                    tr(psb(bt)[:, h * 128:(h + 1) * 128], rr[:, h * 128:(h + 1) * 128], dk, [("ps", bt)],
                       signal=(h == 3))
                cp("act", QT[:, qb * 4:qb * 4 + 4, it * 128:(it + 1) * 128],
                   psb(bt)[:, 0:512].rearrange("p (h t) -> p h t", t=128), [("ps", bt)], [("QT", qb, it)])
            else:
                for h in range(2):
                    tr(psb(bt)[:, h * 128:(h + 1) * 128], rr[:, h * 128:(h + 1) * 128], dk, [("ps", bt)],
                       signal=(h == 1))
                cp("act", KT[:, :, kt_idx * 128:(kt_idx + 1) * 128],
                   psb(bt)[:, 0:256].rearrange("p (h t) -> p h t", t=128), [("ps", bt)], [("KT", kt_idx)])

        stop_items = len(items)
        if stop == 12:
            stop_items = 2 * ntiles
        for ctr, item in enumerate(items[:stop_items]):
            st = stage_a(ctr, item)
            if pend:
                stage_b(pend.pop(0))
            pend.append(st)
        while pend:
            stage_b(pend.pop(0))
        if stop in (12, 13):
            return
        for gb in range(2):
            slot = w_get(("ain", j), 1536 + gb * 512)
            for n in range(4):
                for g in range(G):
                    bk = gen_bank()
                    proj_fm(slot, n, g, bk)
                    act(gT[:, gb * 4 + n, g * 512:(g + 1) * 512], PS[:, bk, :], AF.Silu, [("ps", bk)],
                        [("gT", gb * 4 + n, g)])
        if stop == 14:
            return
        scale = 128.0 ** -0.5
        units = []
        itc = 0
        for (s0, Ls) in seqs:
            kt_lo = 0 if is_sample else s0 // 128
            nkt = (nkt_cache + ntiles) if is_sample else Ls // 128
            for h in range(8):
                for q0 in range(s0, s0 + Ls, 512):
                    nq = min(512, s0 + Ls - q0)
                    for ki in range(nkt):
                        units.append(dict(h=h, q0=q0, nq=nq, ki=ki, nkt=nkt, kt=kt_lo + ki, itc=itc))
                    itc += 1
        LA = 2

        def emit_S(idx, u):
            h, q0, nq, kt = u["h"], u["q0"], u["nq"], u["kt"]
            bS = idx % 3
            mm(PS[:, bS, 0:nq], KT[:, h // 4, kt * 128:(kt + 1) * 128], QT[:, h, q0:q0 + nq], True, True,
               [("KT", kt)] + [("QT", h // 4, t) for t in range(q0 // 128, (q0 + nq) // 128)],
               [("ps", bS)], signal=True)

        def emit_rest(idx, u):
            h, q0, nq, kt, ki, nkt, ic = u["h"], u["q0"], u["nq"], u["kt"], u["ki"], u["nkt"], u["itc"]
            kvh = h // 4
            bS = idx % 3
            pb = idx % 3
            bO = 4 + (ic % 2)
            bD = 6 + (ic % 2)
            act(pT[:, pb, 0:nq], PS[:, bS, 0:nq], AF.Exp, [("ps", bS)], [("pT", pb)], scale=scale)
            mm(PS[:, bO, 0:nq], Vt[:, kt, kvh * 128:(kvh + 1) * 128], pT[:, pb, 0:nq], ki == 0,
               ki == nkt - 1, [("Vt", kt), ("pT", pb)], [("ps", bO)], signal=False)
            mm(PS[:, bD, 0:nq], onesb[:], pT[:, pb, 0:nq], ki == 0, ki == nkt - 1,
               [("pT", pb), "onesb"], [("ps", bD)], signal=True)
            if ki == nkt - 1:
                gg = q0 // 512
                o4 = 2 * (ic % 2)
                R1, R2 = sc(o4, nq), sc(o4 + 1, nq)
                S.op("dve", lambda e, R1=R1, bD=bD, nq=nq: e.reciprocal(R1, PS[:, bD, 0:nq]),
                     [("ps", bD)], [("sc", o4)])
                tt("dve", R2, PS[:, bO, 0:nq], R1, ALU.mult, [("ps", bO), ("sc", o4)], [("sc", o4 + 1)])
                gv = gT[:, h, q0:q0 + nq]
                tt("pool", gv, gv, R2, ALU.mult, [("gT", h, gg), ("sc", o4 + 1)], [("gT", h, gg)])

        dstep = max(1, len(units) // 10)
        for idx in range(len(units) + LA):
            if deferred and idx % dstep == dstep - 1:
                run_deferred(3)
            if idx < len(units):
                emit_S(idx, units[idx])
            if idx >= LA:
                emit_rest(idx - LA, units[idx - LA])

        if stop == 15:
            return

        def zin_fn(kc, it):
            return gT[:, kc, it * 128:(it + 1) * 128]

        def zin_keys(kc, it):
            return [("gT", kc, (it * 128) // 512)]

        out_proj_and_residual(l, tile0, ntiles, v, zin_fn, zin_keys, ("aout", j))
        S.barrier()

    for l in range(nlayers):
        j = l // 2
        if l == 0:
            queue_mod_ss(0)
            while deferred:
                run_deferred(6)
        if l % 2 == 0:
            rec_pass(l, j, 0, 4, 0, [(0, 256), (256, 256)], False)
            if not os.environ.get("SKIPS"):
                rec_pass(l, j, 4, 8, 1, [(0, 1024)], True)
        else:
            att_pass(l, j, 0, 4, 0, [(0, 256), (256, 256)], False)
            att_pass(l, j, 4, 8, 1, [(0, 1024)], True)

    for t in range(12):
        S.dma("sp", lambda e, t=t: e.dma_start(out=y_out[t], in_=X[:, t, :]), reads=[("X", t)])

    if wlist_in is not None:
        S.emit_all()
    es.close()
    return nc, dbg_out, wcollect


def _consts():
    ident = np.eye(128, dtype=np.float32)
    ones = np.ones((128, 128), np.float32)
    s = np.arange(64)[:, None]
    t = np.arange(64)[None, :]
    mk = np.stack([(s <= t), (s >= t)], axis=1).astype(np.float32)
    mk = np.ascontiguousarray(np.tile(mk, (1, 1, 4)))
    rmask = np.ones((128, 512), np.float32)
    rmask[:, ::CH] = 0.0

    def icnt(L):
        tt_ = np.arange(L)
        out = []
        for win in (2, 4, 8, 16):
            lo = np.clip(tt_ - win // 2, 0, L)
            hi = np.clip(tt_ + win // 2, 0, L)
            out.append(1.0 / (hi - lo).astype(np.float32))
        return np.stack(out).astype(np.float32)

    tpos = np.arange(1024)
    row = (tpos // 64).astype(np.float32)
    col = (tpos % 64).astype(np.float32)
    inv = (10000.0 ** (-np.arange(0, 64, 2, dtype=np.float32) / 64)).astype(np.float32)
    ang = np.concatenate([row[:, None] * inv[None, :], col[:, None] * inv[None, :]], axis=-1).astype(np.float32)
    cos_t = np.repeat(np.cos(ang).astype(np.float32), 2, axis=-1).reshape(8, 128, 128)
    sn = np.sin(ang).astype(np.float32)
    sin_t = np.stack([-sn, sn], axis=-1).reshape(8, 128, 128)
    return dict(ident=ident, ones=ones, mk=mk, rmask=rmask, icnt_s=icnt(1024), icnt_p=icnt(256),
                cos_t=cos_t, sin_t=sin_t)


def _prep_inputs(x_prompt, x_sample, c, state_hgrn, cache_k, cache_v, c_ctx, ada_w, ada_b,
                 norm_pre, norm_post, rec_w_in, rec_lb_logits, rec_head_norm, pool_w, pool_scale,
                 rec_w_out, att_w_in, att_q_norm, att_k_norm, att_w_out):
    f = lambda a: np.ascontiguousarray(np.asarray(a, dtype=np.float32))
    shared = dict(
        ada_w=f(ada_w),
        ada_bT=f(np.asarray(ada_b).reshape(4, 24, 128).transpose(2, 0, 1)),
        ada_bg=f(np.asarray(ada_b)[:, 2048:3072]),
        npreT=f(np.asarray(norm_pre).reshape(4, 8, 128).transpose(2, 0, 1)),
        npost=f(norm_post),
        rec_w_in=f(rec_w_in), rec_w_out=f(rec_w_out), att_w_in=f(att_w_in), att_w_out=f(att_w_out),
        lbT=f(np.asarray(rec_lb_logits).reshape(2, 2, 4, 128).transpose(3, 0, 1, 2)),
        hnT=f(np.asarray(rec_head_norm).T),
        pool_w=f(pool_w),
        pscT=f(np.asarray(pool_scale).reshape(2, 4, 128).transpose(2, 0, 1)),
        qn_row=f(att_q_norm), kn_row=f(att_k_norm),
    )
    shared.update(_consts())
    maps = []
    for i in range(NCORES):
        xin = np.concatenate([np.asarray(x_prompt[2 * i]).reshape(2, 128, 1024),
                              np.asarray(x_prompt[2 * i + 1]).reshape(2, 128, 1024),
                              np.asarray(x_sample[i]).reshape(8, 128, 1024)], axis=0)
        cvec = np.stack([np.asarray(c_ctx), np.asarray(c[i])], axis=0)
        cT = cvec.reshape(2, 8, 128).transpose(2, 1, 0)
        st = np.asarray(state_hgrn[i]).transpose(0, 3, 1, 2, 4).reshape(2, 128, 8, 128)
        m = dict(shared)
        m.update(x_in=f(xin), cT=f(cT), st_in=f(st),
                 ck_in=f(np.asarray(cache_k[i]).reshape(2, 512, 256)),
                 cv_in=f(np.asarray(cache_v[i]).reshape(2, 512, 256)))
        maps.append(m)
    return maps


_CACHE = {}


def kernel(**inputs):
    maps = _prep_inputs(**inputs)
    if "nc" not in _CACHE:
        _CACHE["nc"] = build_program()[0]
    nc = _CACHE["nc"]
    res = run_bass_kernel_spmd(nc, maps, core_ids=list(range(NCORES)))
    R = res.results
    y_prompt = np.zeros((16, 256, 1024), np.float32)
    y_sample = np.zeros((8, 1024, 1024), np.float32)
    new_state = np.zeros((16, 2, 2, 4, 128, 128), np.float32)
    new_k = np.zeros((16, 2, 256, 2, 128), np.float32)
    new_v = np.zeros((16, 2, 256, 2, 128), np.float32)
    for i in range(NCORES):
        y = np.asarray(R[i]["y_out"])
        y_prompt[2 * i] = y[0:2].reshape(256, 1024)
        y_prompt[2 * i + 1] = y[2:4].reshape(256, 1024)
        y_sample[i] = y[4:12].reshape(1024, 1024)
        new_state[2 * i:2 * i + 2] = np.asarray(R[i]["st_out"])
        new_k[2 * i:2 * i + 2] = np.asarray(R[i]["ck_out"]).reshape(2, 2, 256, 2, 128)
        new_v[2 * i:2 * i + 2] = np.asarray(R[i]["cv_out"]).reshape(2, 2, 256, 2, 128)
    return (y_prompt, y_sample, new_state, new_k, new_v)
```

```python
import numpy as np
from contextlib import ExitStack
import concourse.bass as bass
import concourse.mybir as mybir
from concourse.bass_utils import run_bass_kernel_spmd

F32 = mybir.dt.float32
BF16 = mybir.dt.bfloat16
AF = mybir.ActivationFunctionType
ALU = mybir.AluOpType
AX = mybir.AxisListType

ENGS = ["pe", "act", "dve", "pool", "sp"]
EPS = 1e-6
F_MIN = 1e-6
NCORES = 8
CH = 64
CSH = 20.0
NRING = 3


class Sched:
    def __init__(self, nc, n_dma_sems=16):
        self.nc = nc
        self.ops = {e: [] for e in ENGS}
        self.last_w = {}
        self.readers = {}
        self.n_dma_sems = n_dma_sems
        self.dma_slot_val = {}
        self.dma_rr = {"sp": 0, "pool": 0}
        self.all_dma_tokens = []
        self.pending = {e: set() for e in ENGS}
        self.trace = None

    def _deps(self, reads, writes):
        deps = set()
        for k in reads:
            t = self.last_w.get(k)
            if t is not None:
                deps.add(t)
        for k in writes:
            t = self.last_w.get(k)
            if t is not None:
                deps.add(t)
            for r in self.readers.get(k, ()):
                deps.add(r)
        return deps

    def _commit(self, tok, reads, writes):
        for k in writes:
            self.last_w[k] = tok
            self.readers[k] = []
        for k in reads:
            self.readers.setdefault(k, []).append(tok)

    def op(self, eng, emit, reads=(), writes=(), signal=True):
        deps = self._deps(reads, writes)
        deps |= self.pending[eng]
        self.pending[eng] = set()
        idx = len(self.ops[eng])
        tok = ("e", eng, idx)
        if eng == "pe":
            deps = {d for d in deps if not (d[0] == "e" and d[1] == "pe")}
        self.ops[eng].append(dict(emit=emit, deps=deps, signal=signal, tok=tok, dma=None))
        self._commit(tok, reads, writes)
        return tok

    def dma(self, q, emit, reads=(), writes=()):
        deps = self._deps(reads, writes)
        deps |= self.pending[q]
        self.pending[q] = set()
        slot = self.dma_rr[q]
        self.dma_rr[q] = (slot + 1) % self.n_dma_sems
        prev = self.dma_slot_val.get((q, slot), 0)
        if prev > 0:
            deps.add(("d", (q, slot), prev))
        val = prev + 16
        self.dma_slot_val[(q, slot)] = val
        tok = ("d", (q, slot), val)
        self.ops[q].append(dict(emit=emit, deps=deps, signal=False, tok=tok, dma=(q, slot)))
        self._commit(tok, reads, writes)
        self.all_dma_tokens.append(tok)
        return tok

    def barrier(self):
        toks = set()
        for e in ENGS:
            for i in range(len(self.ops[e]) - 1, -1, -1):
                if self.ops[e][i]["dma"] is None:
                    toks.add(self.ops[e][i]["tok"])
                    break
        for k, v in self.dma_slot_val.items():
            toks.add(("d", k, v))
        for e in ENGS:
            self.pending[e] |= toks

    def emit_all(self):
        nc = self.nc
        with ExitStack() as es:
            esem = {e: es.enter_context(nc.semaphore("sem_" + e)) for e in ENGS}
            dsem = {}
            for k in self.dma_slot_val:
                dsem[k] = es.enter_context(nc.semaphore("dsem_%s_%d" % k))
            counts = {}
            for e in ENGS:
                c = 0
                arr = []
                for o in self.ops[e]:
                    if o["signal"] and o["dma"] is None:
                        c += 1
                    arr.append(c)
                res = [None] * len(arr)
                nxt = None
                for i in range(len(arr) - 1, -1, -1):
                    o = self.ops[e][i]
                    if o["signal"] and o["dma"] is None:
                        nxt = arr[i]
                    res[i] = nxt
                counts[e] = res

            def resolve(tok):
                if tok[0] == "e":
                    v = counts[tok[1]][tok[2]]
                    assert v is not None, ("dep on op with no later signal", tok)
                    return ("e", tok[1]), esem[tok[1]], v
                return tok[1], dsem[tok[1]], tok[2]

            block = es.enter_context(nc.Block())

            def run(ename, eobj):
                seen = {}
                for o in self.ops[ename]:
                    waits = {}
                    for d in o["deps"]:
                        key, sem, v = resolve(d)
                        if v > waits.get(key, (None, 0))[1]:
                            waits[key] = (sem, v)
                    wl = []
                    for key, (sem, v) in waits.items():
                        if seen.get(key, 0) >= v:
                            continue
                        eobj.wait_ge(sem, v)
                        seen[key] = v
                        wl.append((key, v))
                    if self.trace is not None:
                        self.trace.append((ename, o["tok"], wl, o["signal"], o.get("tag")))
                    ins = o["emit"](eobj)
                    if o["dma"] is not None:
                        ins.then_inc(dsem[o["dma"]], 16)
                    elif o["signal"]:
                        ins.then_inc(esem[ename], 1)
                if ename == "sp":
                    for k, v in self.dma_slot_val.items():
                        eobj.wait_ge(dsem[k], v)

            @block.sync
            def _(e):
                run("sp", e)

            @block.tensor
            def _(e):
                run("pe", e)

            @block.scalar
            def _(e):
                run("act", e)

            @block.vector
            def _(e):
                run("dve", e)

            @block.gpsimd
            def _(e):
                run("pool", e)


def build_program(nlayers=4, dbg_names=(), stop=99):
    wl = _build(nlayers, (), stop, None)[2]
    nc, dbg_out, _ = _build(nlayers, dbg_names, stop, wl)
    return nc, dbg_out


def _build(nlayers, dbg_names, stop, wlist_in):
    nc = bass.Bass("TRN2", target_bir_lowering=False)
    es = ExitStack()

    def din(name, shape):
        return nc.dram_tensor(name, list(shape), F32, kind="ExternalInput").ap()

    def dout(name, shape):
        return nc.dram_tensor(name, list(shape), F32, kind="ExternalOutput").ap()

    x_in = din("x_in", [12, 128, 1024])
    cT_in = din("cT", [128, 8, 2])
    st_in = din("st_in", [2, 128, 8, 128])
    ck_in = din("ck_in", [2, 512, 256])
    cv_in = din("cv_in", [2, 512, 256])
    ada_w = din("ada_w", [4, 1024, 3072])
    ada_bT = din("ada_bT", [128, 4, 24])
    ada_bg = din("ada_bg", [4, 1024])
    npreT = din("npreT", [128, 4, 8])
    npost = din("npost", [4, 1024])
    rec_w_in = din("rec_w_in", [2, 1024, 3584])
    rec_w_out = din("rec_w_out", [2, 1024, 1024])
    att_w_in = din("att_w_in", [2, 1024, 2560])
    att_w_out = din("att_w_out", [2, 1024, 1024])
    lbT_in = din("lbT", [128, 2, 2, 4])
    hnT_in = din("hnT", [128, 2])
    pool_w = din("pool_w", [2, 4, 128, 128])
    pscT_in = din("pscT", [128, 2, 4])
    qn_row = din("qn_row", [2, 128])
    kn_row = din("kn_row", [2, 128])
    ident_in = din("ident", [128, 128])
    ones_in = din("ones", [128, 128])
    mk_in = din("mk", [64, 2, 256])
    rmask_in = din("rmask", [128, 512])
    icnt_s = din("icnt_s", [4, 1024])
    icnt_p = din("icnt_p", [4, 256])
    cos_in = din("cos_t", [8, 128, 128])
    sin_in = din("sin_t", [8, 128, 128])

    y_out = dout("y_out", [12, 128, 1024])
    st_out = dout("st_out", [2, 2, 2, 4, 128, 128])
    ck_out = dout("ck_out", [2, 2, 256, 256])
    cv_out = dout("cv_out", [2, 2, 256, 256])
    dbg_out = {}

    def sb(name, shape, dt, stack=None):
        return (stack or es).enter_context(nc.sbuf_tensor("sb_" + name, list(shape), dt))

    S = Sched(nc)

    X = sb("X", [128, 12, 1024], F32)
    ring = sb("ring", [128, NRING, 8, 512], BF16)
    hT = sb("hT", [128, 8, 1024], BF16)
    GT = sb("GT", [128, 2, 1024], F32)
    SC = sb("SC", [128, 8, 520], F32)
    idb = sb("idb", [128, 128], BF16)
    onesb = sb("onesb", [128, 128], BF16)
    onesf = sb("onesf", [128, 128], F32)
    rmask = sb("rmask", [128, 512], F32)
    mkf = sb("mkf", [64, 2, 256], F32)
    cT = sb("cTs", [128, 8, 2], F32)
    scT = sb("scT", [128, 8, 2], BF16)
    sc_rep = sb("sc_rep", [128, 8, 2, 128], BF16)
    adabT = sb("adabT", [128, 4, 24], F32)
    npre = sb("npre", [128, 4, 8], F32)
    lbl = sb("lbl", [128, 2, 2, 4], F32)
    lb = sb("lb", [128, 2, 2, 4], F32)
    oml = sb("oml", [128, 2, 2, 4], F32)
    hn = sb("hn", [128, 2], F32)
    psc = sb("psc", [128, 2, 4], F32)
    poolw = sb("poolw", [128, 8, 128], BF16)
    modT2 = sb("modT", [128, 2, 16, 2], F32)
    Gcol2 = sb("Gcol", [128, 2, 8, 2], F32)
    small = sb("small", [128, 64], F32)
    PS = es.enter_context(nc.psum_tensor("PS", [128, 8, 512], F32))
    ARENA_BYTES = 80 * 1024
    arena = sb("arena", [128, ARENA_BYTES // 4], F32)
    arena_off = [0]

    def aalloc(shape, dt):
        esz = 2 if dt == BF16 else 4
        n = int(np.prod(shape[1:])) * esz
        n = (n + 63) // 64 * 64
        off = arena_off[0]
        assert off + n <= ARENA_BYTES, ("arena overflow", off, n)
        arena_off[0] = off + n
        ap = arena[0:shape[0], off // 4:(off + n) // 4]
        if dt == BF16:
            ap = ap.bitcast(BF16)
        ap = ap[:, 0:int(np.prod(shape[1:]))]
        if len(shape) == 3:
            ap = ap.rearrange("p (a b) -> p a b", b=shape[2])
        elif len(shape) == 4:
            ap = ap.rearrange("p (a b c) -> p a b c", b=shape[2], c=shape[3])
        return ap

    def psb(b):
        return PS[:, b, :].bitcast(BF16)

    def sc(i, n=512):
        return SC[:, i, 0:n]

    def scb(i, n=1024):
        return SC[:, i, :].bitcast(BF16)[:, 0:n]

    def sc2(i):
        return SC[:, 2 * i:2 * i + 2, :].rearrange("p a b -> p (a b)")

    def act(out, in_, func, reads, writes, **kw):
        S.op("act", lambda e: e.activation(out=out, in_=in_, func=func, **kw), reads, writes)

    def tt(eng, out, a, b, op, reads, writes):
        S.op(eng, lambda e: e.tensor_tensor(out, a, b, op), reads, writes)

    def ts(eng, out, a, s1, s2, op0, op1, reads, writes):
        if s2 is None:
            S.op(eng, lambda e: e.tensor_scalar(out, a, s1, None, op0), reads, writes)
        else:
            S.op(eng, lambda e: e.tensor_scalar(out, a, s1, s2, op0, op1), reads, writes)

    def stt(eng, out, in0, scalar, in1, op0, op1, reads, writes):
        S.op(eng, lambda e: e.scalar_tensor_tensor(out, in0, scalar, in1, op0, op1), reads, writes)

    def cp(eng, out, in_, reads, writes):
        if eng == "act":
            S.op("act", lambda e: e.activation(out=out, in_=in_, func=AF.Copy), reads, writes)
        else:
            S.op(eng, lambda e: e.tensor_copy(out, in_), reads, writes)

    def mm(out, lhsT, rhs, start, stop, reads, writes, signal):
        S.op("pe", lambda e: e.matmul(out, lhsT, rhs, start=start, stop=stop), reads, writes, signal=signal)

    def tr(out, in_, reads, writes, signal):
        n = in_.shape[0]
        S.op("pe", lambda e: e.transpose(out, in_, idb[0:n, 0:n]), list(reads) + ["idb"], writes, signal=signal)

    def dbg(name, ap, shape, reads, dt=F32):
        if name not in dbg_names:
            return
        d = nc.dram_tensor("dbg_" + name, list(shape), dt, kind="ExternalOutput").ap()
        dbg_out[name] = d
        S.dma("sp", lambda e: e.dma_start(out=d, in_=ap), reads=reads)

    bank_rr = {"gen": 0, "tr": 0}

    def gen_bank():
        b = bank_rr["gen"]
        bank_rr["gen"] ^= 1
        return b

    def tr_bank():
        b = 2 + bank_rr["tr"]
        bank_rr["tr"] ^= 1
        return b

    wlist = list(wlist_in) if wlist_in is not None else []
    wcollect = []
    wstate = {"issued": 0, "next": 0}

    def w_issue_upto(k):
        while wstate["issued"] <= k and wstate["issued"] < len(wlist):
            i = wstate["issued"]
            wap, c0, ncol = wlist[i]
            slot = i % NRING
            src = wap[:, c0:c0 + ncol].rearrange("(c p) n -> p c n", p=128)
            S.dma("pool", lambda e, slot=slot, src=src, ncol=ncol: e.dma_start(out=ring[:, slot, :, 0:ncol], in_=src),
                  writes=[("ring", slot)])
            wstate["issued"] += 1

    def w_get(wkey, c0, ahead=NRING - 1):
        k = wstate["next"]
        wstate["next"] += 1
        wcollect.append((wkey, c0))
        if wlist_in is not None:
            assert wlist_in[k][3] == (wkey, c0), ("weight stream order mismatch", k, wlist_in[k][3], (wkey, c0))
        w_issue_upto(k + ahead)
        return k % NRING

    WSRC = {}
    for l_ in range(4):
        WSRC[("ada", l_)] = ada_w[l_]
    for j_ in range(2):
        WSRC[("rin", j_)] = rec_w_in[j_]
        WSRC[("rout", j_)] = rec_w_out[j_]
        WSRC[("ain", j_)] = att_w_in[j_]
        WSRC[("aout", j_)] = att_w_out[j_]
    if wlist_in is not None:
        wlist = [(WSRC[wk], c0, 512) for (wk, c0) in wlist_in]
        wlist_in = [(WSRC[wk], c0, 512, (wk, c0)) for (wk, c0) in wlist_in]

    for t in range(12):
        S.dma("sp", lambda e, t=t: e.dma_start(out=X[:, t, :], in_=x_in[t]), writes=[("X", t)])
    S.dma("pool", lambda e: e.dma_start(out=idb[:], in_=ident_in), writes=["idb"])
    S.dma("pool", lambda e: e.dma_start(out=onesb[:], in_=ones_in), writes=["onesb"])
    S.dma("sp", lambda e: e.dma_start(out=onesf[:], in_=ones_in), writes=["onesf"])
    S.dma("pool", lambda e: e.dma_start(out=poolw[:], in_=pool_w.rearrange("j g c d -> c (j g) d")), writes=["poolw"])
    S.dma("sp", lambda e: e.dma_start(out=rmask[:], in_=rmask_in), writes=["rmask"])
    S.dma("sp", lambda e: e.dma_start(out=mkf[:], in_=mk_in), writes=["mkf"])
    S.dma("sp", lambda e: e.dma_start(out=cT[:], in_=cT_in), writes=["cT"])
    S.dma("sp", lambda e: e.dma_start(out=adabT[:], in_=ada_bT), writes=["adabT"])
    S.dma("sp", lambda e: e.dma_start(out=npre[:], in_=npreT), writes=["npre"])
    S.dma("sp", lambda e: e.dma_start(out=lbl[:], in_=lbT_in), writes=["lbl"])
    S.dma("sp", lambda e: e.dma_start(out=hn[:], in_=hnT_in), writes=["hn"])
    S.dma("sp", lambda e: e.dma_start(out=psc[:], in_=pscT_in), writes=["psc"])
    w_issue_upto(NRING - 2)

    act(scT[:], cT[:], AF.Silu, ["cT"], ["scT"])
    cp("dve", sc_rep[:], scT[:].unsqueeze(3).to_broadcast([128, 8, 2, 128]), ["scT"], ["sc_rep"])
    S.op("dve", lambda e: e.memset(lb[:, 0, :, :], 0.0), [], ["lb0"])
    tt("dve", small[:, 0:8], lbl[:, 1, :, :].rearrange("p a b -> p (a b)"),
       lbl[:, 0, :, :].rearrange("p a b -> p (a b)"), ALU.subtract, ["lbl"], ["small"])
    act(lb[:, 1, :, :].rearrange("p a b -> p (a b)"), small[:, 0:8], AF.Sigmoid, ["small"], ["lb1"])
    ts("dve", oml[:].rearrange("p j a b -> p (j a b)"), lb[:].rearrange("p j a b -> p (j a b)"),
       -1.0, 1.0, ALU.mult, ALU.add, ["lb0", "lb1"], ["oml"])
    LBK = ["lb0", "lb1", "oml"]

    deferred = []

    def run_deferred(bk, n=1):
        for _ in range(n):
            if deferred:
                deferred.pop(0)(bk)

    def queue_mod_ss(l):
        modT = modT2[:, l % 2]
        Gcol = Gcol2[:, l % 2]

        def blk(b):
            def f(bk):
                slot = w_get(("ada", l), b * 512)
                for n in range(4):
                    for kc in range(8):
                        mm(PS[:, bk, n * 2:n * 2 + 2], ring[:, slot, kc, n * 128:(n + 1) * 128], scT[:, kc, :],
                           kc == 0, kc == 7, [("ring", slot), "scT"], [("ps", bk)], signal=(kc == 7 and n == 3))
                tt("dve", modT[:, 4 * b:4 * b + 4, :], PS[:, bk, 0:8].rearrange("p (c v) -> p c v", v=2),
                   adabT[:, l, 4 * b:4 * b + 4].unsqueeze(2).to_broadcast([128, 4, 2]), ALU.add,
                   [("ps", bk), "adabT"], [("modT", l % 2, b)])
                if b == 3:
                    stt("dve", Gcol, modT[:, 8:16, :], 1.0, npre[:, l, :].unsqueeze(2).to_broadcast([128, 8, 2]),
                        ALU.add, ALU.mult, [("modT", l % 2, 2), ("modT", l % 2, 3), "npre"], [("Gcol", l % 2)])
            return f
        for b in range(4):
            deferred.append(blk(b))

    def queue_mod_gate(l, v):
        def blk(b):
            def f(bk):
                if b == 0:
                    S.dma("sp", lambda e: e.dma_start(out=sc2(2)[:, 0:1024],
                                                      in_=ada_bg[l:l + 1, :].partition_broadcast(128)),
                          writes=[("sc", 4), ("sc", 5)])
                    S.dma("sp", lambda e: e.dma_start(out=sc2(3)[:, 0:1024],
                                                      in_=npost[l:l + 1, :].partition_broadcast(128)),
                          writes=[("sc", 6), ("sc", 7)])
                slot = w_get(("ada", l), (4 + b) * 512)
                for kc in range(8):
                    mm(PS[:, bk, :], sc_rep[:, kc, v, :], ring[:, slot, kc, :], kc == 0, kc == 7,
                       [("ring", slot), "sc_rep"], [("ps", bk)], signal=(kc == 7))
                tt("dve", GT[:, v, b * 512:(b + 1) * 512], PS[:, bk, :], sc2(2)[:, b * 512:(b + 1) * 512], ALU.add,
                   [("ps", bk), ("sc", 4), ("sc", 5)], [("GT", v, b)])
                tt("dve", GT[:, v, b * 512:(b + 1) * 512], GT[:, v, b * 512:(b + 1) * 512],
                   sc2(3)[:, b * 512:(b + 1) * 512], ALU.mult,
                   [("GT", v, b), ("sc", 6), ("sc", 7)], [("GT", v, b)])
            return f
        for b in range(2):
            deferred.append(blk(b))

    def queue_pass_mod(l, v):
        if v == 1 and l + 1 < nlayers:
            queue_mod_ss(l + 1)
        queue_mod_gate(l, v)

    def prenorm(l, tile0, ngroups, v):
        modT = modT2[:, l % 2]
        Gcol = Gcol2[:, l % 2]
        MK = [("Gcol", l % 2), ("modT", l % 2, 0), ("modT", l % 2, 1)]
        for g in range(ngroups):
            for jt in range(4):
                t = tile0 + g * 4 + jt
                ssq = small[:, 16 + jt:17 + jt]
                rs = small[:, 20 + jt:21 + jt]
                act(scb(4 + jt), X[:, t, :], AF.Square, [("X", t)], [("sc", 4 + jt), ("ssq", jt)], accum_out=ssq)
                act(rs, ssq, AF.Sqrt, [("ssq", jt)], [("rs", jt)], scale=1.0 / 1024, bias=EPS)
                S.op("dve", lambda e, rs=rs: e.reciprocal(rs, rs), [("rs", jt)], [("rs", jt)])
                ts("dve", scb(jt), X[:, t, :], rs, None, ALU.mult, None, [("X", t), ("rs", jt)], [("sc", jt)])
            for cpair in range(4):
                bk = tr_bank()
                for cc in range(2):
                    c = 2 * cpair + cc
                    for jt in range(4):
                        tr(psb(bk)[:, cc * 512 + jt * 128: cc * 512 + (jt + 1) * 128],
                           scb(jt)[:, c * 128:(c + 1) * 128], [("sc", jt)], [("ps", bk)],
                           signal=(cc == 1 and jt == 3))
                for cc in range(2):
                    c = 2 * cpair + cc
                    dst = hT[:, c, g * 512:(g + 1) * 512]
                    src = psb(bk)[:, cc * 512:(cc + 1) * 512]
                    if cpair % 2 == 0:
                        act(dst, src, AF.Identity, [("ps", bk)] + MK, [("hT", g, c)],
                            scale=Gcol[:, c, v:v + 1], bias=modT[:, c, v:v + 1])
                    else:
                        ts("dve", dst, src, Gcol[:, c, v:v + 1], modT[:, c, v:v + 1], ALU.mult, ALU.add,
                           [("ps", bk)] + MK, [("hT", g, c)])

    def hT_keys(g):
        return [("hT", g, c) for c in range(8)]

    def proj_fm(slot, n, g, bk):
        for kc in range(8):
            mm(PS[:, bk, :], ring[:, slot, kc, n * 128:(n + 1) * 128], hT[:, kc, g * 512:(g + 1) * 512],
               kc == 0, kc == 7, [("ring", slot)] + hT_keys(g), [("ps", bk)], signal=(kc == 7))

    def proj_tm(slot, tok0, ntok, bk):
        g = tok0 // 512
        for kc in range(8):
            mm(PS[0:ntok, bk, :], hT[:, kc, tok0:tok0 + ntok], ring[:, slot, kc, :],
               kc == 0, kc == 7, [("ring", slot)] + hT_keys(g), [("ps", bk)], signal=(kc == 7))

    def out_proj_and_residual(l, tile0, ntiles, v, zin_fn, zin_keys_fn, wkey):
        while deferred:
            run_deferred(gen_bank())
        s0 = w_get(wkey, 0)
        s1 = w_get(wkey, 512, NRING - 2)
        slots = (s0, s1)
        for it in range(ntiles):
            t = tile0 + it
            b0 = 4 + 2 * (it % 2)
            for nb in range(2):
                for kc in range(8):
                    mm(PS[:, b0 + nb, :], zin_fn(kc, it), ring[:, slots[nb], kc, :], kc == 0, kc == 7,
                       [("ring", slots[nb])] + zin_keys_fn(kc, it), [("ps", b0 + nb)], signal=(kc == 7))
            po = PS[:, b0:b0 + 2, :].rearrange("p a b -> p (a b)")
            pk = [("ps", b0), ("ps", b0 + 1)]
            ssq = small[:, 24 + (it % 2):25 + (it % 2)]
            rs = small[:, 26 + (it % 2):27 + (it % 2)]
            jk = 6 + (it % 2)
            act(scb(jk), po, AF.Square, pk, [("sc", jk), ("ssq2", it % 2)], accum_out=ssq)
            act(rs, ssq, AF.Sqrt, [("ssq2", it % 2)], [("rs2", it % 2)], scale=1.0 / 1024, bias=EPS)
            S.op("dve", lambda e, rs=rs: e.reciprocal(rs, rs), [("rs2", it % 2)], [("rs2", it % 2)])
            tmpk = 2 * (it % 2)
            tmp = sc2(it % 2)[:, 0:1024]
            stt("dve", tmp, po, rs, GT[:, v, :], ALU.mult, ALU.mult,
                pk + [("rs2", it % 2), ("GT", v, 0), ("GT", v, 1)], [("sc", tmpk), ("sc", tmpk + 1)])
            tt("pool", X[:, t, :], X[:, t, :], tmp, ALU.add,
               [("X", t), ("sc", tmpk), ("sc", tmpk + 1)], [("X", t)])

    def rec_pass(l, j, tile0, ntiles, v, seqs, is_sample):
        L = ntiles * 128
        G = L // 512
        nch = L // CH
        arena_off[0] = 0
        qo = aalloc([128, 4, L], BF16)
        qp = [aalloc([128, 4, L], BF16) for r in range(2)]
        kp = [aalloc([128, 4, L], BF16) for r in range(2)]
        v_c = aalloc([64, nch, 512], BF16)
        ypool = aalloc([128, 4, L], BF16)
        tabS = aalloc([128, 2, 4, nch], F32)
        tabG = aalloc([128, 2, 4, nch], F32)
        tabA = aalloc([128, 2, 4, nch], F32)
        St = aalloc([128, 8, 128], F32)
        Stb = aalloc([128, 8, 128], BF16)
        kTt = aalloc([64, 8, 128], BF16)
        ATm = aalloc([64, 8, 64], BF16)
        Lseq = seqs[0][1]
        icnt = aalloc([128, Lseq], F32)

        if stop <= 1:
            return
        prenorm(l, tile0, G, v)
        src_ic = icnt_s if is_sample else icnt_p
        if stop <= 2:
            return

        queue_pass_mod(l, v)
        slot = w_get(("rin", j), 1024)
        for n in range(4):
            for g in range(G):
                bk = gen_bank()
                proj_fm(slot, n, g, bk)
                act(qo[:, n, g * 512:(g + 1) * 512], PS[:, bk, :], AF.Silu, [("ps", bk)], [("qo", n, g)])

        f_items = [(r, n, g) for r in range(2) for n in range(4) for g in range(G)]
        f_slots = {}

        def f_ctx(idx):
            r, n, g = f_items[idx]
            o4 = 4 * (idx % 2)
            Ts = [sc(o4 + i) for i in range(4)]
            Ks = [("sc", o4 + i) for i in range(4)]
            return r, n, g, Ts, Ks

        def f_s0(idx):
            r, n, g, (T1, T2, T3, T4), (K1, K2, K3, K4) = f_ctx(idx)
            if r not in f_slots:
                f_slots[r] = w_get(("rin", j), 1536 + 512 * r)
            bk = gen_bank()
            proj_fm(f_slots[r], n, g, bk)
            act(T1, PS[:, bk, :], AF.Exp, [("ps", bk)], [K1], scale=-1.0)

        def f_s1(idx):
            r, n, g, (T1, T2, T3, T4), (K1, K2, K3, K4) = f_ctx(idx)
            act(T1, T1, AF.Ln, [K1], [K1], bias=1.0)
            act(T1, T1, AF.Exp, [K1], [K1], scale=-1.0)
            ts("dve", T1, T1, oml[:, j, r, n:n + 1], lb[:, j, r, n:n + 1], ALU.mult, ALU.add, [K1] + LBK, [K1])
            ts("dve", T1, T1, F_MIN, 1.0, ALU.max, ALU.min, [K1], [K1])
            act(T2, T1, AF.Ln, [K1], [K2])
            act(T3, T1, AF.Identity, [K1], [K3], scale=-1.0, bias=1.0)

        def f_s2(idx):
            r, n, g, (T1, T2, T3, T4), (K1, K2, K3, K4) = f_ctx(idx)
            S.op("dve", lambda e: e.tensor_tensor_scan(T4, rmask[:], T2, 0.0, ALU.mult, ALU.add),
                 [K2, "rmask"], [K4])
            b3 = T4.rearrange("p (c t) -> p c t", t=CH)
            c0 = g * 8
            tk = ("tab", r, n, g)
            smt = small[:, 32 + 8 * (idx % 2):40 + 8 * (idx % 2)]
            smk = ("smt", idx % 2)
            tt("dve", smt, b3[:, :, CH - 1], b3[:, :, CH // 2 - 1], ALU.subtract, [K4], [smk])
            e_mid = tabS if r == 0 else tabG
            e_dif = tabG if r == 0 else tabS
            act(e_mid[:, r, n, c0:c0 + 8], b3[:, :, CH // 2 - 1], AF.Exp, [K4], [tk + (0,)],
                bias=(-CSH if r == 0 else CSH))
            act(e_dif[:, r, n, c0:c0 + 8], smt, AF.Exp, [smk], [tk + (1,)], bias=(CSH if r == 0 else -CSH))
            act(tabA[:, r, n, c0:c0 + 8], b3[:, :, CH - 1], AF.Exp, [K4], [tk + (2,)])
            tt("pool", T1.rearrange("p (c t) -> p c t", t=CH), b3,
               b3[:, :, CH // 2 - 1:CH // 2].to_broadcast([128, 8, CH]), ALU.subtract, [K4, K1], [K1])
            if r == 1:
                tt("dve", T1, T1, T2, ALU.subtract, [K1, K2], [K1])

        def f_s3(idx):
            r, n, g, (T1, T2, T3, T4), (K1, K2, K3, K4) = f_ctx(idx)
            sg = 1.0 if r == 0 else -1.0
            act(T2, T1, AF.Exp, [K1], [K2], scale=sg, bias=-CSH)
            act(T4, T1, AF.Exp, [K1], [K4], scale=-sg, bias=-CSH)
            tt("dve", qp[r][:, n, g * 512:(g + 1) * 512], qo[:, n, g * 512:(g + 1) * 512], T2, ALU.mult,
               [("qo", n, g), K2], [("qp", r, n, g)])
            tt("pool", kp[r][:, n, g * 512:(g + 1) * 512], T3, T4, ALU.mult, [K3, K4], [("kp", r, n, g)])

        NF = len(f_items)
        for idx in range(NF + 1):
            if idx < NF:
                f_s0(idx)
            if idx >= 1:
                f_s2(idx - 1)
            if idx < NF:
                f_s1(idx)
            if idx >= 1:
                f_s3(idx - 1)

        tg = "_%d_%d" % (l, int(is_sample))
        allk = lambda nm, *pre: [(nm,) + pre + (n, g) for n in range(4) for g in range(G)]
        dbg("hT" + tg, hT[:, :, 0:L], [128, 8, L], [("hT", g, c) for g in range(G) for c in range(8)], BF16)
        dbg("qp0" + tg, qp[0], [128, 4, L], allk("qp", 0), BF16)
        dbg("kp0" + tg, kp[0], [128, 4, L], allk("kp", 0), BF16)
        dbg("qp1" + tg, qp[1], [128, 4, L], allk("qp", 1), BF16)
        dbg("kp1" + tg, kp[1], [128, 4, L], allk("kp", 1), BF16)
        dbg("tabS" + tg, tabS, [128, 2, 4, nch], [("tab", r, n, g, k) for r in range(2) for n in range(4) for g in range(G) for k in range(3)])
        dbg("tabG" + tg, tabG, [128, 2, 4, nch], [("tab", r, n, g, k) for r in range(2) for n in range(4) for g in range(G) for k in range(3)])
        dbg("tabA" + tg, tabA, [128, 2, 4, nch], [("tab", r, n, g, k) for r in range(2) for n in range(4) for g in range(G) for k in range(3)])
        if stop <= 3:
            return
        slot = w_get(("rin", j), 2560)
        for c in range(nch):
            bk = gen_bank()
            proj_tm(slot, c * CH, CH, bk)
            cp("act" if c % 2 == 0 else "dve", v_c[:, c, :], PS[0:64, bk, :], [("ps", bk)], [("v_c", c)])

        slot = w_get(("rin", j), 512)
        for n in range(4):
            for g in range(G):
                bk = gen_bank()
                proj_fm(slot, n, g, bk)
                act(ypool[:, n, g * 512:(g + 1) * 512], PS[:, bk, :], AF.Silu, [("ps", bk)], [("yp", n, g)])

        slot = w_get(("rin", j), 0)
        nseq = len(seqs)
        Wd = nseq * (Lseq + 16)
        for n in range(4):
            win = (2, 4, 8, 16)[n]
            UB, WA, WB = sc2(0), sc2(1), sc2(2)
            DB = sc2(3).bitcast(BF16)
            KU, KA, KB, KD = [[("sc", 2 * i), ("sc", 2 * i + 1)] for i in range(4)]
            S.dma("sp", lambda e, n=n: e.dma_start(out=icnt, in_=src_ic[n:n + 1, :].partition_broadcast(128)),
                  writes=["icnt"])
            S.op("pool", lambda e, UB=UB: e.memset(UB[:, 0:Wd], 0.0), [], KU)
            for g in range(G):
                bk = gen_bank()
                proj_fm(slot, n, g, bk)
                for k, (s0, Ls) in enumerate(seqs):
                    a = max(s0, g * 512)
                    b = min(s0 + Ls, (g + 1) * 512)
                    if a >= b:
                        continue
                    dst0 = k * (Ls + 16) + 8 + (a - s0)
                    cp("act", UB[:, dst0:dst0 + (b - a)], PS[:, bk, a - g * 512:b - g * 512], [("ps", bk)], KU)
            tt("dve", WA[:, 1:Wd], UB[:, 0:Wd - 1], UB[:, 1:Wd], ALU.add, KU, KA)
            cur, curk, oth, othk = WA, KA, WB, KB
            lo, hi = 1, Wd
            if win >= 4:
                tt("dve", oth[:, lo + 1:hi - 1], cur[:, lo:hi - 2], cur[:, lo + 2:hi], ALU.add, curk, othk)
                cur, curk, oth, othk = oth, othk, cur, curk
                lo, hi = lo + 1, hi - 1
            if win >= 8:
                tt("dve", oth[:, lo + 2:hi - 2], cur[:, lo:hi - 4], cur[:, lo + 4:hi], ALU.add, curk, othk)
                cur, curk, oth, othk = oth, othk, cur, curk
                lo, hi = lo + 2, hi - 2
            if win >= 16:
                tt("dve", oth[:, lo + 4:hi - 4], cur[:, lo:hi - 8], cur[:, lo + 8:hi], ALU.add, curk, othk)
                cur, curk, oth, othk = oth, othk, cur, curk
                lo, hi = lo + 4, hi - 4
            assert lo <= 8 and hi >= Wd - 8
            for k, (s0, Ls) in enumerate(seqs):
                base = k * (Ls + 16) + 8
                tt("dve", oth[:, base:base + Ls], cur[:, base:base + Ls], icnt, ALU.mult, curk + ["icnt"], othk)
                tt("dve", DB[:, s0:s0 + Ls], oth[:, base:base + Ls], UB[:, base:base + Ls], ALU.subtract,
                   othk + KU, KD)
            for g in range(G):
                bk = gen_bank()
                mm(PS[:, bk, :], poolw[:, j * 4 + n, :], DB[:, g * 512:(g + 1) * 512], True, True,
                   KD + ["poolw"], [("ps", bk)], signal=True)
                yv = ypool[:, n, g * 512:(g + 1) * 512]
                stt("dve", yv, PS[:, bk, :], psc[:, j, n:n + 1], yv, ALU.mult, ALU.mult,
                    [("ps", bk), "psc", ("yp", n, g)], [("yp", n, g)])

        dbg("vc" + tg, v_c, [64, nch, 512], [("v_c", c) for c in range(nch)], BF16)
        dbg("yp" + tg, ypool, [128, 4, L], allk("yp"), BF16)
        if stop <= 4:
            return
        if is_sample:
            ic_bf = icnt.bitcast(BF16)
            kT2 = [kTt, ic_bf[0:64, 0:1024].rearrange("p (a b) -> p a b", b=128)]
            AT2 = [ATm, ic_bf[0:64, 1024:1536].rearrange("p (a b) -> p a b", b=64)]
            alias_k = ["icnt"]
        else:
            kT2 = [kTt, aalloc([64, 8, 128], BF16)]
            AT2 = [ATm, aalloc([64, 8, 64], BF16)]
            alias_k = []
        for si, (s0, Ls) in enumerate(seqs):
            nst = Ls // CH
            cb = s0 // CH
            for bsel in range(2):
                S.op("dve", lambda e, bsel=bsel: e.memset(AT2[bsel], 0.0), [],
                     [("ATm", bsel, 0), ("ATm", bsel, 1)] + alias_k)
            if is_sample:
                S.dma("sp", lambda e: e.dma_start(out=St, in_=st_in[j]), writes=["St"])
            else:
                S.op("dve", lambda e: e.memset(St, 0.0), [], ["St"])

            def geo(i):
                cr = (cb + i, cb + nst - 1 - i)
                t0 = (cr[0] * CH, cr[1] * CH)
                gq = (t0[0] // 512, t0[1] // 512)
                return cr, t0, gq

            def scan_p(i):
                cr, t0, gq = geo(i)
                bsel = i % 2
                kTb, ATb = kT2[bsel], AT2[bsel]
                bT = 2 + bsel
                for r in range(2):
                    for n in range(4):
                        tr(psb(bT)[0:64, (4 * r + n) * 128:(4 * r + n + 1) * 128], kp[r][:, n, t0[r]:t0[r] + CH],
                           [("kp", r, n, gq[r])], [("ps", bT)], signal=(r == 1 and n == 3))
                cp("act", kTb.rearrange("p a b -> p (a b)"), psb(bT)[0:64, :], [("ps", bT)],
                   [("kTt", bsel)] + alias_k)
                bA = (0, 5)[bsel]
                for r in range(2):
                    for n in range(4):
                        mm(PS[0:64, bA, (4 * r + n) * 64:(4 * r + n + 1) * 64], kp[r][:, n, t0[r]:t0[r] + CH],
                           qp[r][:, n, t0[r]:t0[r] + CH], True, True,
                           [("kp", r, n, gq[r]), ("qp", r, n, gq[r])], [("ps", bA)], signal=(r == 1 and n == 3))
                for r in range(2):
                    mku = mkf[:, r, :].bitcast(mybir.dt.uint32)
                    S.op("dve", lambda e, r=r, mku=mku, bA=bA, ATb=ATb: e.copy_predicated(
                        ATb[:, 4 * r:4 * r + 4, :].rearrange("p a b -> p (a b)"), mku,
                        PS[0:64, bA, 256 * r:256 * r + 256]),
                        [("ps", bA), "mkf"], [("ATm", bsel, r)] + alias_k)

            def scan_q(i):
                cr, t0, gq = geo(i)
                bsel = i % 2
                kTb, ATb = kT2[bsel], AT2[bsel]
                first = i < nst // 2
                tabk = lambda r, kind: [("tab", r, n, gq[r], kind) for n in range(4)]
                if deferred:
                    run_deferred(1)
                for r in range(2):
                    for n in range(4):
                        mm(PS[:, 6 + r, n * 128:(n + 1) * 128], kTb[:, 4 * r + n, :],
                           v_c[:, cr[r], n * 128:(n + 1) * 128], True, True,
                           [("kTt", bsel), ("v_c", cr[r])], [("ps", 6 + r)], signal=(n == 3))
                tmpKV = SC[:, 0:2, 0:512].rearrange("p a (h v) -> p a h v", v=128)
                for r in range(2):
                    for n in range(4):
                        act(tmpKV[:, r, n, :], PS[:, 6 + r, n * 128:(n + 1) * 128], AF.Identity,
                            [("ps", 6 + r)] + tabk(r, 0) + tabk(r, 1), [("sc", r, n)],
                            scale=tabG[:, r, n, cr[r]:cr[r] + 1])
                for r in range(2):
                    tt("dve", Stb[:, 4 * r:4 * r + 4, :], St[:, 4 * r:4 * r + 4, :],
                       tabS[:, r, :, cr[r]:cr[r] + 1].to_broadcast([128, 4, 128]), ALU.mult,
                       ["St"] + tabk(r, 0) + tabk(r, 1), [("Stb", r)])
                for r in range(2):
                    tt("dve", St[:, 4 * r:4 * r + 4, :], St[:, 4 * r:4 * r + 4, :],
                       tabA[:, r, :, cr[r]:cr[r] + 1].to_broadcast([128, 4, 128]), ALU.mult,
                       ["St"] + tabk(r, 2), ["St"])
                St4 = St.rearrange("p (a h) v -> p a h v", a=2)
                tt("dve", St4, St4, tmpKV, ALU.add,
                   ["St"] + [("sc", r, n) for r in range(2) for n in range(4)], ["St", ("sc", 0), ("sc", 1)])
                bO = 4
                for r in range(2):
                    for n in range(4):
                        o_ap = PS[:, bO, (4 * r + n) * 64:(4 * r + n + 1) * 64]
                        mm(o_ap, v_c[:, cr[r], n * 128:(n + 1) * 128], ATb[:, 4 * r + n, :], True, False,
                           [("v_c", cr[r]), ("ATm", bsel, r)], [("ps", bO)], signal=False)
                        mm(o_ap, Stb[:, 4 * r + n, :], qp[r][:, n, t0[r]:t0[r] + CH], False, True,
                           [("Stb", r), ("qp", r, n, gq[r])], [("ps", bO)], signal=(r == 1 and n == 3))
                for r in range(2):
                    dst = qo[:, :, t0[r]:t0[r] + CH]
                    src = PS[:, bO, 256 * r:256 * r + 256].rearrange("p (a b) -> p a b", b=64)
                    ok = [("qo", n, gq[r]) for n in range(4)]
                    E2 = float(np.exp(2.0 * CSH))
                    if first:
                        act(dst, src, AF.Identity, [("ps", bO)], ok, scale=E2)
                    else:
                        stt("dve", dst, src, E2, dst, ALU.mult, ALU.add, [("ps", bO)] + ok, ok)

            S.op("dve", lambda e: e.memset(SC[:, 0:2, 0:8], 0.0), [], [("sc", 0), ("sc", 1)] +
                 [("sc", r, n) for r in range(2) for n in range(4)])
            scan_p(0)
            for i in range(nst):
                if i + 1 < nst:
                    scan_p(i + 1)
                scan_q(i)
            if not is_sample:
                dst = st_out[si, j].rearrange("r h d v -> d (r h) v")
                S.dma("sp", lambda e, dst=dst: e.dma_start(out=dst, in_=St), reads=["St"])

        dbg("o" + tg, qo, [128, 4, L], allk("qo"), BF16)
        if stop <= 5:
            return
        ctr = 0
        for n in range(4):
            for g in range(G):
                o4 = 4 * (ctr % 2)
                ctr += 1
                T1, T2 = sc(o4), sc(o4 + 1)
                K1, K2 = ("sc", o4), ("sc", o4 + 1)
                ov = qo[:, n, g * 512:(g + 1) * 512]
                sqb = scb(o4 + 2, 512)
                act(sqb, ov, AF.Square, [("qo", n, g)], [("sc", o4 + 2)])
                bk = gen_bank()
                mm(PS[:, bk, :], onesb[:], sqb, True, True, [("sc", o4 + 2), "onesb"], [("ps", bk)], signal=True)
                act(T1, PS[:, bk, :], AF.Sqrt, [("ps", bk)], [K1], scale=1.0 / 128, bias=EPS)
                S.op("dve", lambda e, T1=T1: e.reciprocal(T1, T1), [K1], [K1])
                stt("dve", qp[0][:, n, g * 512:(g + 1) * 512], ov, hn[:, j:j + 1], T1, ALU.mult, ALU.mult,
                    [("qo", n, g), "hn", K1, ("qp", 0, n, g)], [("qp", 0, n, g)])

        slot = w_get(("rin", j), 3072)
        ctr = 0
        for n in range(4):
            for g in range(G):
                bk = gen_bank()
                proj_fm(slot, n, g, bk)
                o4 = 4 * (ctr % 2) + 3
                ctr += 1
                sgt = scb(o4, 512)
                act(sgt, PS[:, bk, :], AF.Silu, [("ps", bk)], [("sc", o4)])
                zv = qp[0][:, n, g * 512:(g + 1) * 512]
                tt("dve", zv, zv, sgt, ALU.mult, [("qp", 0, n, g), ("sc", o4)], [("qp", 0, n, g)])
        dbg("z" + tg, qp[0], [128, 4, L], allk("qp", 0), BF16)
        dbg("zin%d_%d" % (l, int(is_sample)), ypool, [128, 4, L], [("yp", n, g) for n in range(4) for g in range(G)])

        def zin_fn(kc, it):
            buf = ypool if kc < 4 else qp[0]
            return buf[:, kc % 4, it * 128:(it + 1) * 128]

        def zin_keys(kc, it):
            g = (it * 128) // 512
            return [("yp", kc, g)] if kc < 4 else [("qp", 0, kc - 4, g)]

        out_proj_and_residual(l, tile0, ntiles, v, zin_fn, zin_keys, ("rout", j))
        S.barrier()

    def att_pass(l, j, tile0, ntiles, v, seqs, is_sample):
        L = ntiles * 128
        G = L // 512
        nkt_cache = 4 if is_sample else 0
        arena_off[0] = 0
        QT = aalloc([128, 8, L], BF16)
        KT = aalloc([128, 2, 512 + L], BF16)
        Vt = aalloc([128, nkt_cache + ntiles, 256], BF16)
        gT = aalloc([128, 8, L], BF16)
        pT = aalloc([128, 3, 512], BF16)
        gq = aalloc([128, 128], F32)
        gk = aalloc([128, 128], F32)
        stg = aalloc([128, 2, 512], F32)
        cosT = aalloc([128, 8, 128], F32)
        sinT = aalloc([128, 8, 128], F32)
        ckt = aalloc([128, 4, 256], BF16)
        if is_sample:
            CGq = aalloc([128, 8, 128], F32)
            SGq = aalloc([128, 8, 128], F32)
            CGk = aalloc([128, 8, 128], F32)
            SGk = aalloc([128, 8, 128], F32)

        S.dma("sp", lambda e: e.dma_start(out=gq, in_=qn_row[j:j + 1, :].partition_broadcast(128)), writes=["gq"])
        S.dma("sp", lambda e: e.dma_start(out=gk, in_=kn_row[j:j + 1, :].partition_broadcast(128)), writes=["gk"])
        if is_sample:
            S.dma("sp", lambda e: e.dma_start(out=cosT, in_=cos_in.rearrange("t p f -> p t f")), writes=["cosT"])
            S.dma("sp", lambda e: e.dma_start(out=sinT, in_=sin_in.rearrange("t p f -> p t f")), writes=["sinT"])
            S.dma("pool", lambda e: e.dma_start(out=ckt, in_=ck_in[j].rearrange("(t p) f -> p t f", p=128)),
                  writes=["ckt"])
            S.dma("pool", lambda e: e.dma_start(out=Vt[:, 0:4, :], in_=cv_in[j].rearrange("(t p) f -> p t f", p=128)),
                  writes=[("Vt", t) for t in range(4)])
        prenorm(l, tile0, G, v)
        if is_sample:
            for t in range(4):
                bk = tr_bank()
                for h in range(2):
                    tr(psb(bk)[:, h * 128:(h + 1) * 128], ckt[:, t, h * 128:(h + 1) * 128], ["ckt"], [("ps", bk)],
                       signal=(h == 1))
                cp("act", KT[:, :, t * 128:(t + 1) * 128], psb(bk)[:, 0:256].rearrange("p (h t) -> p h t", t=128),
                   [("ps", bk)], [("KT", t)])

        def normrope(src_ps, pskeys, nh, gain, it, dst_bf, dkeys, ctr):
            o4 = 4 * (ctr % 2)
            A, B, C = sc(o4, nh * 128), sc(o4 + 1, nh * 128), sc(o4 + 2, nh * 128)
            KA, KB, KC = ("sc", o4), ("sc", o4 + 1), ("sc", o4 + 2)
            ssq = small[:, 48 + 4 * (ctr % 2):48 + 4 * (ctr % 2) + nh]
            sk = ("ssq3", ctr % 2)
            v3 = lambda ap: ap.rearrange("p (h d) -> p h d", d=128)
            act(A, src_ps, AF.Square, pskeys, [KA])
            S.op("dve", lambda e: e.tensor_reduce(out=ssq, in_=v3(A), axis=AX.X, op=ALU.add), [KA], [sk])
            act(ssq, ssq, AF.Sqrt, [sk], [sk], scale=1.0 / 128, bias=EPS)
            S.op("dve", lambda e: e.reciprocal(ssq, ssq), [sk], [sk])
            tt("dve", v3(B), v3(src_ps), ssq.unsqueeze(2).to_broadcast([128, nh, 128]), ALU.mult,
               pskeys + [sk], [KB])
            gainb = gain.unsqueeze(1).to_broadcast([128, nh, 128])
            if not is_sample:
                tt("dve", v3(dst_bf), v3(B), gainb, ALU.mult, [KB, "gq", "gk"], dkeys)
                return B
            CG, SG = (CGq, SGq) if nh == 4 else (CGk, SGk)
            c2 = CG[:, it, :].unsqueeze(1).to_broadcast([128, nh, 128])
            tt("dve", v3(A), v3(B), c2, ALU.mult, [KB, "ropetab"], [KA])
            B4 = B.rearrange("p (h i two) -> p h i two", i=64, two=2)
            C4 = C.rearrange("p (h i two) -> p h i two", i=64, two=2)
            s4 = SG[:, it, :].rearrange("p (i two) -> p i two", two=2)
            for e_ in range(2):
                tt("pool" if e_ == 0 else "dve", C4[:, :, :, e_], B4[:, :, :, 1 - e_],
                   s4[:, :, e_].unsqueeze(1).to_broadcast([128, nh, 64]), ALU.mult, [KB, "ropetab", KC], [(KC[0], KC[1], e_)])
            tt("pool", dst_bf, A, C, ALU.add, [KA, (KC[0], KC[1], 0), (KC[0], KC[1], 1)], dkeys + [KC])
            return B

        if is_sample:
            for (CG, SG, gain) in ((CGq, SGq, gq), (CGk, SGk, gk)):
                gb8 = gain.unsqueeze(1).to_broadcast([128, 8, 128])
                tt("dve", CG, cosT, gb8, ALU.mult, ["cosT", "gq", "gk"], ["ropetab"])
                g2 = gain.rearrange("p (i two) -> p i two", two=2)
                S4 = SG.rearrange("p t (i two) -> p t i two", two=2)
                s8 = sinT.rearrange("p t (i two) -> p t i two", two=2)
                for e_ in range(2):
                    tt("dve", S4[:, :, :, e_], s8[:, :, :, e_],
                       g2[:, :, 1 - e_].unsqueeze(1).to_broadcast([128, 8, 64]), ALU.mult,
                       ["sinT", "gq", "gk"], ["ropetab"])
        queue_pass_mod(l, v)
        if stop == 11:
            return
        items = []
        for qb in range(2):
            for it in range(ntiles):
                items.append(("q", qb, it))
        for it in range(ntiles):
            items.append(("kv", 0, it))
        slots = {}
        pend = []

        def stage_a(ctr, item):
            kind, qb, it = item
            key = (kind, qb)
            if key not in slots:
                slots[key] = w_get(("ain", j), qb * 512 if kind == "q" else 1024)
            slot = slots[key]
            bk = gen_bank()
            proj_tm(slot, it * 128, 128, bk)
            pk = [("ps", bk)]
            dk = [("sc", 4 * (ctr % 2) + 3)]
            if kind == "q":
                qr = scb(4 * (ctr % 2) + 3, 512)
                normrope(PS[:, bk, :], pk, 4, gq, it, qr, dk, ctr)
                return (kind, qb, it, qr, dk, bk, None)
            kr = scb(4 * (ctr % 2) + 3, 256)
            Bn = normrope(PS[:, bk, 0:256], pk, 2, gk, it, kr, dk, ctr)
            kt_idx = nkt_cache + it
            cp("dve", Vt[:, kt_idx, :], PS[:, bk, 256:512], pk, [("Vt", kt_idx)])
            if not is_sample:
                sq = it % 2
                tt("dve", stg[:, sq, 0:256].rearrange("p (h d) -> p h d", d=128),
                   Bn.rearrange("p (h d) -> p h d", d=128), gk.unsqueeze(1).to_broadcast([128, 2, 128]), ALU.mult,
                   [("sc", 4 * (ctr % 2) + 1), "gk"], [("stg", sq, 0)])
                cp("dve", stg[:, sq, 256:512], PS[:, bk, 256:512], pk, [("stg", sq, 1)])
                si, tt0 = divmod(it * 128, 256)
                S.dma("sp", lambda e, si=si, tt0=tt0, sq=sq: e.dma_start(out=ck_out[si, j, tt0:tt0 + 128, :],
                                                                         in_=stg[:, sq, 0:256]), reads=[("stg", sq, 0)])
                S.dma("sp", lambda e, si=si, tt0=tt0, sq=sq: e.dma_start(out=cv_out[si, j, tt0:tt0 + 128, :],
                                                                         in_=stg[:, sq, 256:512]), reads=[("stg", sq, 1)])
            return (kind, qb, it, kr, dk, bk, kt_idx)

        def stage_b(st):
            kind, qb, it, rr, dk, bk, kt_idx = st
            bt = tr_bank()
            if kind == "q":
                for h in range(4):
                    tr(psb(bt)[:, h * 128:(h + 1) * 128], rr[:, h * 128:(h + 1) * 128], dk, [("ps", bt)],
                       signal=(h == 3))
                cp("act", QT[:, qb * 4:qb * 4 + 4, it * 128:(it + 1) * 128],
                   psb(bt)[:, 0:512].rearrange("p (h t) -> p h t", t=128), [("ps", bt)], [("QT", qb, it)])
            else:
                for h in range(2):
                    tr(psb(bt)[:, h * 128:(h + 1) * 128], rr[:, h * 128:(h + 1) * 128], dk, [("ps", bt)],
                       signal=(h == 1))
                cp("act", KT[:, :, kt_idx * 128:(kt_idx + 1) * 128],
                   psb(bt)[:, 0:256].rearrange("p (h t) -> p h t", t=128), [("ps", bt)], [("KT", kt_idx)])

        stop_items = len(items)
        if stop == 12:
            stop_items = 2 * ntiles
        for ctr, item in enumerate(items[:stop_items]):
            st = stage_a(ctr, item)
            if pend:
                stage_b(pend.pop(0))
            pend.append(st)
        while pend:
            stage_b(pend.pop(0))
        if stop in (12, 13):
            return
        for gb in range(2):
            slot = w_get(("ain", j), 1536 + gb * 512)
            for n in range(4):
                for g in range(G):
                    bk = gen_bank()
                    proj_fm(slot, n, g, bk)
                    act(gT[:, gb * 4 + n, g * 512:(g + 1) * 512], PS[:, bk, :], AF.Silu, [("ps", bk)],
                        [("gT", gb * 4 + n, g)])
        if stop == 14:
            return
        scale = 128.0 ** -0.5
        units = []
        itc = 0
        for (s0, Ls) in seqs:
            kt_lo = 0 if is_sample else s0 // 128
            nkt = (nkt_cache + ntiles) if is_sample else Ls // 128
            for h in range(8):
                for q0 in range(s0, s0 + Ls, 512):
                    nq = min(512, s0 + Ls - q0)
                    for ki in range(nkt):
                        units.append(dict(h=h, q0=q0, nq=nq, ki=ki, nkt=nkt, kt=kt_lo + ki, itc=itc))
                    itc += 1
        LA = 2

        def emit_S(idx, u):
            h, q0, nq, kt = u["h"], u["q0"], u["nq"], u["kt"]
            bS = idx % 3
            mm(PS[:, bS, 0:nq], KT[:, h // 4, kt * 128:(kt + 1) * 128], QT[:, h, q0:q0 + nq], True, True,
               [("KT", kt)] + [("QT", h // 4, t) for t in range(q0 // 128, (q0 + nq) // 128)],
               [("ps", bS)], signal=True)

        def emit_rest(idx, u):
            h, q0, nq, kt, ki, nkt, ic = u["h"], u["q0"], u["nq"], u["kt"], u["ki"], u["nkt"], u["itc"]
            kvh = h // 4
            bS = idx % 3
            pb = idx % 3
            bO = 4 + (ic % 2)
            bD = 6 + (ic % 2)
            act(pT[:, pb, 0:nq], PS[:, bS, 0:nq], AF.Exp, [("ps", bS)], [("pT", pb)], scale=scale)
            mm(PS[:, bO, 0:nq], Vt[:, kt, kvh * 128:(kvh + 1) * 128], pT[:, pb, 0:nq], ki == 0,
               ki == nkt - 1, [("Vt", kt), ("pT", pb)], [("ps", bO)], signal=False)
            mm(PS[:, bD, 0:nq], onesb[:], pT[:, pb, 0:nq], ki == 0, ki == nkt - 1,
               [("pT", pb), "onesb"], [("ps", bD)], signal=True)
            if ki == nkt - 1:
                gg = q0 // 512
                o4 = 2 * (ic % 2)
                R1, R2 = sc(o4, nq), sc(o4 + 1, nq)
                S.op("dve", lambda e, R1=R1, bD=bD, nq=nq: e.reciprocal(R1, PS[:, bD, 0:nq]),
                     [("ps", bD)], [("sc", o4)])
                tt("dve", R2, PS[:, bO, 0:nq], R1, ALU.mult, [("ps", bO), ("sc", o4)], [("sc", o4 + 1)])
                gv = gT[:, h, q0:q0 + nq]
                tt("pool", gv, gv, R2, ALU.mult, [("gT", h, gg), ("sc", o4 + 1)], [("gT", h, gg)])

        dstep = max(1, len(units) // 10)
        for idx in range(len(units) + LA):
            if deferred and idx % dstep == dstep - 1:
                run_deferred(3)
            if idx < len(units):
                emit_S(idx, units[idx])
            if idx >= LA:
                emit_rest(idx - LA, units[idx - LA])

        if stop == 15:
            return

        def zin_fn(kc, it):
            return gT[:, kc, it * 128:(it + 1) * 128]

        def zin_keys(kc, it):
            return [("gT", kc, (it * 128) // 512)]

        out_proj_and_residual(l, tile0, ntiles, v, zin_fn, zin_keys, ("aout", j))
        S.barrier()

    for l in range(nlayers):
        j = l // 2
        if l == 0:
            queue_mod_ss(0)
            while deferred:
                run_deferred(6)
        if l % 2 == 0:
            rec_pass(l, j, 0, 4, 0, [(0, 256), (256, 256)], False)
            rec_pass(l, j, 4, 8, 1, [(0, 1024)], True)
        else:
            att_pass(l, j, 0, 4, 0, [(0, 256), (256, 256)], False)
            att_pass(l, j, 4, 8, 1, [(0, 1024)], True)

    for t in range(12):
        S.dma("sp", lambda e, t=t: e.dma_start(out=y_out[t], in_=X[:, t, :]), reads=[("X", t)])

    if wlist_in is not None:
        S.emit_all()
    es.close()
    return nc, dbg_out, wcollect


def _consts():
    ident = np.eye(128, dtype=np.float32)
    ones = np.ones((128, 128), np.float32)
    s = np.arange(64)[:, None]
    t = np.arange(64)[None, :]
    mk = np.stack([(s <= t), (s >= t)], axis=1).astype(np.float32)
    mk = np.ascontiguousarray(np.tile(mk, (1, 1, 4)))
    rmask = np.ones((128, 512), np.float32)
    rmask[:, ::CH] = 0.0

    def icnt(L):
        tt_ = np.arange(L)
        out = []
        for win in (2, 4, 8, 16):
            lo = np.clip(tt_ - win // 2, 0, L)
            hi = np.clip(tt_ + win // 2, 0, L)
            out.append(1.0 / (hi - lo).astype(np.float32))
        return np.stack(out).astype(np.float32)

    tpos = np.arange(1024)
    row = (tpos // 64).astype(np.float32)
    col = (tpos % 64).astype(np.float32)
    inv = (10000.0 ** (-np.arange(0, 64, 2, dtype=np.float32) / 64)).astype(np.float32)
    ang = np.concatenate([row[:, None] * inv[None, :], col[:, None] * inv[None, :]], axis=-1).astype(np.float32)
    cos_t = np.repeat(np.cos(ang).astype(np.float32), 2, axis=-1).reshape(8, 128, 128)
    sn = np.sin(ang).astype(np.float32)
    sin_t = np.stack([-sn, sn], axis=-1).reshape(8, 128, 128)
    return dict(ident=ident, ones=ones, mk=mk, rmask=rmask, icnt_s=icnt(1024), icnt_p=icnt(256),
                cos_t=cos_t, sin_t=sin_t)


def _prep_inputs(x_prompt, x_sample, c, state_hgrn, cache_k, cache_v, c_ctx, ada_w, ada_b,
                 norm_pre, norm_post, rec_w_in, rec_lb_logits, rec_head_norm, pool_w, pool_scale,
                 rec_w_out, att_w_in, att_q_norm, att_k_norm, att_w_out):
    f = lambda a: np.ascontiguousarray(np.asarray(a, dtype=np.float32))
    shared = dict(
        ada_w=f(ada_w),
        ada_bT=f(np.asarray(ada_b).reshape(4, 24, 128).transpose(2, 0, 1)),
        ada_bg=f(np.asarray(ada_b)[:, 2048:3072]),
        npreT=f(np.asarray(norm_pre).reshape(4, 8, 128).transpose(2, 0, 1)),
        npost=f(norm_post),
        rec_w_in=f(rec_w_in), rec_w_out=f(rec_w_out), att_w_in=f(att_w_in), att_w_out=f(att_w_out),
        lbT=f(np.asarray(rec_lb_logits).reshape(2, 2, 4, 128).transpose(3, 0, 1, 2)),
        hnT=f(np.asarray(rec_head_norm).T),
        pool_w=f(pool_w),
        pscT=f(np.asarray(pool_scale).reshape(2, 4, 128).transpose(2, 0, 1)),
        qn_row=f(att_q_norm), kn_row=f(att_k_norm),
    )
    shared.update(_consts())
    maps = []
    for i in range(NCORES):
        xin = np.concatenate([np.asarray(x_prompt[2 * i]).reshape(2, 128, 1024),
                              np.asarray(x_prompt[2 * i + 1]).reshape(2, 128, 1024),
                              np.asarray(x_sample[i]).reshape(8, 128, 1024)], axis=0)
        cvec = np.stack([np.asarray(c_ctx), np.asarray(c[i])], axis=0)
        cT = cvec.reshape(2, 8, 128).transpose(2, 1, 0)
        st = np.asarray(state_hgrn[i]).transpose(0, 3, 1, 2, 4).reshape(2, 128, 8, 128)
        m = dict(shared)
        m.update(x_in=f(xin), cT=f(cT), st_in=f(st),
                 ck_in=f(np.asarray(cache_k[i]).reshape(2, 512, 256)),
                 cv_in=f(np.asarray(cache_v[i]).reshape(2, 512, 256)))
        maps.append(m)
    return maps


_CACHE = {}


def kernel(**inputs):
    maps = _prep_inputs(**inputs)
    if "nc" not in _CACHE:
        _CACHE["nc"] = build_program()[0]
    nc = _CACHE["nc"]
    res = run_bass_kernel_spmd(nc, maps, core_ids=list(range(NCORES)))
    R = res.results
    y_prompt = np.zeros((16, 256, 1024), np.float32)
    y_sample = np.zeros((8, 1024, 1024), np.float32)
    new_state = np.zeros((16, 2, 2, 4, 128, 128), np.float32)
    new_k = np.zeros((16, 2, 256, 2, 128), np.float32)
    new_v = np.zeros((16, 2, 256, 2, 128), np.float32)
    for i in range(NCORES):
        y = np.asarray(R[i]["y_out"])
        y_prompt[2 * i] = y[0:2].reshape(256, 1024)
        y_prompt[2 * i + 1] = y[2:4].reshape(256, 1024)
        y_sample[i] = y[4:12].reshape(1024, 1024)
        new_state[2 * i:2 * i + 2] = np.asarray(R[i]["st_out"])
        new_k[2 * i:2 * i + 2] = np.asarray(R[i]["ck_out"]).reshape(2, 2, 256, 2, 128)
        new_v[2 * i:2 * i + 2] = np.asarray(R[i]["cv_out"]).reshape(2, 2, 256, 2, 128)
    return (y_prompt, y_sample, new_state, new_k, new_v)
```

```python
import numpy as np
from contextlib import ExitStack
import concourse.bass as bass
import concourse.mybir as mybir
from concourse.bass_utils import run_bass_kernel_spmd

F32 = mybir.dt.float32
BF16 = mybir.dt.bfloat16
AF = mybir.ActivationFunctionType
ALU = mybir.AluOpType
AX = mybir.AxisListType

ENGS = ["pe", "act", "dve", "pool", "sp"]
EPS = 1e-6
F_MIN = 1e-6
NCORES = 8
CH = 64
CSH = 20.0
NRING = 3


class Sched:
    def __init__(self, nc, n_dma_sems=16):
        self.nc = nc
        self.ops = {e: [] for e in ENGS}
        self.last_w = {}
        self.readers = {}
        self.n_dma_sems = n_dma_sems
        self.dma_slot_val = {}
        self.dma_rr = {"sp": 0, "pool": 0}
        self.all_dma_tokens = []
        self.pending = {e: set() for e in ENGS}
        self.trace = None

    def _deps(self, reads, writes):
        deps = set()
        for k in reads:
            t = self.last_w.get(k)
            if t is not None:
                deps.add(t)
        for k in writes:
            t = self.last_w.get(k)
            if t is not None:
                deps.add(t)
            for r in self.readers.get(k, ()):
                deps.add(r)
        return deps

    def _commit(self, tok, reads, writes):
        for k in writes:
            self.last_w[k] = tok
            self.readers[k] = []
        for k in reads:
            self.readers.setdefault(k, []).append(tok)

    def op(self, eng, emit, reads=(), writes=(), signal=True):
        deps = self._deps(reads, writes)
        deps |= self.pending[eng]
        self.pending[eng] = set()
        idx = len(self.ops[eng])
        tok = ("e", eng, idx)
        if eng == "pe":
            deps = {d for d in deps if not (d[0] == "e" and d[1] == "pe")}
        self.ops[eng].append(dict(emit=emit, deps=deps, signal=signal, tok=tok, dma=None))
        self._commit(tok, reads, writes)
        return tok

    def dma(self, q, emit, reads=(), writes=()):
        deps = self._deps(reads, writes)
        deps |= self.pending[q]
        self.pending[q] = set()
        slot = self.dma_rr[q]
        self.dma_rr[q] = (slot + 1) % self.n_dma_sems
        prev = self.dma_slot_val.get((q, slot), 0)
        if prev > 0:
            deps.add(("d", (q, slot), prev))
        val = prev + 16
        self.dma_slot_val[(q, slot)] = val
        tok = ("d", (q, slot), val)
        self.ops[q].append(dict(emit=emit, deps=deps, signal=False, tok=tok, dma=(q, slot)))
        self._commit(tok, reads, writes)
        self.all_dma_tokens.append(tok)
        return tok

    def barrier(self):
        toks = set()
        for e in ENGS:
            for i in range(len(self.ops[e]) - 1, -1, -1):
                if self.ops[e][i]["dma"] is None:
                    toks.add(self.ops[e][i]["tok"])
                    break
        for k, v in self.dma_slot_val.items():
            toks.add(("d", k, v))
        for e in ENGS:
            self.pending[e] |= toks

    def emit_all(self):
        nc = self.nc
        with ExitStack() as es:
            esem = {e: es.enter_context(nc.semaphore("sem_" + e)) for e in ENGS}
            dsem = {}
            for k in self.dma_slot_val:
                dsem[k] = es.enter_context(nc.semaphore("dsem_%s_%d" % k))
            counts = {}
            for e in ENGS:
                c = 0
                arr = []
                for o in self.ops[e]:
                    if o["signal"] and o["dma"] is None:
                        c += 1
                    arr.append(c)
                res = [None] * len(arr)
                nxt = None
                for i in range(len(arr) - 1, -1, -1):
                    o = self.ops[e][i]
                    if o["signal"] and o["dma"] is None:
                        nxt = arr[i]
                    res[i] = nxt
                counts[e] = res

            def resolve(tok):
                if tok[0] == "e":
                    v = counts[tok[1]][tok[2]]
                    assert v is not None, ("dep on op with no later signal", tok)
                    return ("e", tok[1]), esem[tok[1]], v
                return tok[1], dsem[tok[1]], tok[2]

            block = es.enter_context(nc.Block())

            def run(ename, eobj):
                seen = {}
                for o in self.ops[ename]:
                    waits = {}
                    for d in o["deps"]:
                        key, sem, v = resolve(d)
                        if v > waits.get(key, (None, 0))[1]:
                            waits[key] = (sem, v)
                    wl = []
                    for key, (sem, v) in waits.items():
                        if seen.get(key, 0) >= v:
                            continue
                        eobj.wait_ge(sem, v)
                        seen[key] = v
                        wl.append((key, v))
                    if self.trace is not None:
                        self.trace.append((ename, o["tok"], wl, o["signal"], o.get("tag")))
                    ins = o["emit"](eobj)
                    if o["dma"] is not None:
                        ins.then_inc(dsem[o["dma"]], 16)
                    elif o["signal"]:
                        ins.then_inc(esem[ename], 1)
                if ename == "sp":
                    for k, v in self.dma_slot_val.items():
                        eobj.wait_ge(dsem[k], v)

            @block.sync
            def _(e):
                run("sp", e)

            @block.tensor
            def _(e):
                run("pe", e)

            @block.scalar
            def _(e):
                run("act", e)

            @block.vector
            def _(e):
                run("dve", e)

            @block.gpsimd
            def _(e):
                run("pool", e)


def build_program(nlayers=4, dbg_names=(), stop=99):
    wl = _build(nlayers, (), stop, None)[2]
    nc, dbg_out, _ = _build(nlayers, dbg_names, stop, wl)
    return nc, dbg_out


def _build(nlayers, dbg_names, stop, wlist_in):
    nc = bass.Bass("TRN2", target_bir_lowering=False)
    es = ExitStack()

    def din(name, shape):
        return nc.dram_tensor(name, list(shape), F32, kind="ExternalInput").ap()

    def dout(name, shape):
        return nc.dram_tensor(name, list(shape), F32, kind="ExternalOutput").ap()

    x_in = din("x_in", [12, 128, 1024])
    cT_in = din("cT", [128, 8, 2])
    st_in = din("st_in", [2, 128, 8, 128])
    ck_in = din("ck_in", [2, 512, 256])
    cv_in = din("cv_in", [2, 512, 256])
    ada_w = din("ada_w", [4, 1024, 3072])
    ada_bT = din("ada_bT", [128, 4, 24])
    ada_bg = din("ada_bg", [4, 1024])
    npreT = din("npreT", [128, 4, 8])
    npost = din("npost", [4, 1024])
    rec_w_in = din("rec_w_in", [2, 1024, 3584])
    rec_w_out = din("rec_w_out", [2, 1024, 1024])
    att_w_in = din("att_w_in", [2, 1024, 2560])
    att_w_out = din("att_w_out", [2, 1024, 1024])
    lbT_in = din("lbT", [128, 2, 2, 4])
    hnT_in = din("hnT", [128, 2])
    pool_w = din("pool_w", [2, 4, 128, 128])
    pscT_in = din("pscT", [128, 2, 4])
    qn_row = din("qn_row", [2, 128])
    kn_row = din("kn_row", [2, 128])
    ident_in = din("ident", [128, 128])
    ones_in = din("ones", [128, 128])
    mk_in = din("mk", [64, 2, 256])
    rmask_in = din("rmask", [128, 512])
    icnt_s = din("icnt_s", [4, 1024])
    icnt_p = din("icnt_p", [4, 256])
    cos_in = din("cos_t", [8, 128, 128])
    sin_in = din("sin_t", [8, 128, 128])

    y_out = dout("y_out", [12, 128, 1024])
    st_out = dout("st_out", [2, 2, 2, 4, 128, 128])
    ck_out = dout("ck_out", [2, 2, 256, 256])
    cv_out = dout("cv_out", [2, 2, 256, 256])
    dbg_out = {}

    def sb(name, shape, dt, stack=None):
        return (stack or es).enter_context(nc.sbuf_tensor("sb_" + name, list(shape), dt))

    S = Sched(nc)

    X = sb("X", [128, 12, 1024], F32)
    ring = sb("ring", [128, NRING, 8, 512], BF16)
    hT = sb("hT", [128, 8, 1024], BF16)
    GT = sb("GT", [128, 2, 1024], F32)
    SC = sb("SC", [128, 8, 520], F32)
    idb = sb("idb", [128, 128], BF16)
    onesb = sb("onesb", [128, 128], BF16)
    onesf = sb("onesf", [128, 128], F32)
    rmask = sb("rmask", [128, 512], F32)
    mkf = sb("mkf", [64, 2, 256], F32)
    cT = sb("cTs", [128, 8, 2], F32)
    scT = sb("scT", [128, 8, 2], BF16)
    sc_rep = sb("sc_rep", [128, 8, 2, 128], BF16)
    adabT = sb("adabT", [128, 4, 24], F32)
    npre = sb("npre", [128, 4, 8], F32)
    lbl = sb("lbl", [128, 2, 2, 4], F32)
    lb = sb("lb", [128, 2, 2, 4], F32)
    oml = sb("oml", [128, 2, 2, 4], F32)
    hn = sb("hn", [128, 2], F32)
    psc = sb("psc", [128, 2, 4], F32)
    poolw = sb("poolw", [128, 8, 128], BF16)
    modT2 = sb("modT", [128, 2, 16, 2], F32)
    Gcol2 = sb("Gcol", [128, 2, 8, 2], F32)
    small = sb("small", [128, 64], F32)
    PS = es.enter_context(nc.psum_tensor("PS", [128, 8, 512], F32))
    ARENA_BYTES = 80 * 1024
    arena = sb("arena", [128, ARENA_BYTES // 4], F32)
    arena_off = [0]

    def aalloc(shape, dt):
        esz = 2 if dt == BF16 else 4
        n = int(np.prod(shape[1:])) * esz
        n = (n + 63) // 64 * 64
        off = arena_off[0]
        assert off + n <= ARENA_BYTES, ("arena overflow", off, n)
        arena_off[0] = off + n
        ap = arena[0:shape[0], off // 4:(off + n) // 4]
        if dt == BF16:
            ap = ap.bitcast(BF16)
        ap = ap[:, 0:int(np.prod(shape[1:]))]
        if len(shape) == 3:
            ap = ap.rearrange("p (a b) -> p a b", b=shape[2])
        elif len(shape) == 4:
            ap = ap.rearrange("p (a b c) -> p a b c", b=shape[2], c=shape[3])
        return ap

    def psb(b):
        return PS[:, b, :].bitcast(BF16)

    def sc(i, n=512):
        return SC[:, i, 0:n]

    def scb(i, n=1024):
        return SC[:, i, :].bitcast(BF16)[:, 0:n]

    def sc2(i):
        return SC[:, 2 * i:2 * i + 2, :].rearrange("p a b -> p (a b)")

    def act(out, in_, func, reads, writes, **kw):
        S.op("act", lambda e: e.activation(out=out, in_=in_, func=func, **kw), reads, writes)

    def tt(eng, out, a, b, op, reads, writes):
        S.op(eng, lambda e: e.tensor_tensor(out, a, b, op), reads, writes)

    def ts(eng, out, a, s1, s2, op0, op1, reads, writes):
        if s2 is None:
            S.op(eng, lambda e: e.tensor_scalar(out, a, s1, None, op0), reads, writes)
        else:
            S.op(eng, lambda e: e.tensor_scalar(out, a, s1, s2, op0, op1), reads, writes)

    def stt(eng, out, in0, scalar, in1, op0, op1, reads, writes):
        S.op(eng, lambda e: e.scalar_tensor_tensor(out, in0, scalar, in1, op0, op1), reads, writes)

    def cp(eng, out, in_, reads, writes):
        if eng == "act":
            S.op("act", lambda e: e.activation(out=out, in_=in_, func=AF.Copy), reads, writes)
        else:
            S.op(eng, lambda e: e.tensor_copy(out, in_), reads, writes)

    def mm(out, lhsT, rhs, start, stop, reads, writes, signal):
        S.op("pe", lambda e: e.matmul(out, lhsT, rhs, start=start, stop=stop), reads, writes, signal=signal)

    def tr(out, in_, reads, writes, signal):
        n = in_.shape[0]
        S.op("pe", lambda e: e.transpose(out, in_, idb[0:n, 0:n]), list(reads) + ["idb"], writes, signal=signal)

    def dbg(name, ap, shape, reads, dt=F32):
        if name not in dbg_names:
            return
        d = nc.dram_tensor("dbg_" + name, list(shape), dt, kind="ExternalOutput").ap()
        dbg_out[name] = d
        S.dma("sp", lambda e: e.dma_start(out=d, in_=ap), reads=reads)

    bank_rr = {"gen": 0, "tr": 0}

    def gen_bank():
        b = bank_rr["gen"]
        bank_rr["gen"] ^= 1
        return b

    def tr_bank():
        b = 2 + bank_rr["tr"]
        bank_rr["tr"] ^= 1
        return b

    wlist = list(wlist_in) if wlist_in is not None else []
    wcollect = []
    wstate = {"issued": 0, "next": 0}

    def w_issue_upto(k):
        while wstate["issued"] <= k and wstate["issued"] < len(wlist):
            i = wstate["issued"]
            wap, c0, ncol = wlist[i]
            slot = i % NRING
            src = wap[:, c0:c0 + ncol].rearrange("(c p) n -> p c n", p=128)
            S.dma("pool", lambda e, slot=slot, src=src, ncol=ncol: e.dma_start(out=ring[:, slot, :, 0:ncol], in_=src),
                  writes=[("ring", slot)])
            wstate["issued"] += 1

    def w_get(wkey, c0, ahead=NRING - 1):
        k = wstate["next"]
        wstate["next"] += 1
        wcollect.append((wkey, c0))
        if wlist_in is not None:
            assert wlist_in[k][3] == (wkey, c0), ("weight stream order mismatch", k, wlist_in[k][3], (wkey, c0))
        w_issue_upto(k + ahead)
        return k % NRING

    WSRC = {}
    for l_ in range(4):
        WSRC[("ada", l_)] = ada_w[l_]
    for j_ in range(2):
        WSRC[("rin", j_)] = rec_w_in[j_]
        WSRC[("rout", j_)] = rec_w_out[j_]
        WSRC[("ain", j_)] = att_w_in[j_]
        WSRC[("aout", j_)] = att_w_out[j_]
    if wlist_in is not None:
        wlist = [(WSRC[wk], c0, 512) for (wk, c0) in wlist_in]
        wlist_in = [(WSRC[wk], c0, 512, (wk, c0)) for (wk, c0) in wlist_in]

    for t in range(12):
        S.dma("sp", lambda e, t=t: e.dma_start(out=X[:, t, :], in_=x_in[t]), writes=[("X", t)])
    S.dma("pool", lambda e: e.dma_start(out=idb[:], in_=ident_in), writes=["idb"])
    S.dma("pool", lambda e: e.dma_start(out=onesb[:], in_=ones_in), writes=["onesb"])
    S.dma("sp", lambda e: e.dma_start(out=onesf[:], in_=ones_in), writes=["onesf"])
    S.dma("pool", lambda e: e.dma_start(out=poolw[:], in_=pool_w.rearrange("j g c d -> c (j g) d")), writes=["poolw"])
    S.dma("sp", lambda e: e.dma_start(out=rmask[:], in_=rmask_in), writes=["rmask"])
    S.dma("sp", lambda e: e.dma_start(out=mkf[:], in_=mk_in), writes=["mkf"])
    S.dma("sp", lambda e: e.dma_start(out=cT[:], in_=cT_in), writes=["cT"])
    S.dma("sp", lambda e: e.dma_start(out=adabT[:], in_=ada_bT), writes=["adabT"])
    S.dma("sp", lambda e: e.dma_start(out=npre[:], in_=npreT), writes=["npre"])
    S.dma("sp", lambda e: e.dma_start(out=lbl[:], in_=lbT_in), writes=["lbl"])
    S.dma("sp", lambda e: e.dma_start(out=hn[:], in_=hnT_in), writes=["hn"])
    S.dma("sp", lambda e: e.dma_start(out=psc[:], in_=pscT_in), writes=["psc"])
    w_issue_upto(NRING - 2)

    act(scT[:], cT[:], AF.Silu, ["cT"], ["scT"])
    cp("dve", sc_rep[:], scT[:].unsqueeze(3).to_broadcast([128, 8, 2, 128]), ["scT"], ["sc_rep"])
    S.op("dve", lambda e: e.memset(lb[:, 0, :, :], 0.0), [], ["lb0"])
    tt("dve", small[:, 0:8], lbl[:, 1, :, :].rearrange("p a b -> p (a b)"),
       lbl[:, 0, :, :].rearrange("p a b -> p (a b)"), ALU.subtract, ["lbl"], ["small"])
    act(lb[:, 1, :, :].rearrange("p a b -> p (a b)"), small[:, 0:8], AF.Sigmoid, ["small"], ["lb1"])
    ts("dve", oml[:].rearrange("p j a b -> p (j a b)"), lb[:].rearrange("p j a b -> p (j a b)"),
       -1.0, 1.0, ALU.mult, ALU.add, ["lb0", "lb1"], ["oml"])
    LBK = ["lb0", "lb1", "oml"]

    deferred = []

    def run_deferred(bk, n=1):
        for _ in range(n):
            if deferred:
                deferred.pop(0)(bk)

    def queue_mod_ss(l):
        modT = modT2[:, l % 2]
        Gcol = Gcol2[:, l % 2]

        def blk(b):
            def f(bk):
                slot = w_get(("ada", l), b * 512)
                for n in range(4):
                    for kc in range(8):
                        mm(PS[:, bk, n * 2:n * 2 + 2], ring[:, slot, kc, n * 128:(n + 1) * 128], scT[:, kc, :],
                           kc == 0, kc == 7, [("ring", slot), "scT"], [("ps", bk)], signal=(kc == 7 and n == 3))
                tt("dve", modT[:, 4 * b:4 * b + 4, :], PS[:, bk, 0:8].rearrange("p (c v) -> p c v", v=2),
                   adabT[:, l, 4 * b:4 * b + 4].unsqueeze(2).to_broadcast([128, 4, 2]), ALU.add,
                   [("ps", bk), "adabT"], [("modT", l % 2, b)])
                if b == 3:
                    stt("dve", Gcol, modT[:, 8:16, :], 1.0, npre[:, l, :].unsqueeze(2).to_broadcast([128, 8, 2]),
                        ALU.add, ALU.mult, [("modT", l % 2, 2), ("modT", l % 2, 3), "npre"], [("Gcol", l % 2)])
            return f
        for b in range(4):
            deferred.append(blk(b))

    def queue_mod_gate(l, v):
        def blk(b):
            def f(bk):
                if b == 0:
                    S.dma("sp", lambda e: e.dma_start(out=sc2(2)[:, 0:1024],
                                                      in_=ada_bg[l:l + 1, :].partition_broadcast(128)),
                          writes=[("sc", 4), ("sc", 5)])
                    S.dma("sp", lambda e: e.dma_start(out=sc2(3)[:, 0:1024],
                                                      in_=npost[l:l + 1, :].partition_broadcast(128)),
                          writes=[("sc", 6), ("sc", 7)])
                slot = w_get(("ada", l), (4 + b) * 512)
                for kc in range(8):
                    mm(PS[:, bk, :], sc_rep[:, kc, v, :], ring[:, slot, kc, :], kc == 0, kc == 7,
                       [("ring", slot), "sc_rep"], [("ps", bk)], signal=(kc == 7))
                tt("dve", GT[:, v, b * 512:(b + 1) * 512], PS[:, bk, :], sc2(2)[:, b * 512:(b + 1) * 512], ALU.add,
                   [("ps", bk), ("sc", 4), ("sc", 5)], [("GT", v, b)])
                tt("dve", GT[:, v, b * 512:(b + 1) * 512], GT[:, v, b * 512:(b + 1) * 512],
                   sc2(3)[:, b * 512:(b + 1) * 512], ALU.mult,
                   [("GT", v, b), ("sc", 6), ("sc", 7)], [("GT", v, b)])
            return f
        for b in range(2):
            deferred.append(blk(b))

    def queue_pass_mod(l, v):
        if v == 1 and l + 1 < nlayers:
            queue_mod_ss(l + 1)
        queue_mod_gate(l, v)

    def prenorm(l, tile0, ngroups, v):
        modT = modT2[:, l % 2]
        Gcol = Gcol2[:, l % 2]
        MK = [("Gcol", l % 2), ("modT", l % 2, 0), ("modT", l % 2, 1)]
        for g in range(ngroups):
            for jt in range(4):
                t = tile0 + g * 4 + jt
                ssq = small[:, 16 + jt:17 + jt]
                rs = small[:, 20 + jt:21 + jt]
                act(scb(4 + jt), X[:, t, :], AF.Square, [("X", t)], [("sc", 4 + jt), ("ssq", jt)], accum_out=ssq)
                act(rs, ssq, AF.Sqrt, [("ssq", jt)], [("rs", jt)], scale=1.0 / 1024, bias=EPS)
                S.op("dve", lambda e, rs=rs: e.reciprocal(rs, rs), [("rs", jt)], [("rs", jt)])
                ts("dve", scb(jt), X[:, t, :], rs, None, ALU.mult, None, [("X", t), ("rs", jt)], [("sc", jt)])
            for cpair in range(4):
                bk = tr_bank()
                for cc in range(2):
                    c = 2 * cpair + cc
                    for jt in range(4):
                        tr(psb(bk)[:, cc * 512 + jt * 128: cc * 512 + (jt + 1) * 128],
                           scb(jt)[:, c * 128:(c + 1) * 128], [("sc", jt)], [("ps", bk)],
                           signal=(cc == 1 and jt == 3))
                for cc in range(2):
                    c = 2 * cpair + cc
                    dst = hT[:, c, g * 512:(g + 1) * 512]
                    src = psb(bk)[:, cc * 512:(cc + 1) * 512]
                    if cpair % 2 == 0:
                        act(dst, src, AF.Identity, [("ps", bk)] + MK, [("hT", g, c)],
                            scale=Gcol[:, c, v:v + 1], bias=modT[:, c, v:v + 1])
                    else:
                        ts("dve", dst, src, Gcol[:, c, v:v + 1], modT[:, c, v:v + 1], ALU.mult, ALU.add,
                           [("ps", bk)] + MK, [("hT", g, c)])

    def hT_keys(g):
        return [("hT", g, c) for c in range(8)]

    def proj_fm(slot, n, g, bk):
        for kc in range(8):
            mm(PS[:, bk, :], ring[:, slot, kc, n * 128:(n + 1) * 128], hT[:, kc, g * 512:(g + 1) * 512],
               kc == 0, kc == 7, [("ring", slot)] + hT_keys(g), [("ps", bk)], signal=(kc == 7))

    def proj_tm(slot, tok0, ntok, bk):
        g = tok0 // 512
        for kc in range(8):
            mm(PS[0:ntok, bk, :], hT[:, kc, tok0:tok0 + ntok], ring[:, slot, kc, :],
               kc == 0, kc == 7, [("ring", slot)] + hT_keys(g), [("ps", bk)], signal=(kc == 7))

    def out_proj_and_residual(l, tile0, ntiles, v, zin_fn, zin_keys_fn, wkey):
        while deferred:
            run_deferred(gen_bank())
        s0 = w_get(wkey, 0)
        s1 = w_get(wkey, 512, NRING - 2)
        slots = (s0, s1)
        for it in range(ntiles):
            t = tile0 + it
            b0 = 4 + 2 * (it % 2)
            for nb in range(2):
                for kc in range(8):
                    mm(PS[:, b0 + nb, :], zin_fn(kc, it), ring[:, slots[nb], kc, :], kc == 0, kc == 7,
                       [("ring", slots[nb])] + zin_keys_fn(kc, it), [("ps", b0 + nb)], signal=(kc == 7))
            po = PS[:, b0:b0 + 2, :].rearrange("p a b -> p (a b)")
            pk = [("ps", b0), ("ps", b0 + 1)]
            ssq = small[:, 24 + (it % 2):25 + (it % 2)]
            rs = small[:, 26 + (it % 2):27 + (it % 2)]
            jk = 6 + (it % 2)
            act(scb(jk), po, AF.Square, pk, [("sc", jk), ("ssq2", it % 2)], accum_out=ssq)
            act(rs, ssq, AF.Sqrt, [("ssq2", it % 2)], [("rs2", it % 2)], scale=1.0 / 1024, bias=EPS)
            S.op("dve", lambda e, rs=rs: e.reciprocal(rs, rs), [("rs2", it % 2)], [("rs2", it % 2)])
            tmpk = 2 * (it % 2)
            tmp = sc2(it % 2)[:, 0:1024]
            stt("dve", tmp, po, rs, GT[:, v, :], ALU.mult, ALU.mult,
                pk + [("rs2", it % 2), ("GT", v, 0), ("GT", v, 1)], [("sc", tmpk), ("sc", tmpk + 1)])
            tt("pool", X[:, t, :], X[:, t, :], tmp, ALU.add,
               [("X", t), ("sc", tmpk), ("sc", tmpk + 1)], [("X", t)])

    def rec_pass(l, j, tile0, ntiles, v, seqs, is_sample):
        L = ntiles * 128
        G = L // 512
        nch = L // CH
        arena_off[0] = 0
        qo = aalloc([128, 4, L], BF16)
        qp = [aalloc([128, 4, L], BF16) for r in range(2)]
        kp = [aalloc([128, 4, L], BF16) for r in range(2)]
        v_c = aalloc([64, nch, 512], BF16)
        ypool = aalloc([128, 4, L], BF16)
        tabS = aalloc([128, 2, 4, nch], F32)
        tabG = aalloc([128, 2, 4, nch], F32)
        tabA = aalloc([128, 2, 4, nch], F32)
        St = aalloc([128, 8, 128], F32)
        Stb = aalloc([128, 8, 128], BF16)
        kTt = aalloc([64, 8, 128], BF16)
        ATm = aalloc([64, 8, 64], BF16)
        Lseq = seqs[0][1]
        icnt = aalloc([128, Lseq], F32)

        if stop <= 1:
            return
        prenorm(l, tile0, G, v)
        src_ic = icnt_s if is_sample else icnt_p
        if stop <= 2:
            return

        queue_pass_mod(l, v)
        slot = w_get(("rin", j), 1024)
        for n in range(4):
            for g in range(G):
                bk = gen_bank()
                proj_fm(slot, n, g, bk)
                act(qo[:, n, g * 512:(g + 1) * 512], PS[:, bk, :], AF.Silu, [("ps", bk)], [("qo", n, g)])

        f_items = [(r, n, g) for r in range(2) for n in range(4) for g in range(G)]
        f_slots = {}

        def f_ctx(idx):
            r, n, g = f_items[idx]
            o4 = 4 * (idx % 2)
            Ts = [sc(o4 + i) for i in range(4)]
            Ks = [("sc", o4 + i) for i in range(4)]
            return r, n, g, Ts, Ks

        def f_s0(idx):
            r, n, g, (T1, T2, T3, T4), (K1, K2, K3, K4) = f_ctx(idx)
            if r not in f_slots:
                f_slots[r] = w_get(("rin", j), 1536 + 512 * r)
            bk = gen_bank()
            proj_fm(f_slots[r], n, g, bk)
            act(T1, PS[:, bk, :], AF.Exp, [("ps", bk)], [K1], scale=-1.0)

        def f_s1(idx):
            r, n, g, (T1, T2, T3, T4), (K1, K2, K3, K4) = f_ctx(idx)
            act(T1, T1, AF.Ln, [K1], [K1], bias=1.0)
            act(T1, T1, AF.Exp, [K1], [K1], scale=-1.0)
            ts("dve", T1, T1, oml[:, j, r, n:n + 1], lb[:, j, r, n:n + 1], ALU.mult, ALU.add, [K1] + LBK, [K1])
            ts("dve", T1, T1, F_MIN, 1.0, ALU.max, ALU.min, [K1], [K1])
            act(T2, T1, AF.Ln, [K1], [K2])
            ts("pool", T3, T1, -1.0, 1.0, ALU.mult, ALU.add, [K1], [K3])

        def f_s2(idx):
            r, n, g, (T1, T2, T3, T4), (K1, K2, K3, K4) = f_ctx(idx)
            S.op("dve", lambda e: e.tensor_tensor_scan(T4, rmask[:], T2, 0.0, ALU.mult, ALU.add),
                 [K2, "rmask"], [K4])
            b3 = T4.rearrange("p (c t) -> p c t", t=CH)
            c0 = g * 8
            tk = ("tab", r, n, g)
            smt = small[:, 32 + 8 * (idx % 2):40 + 8 * (idx % 2)]
            smk = ("smt", idx % 2)
            tt("dve", smt, b3[:, :, CH - 1], b3[:, :, CH // 2 - 1], ALU.subtract, [K4], [smk])
            e_mid = tabS if r == 0 else tabG
            e_dif = tabG if r == 0 else tabS
            act(e_mid[:, r, n, c0:c0 + 8], b3[:, :, CH // 2 - 1], AF.Exp, [K4], [tk + (0,)],
                bias=(-CSH if r == 0 else CSH))
            act(e_dif[:, r, n, c0:c0 + 8], smt, AF.Exp, [smk], [tk + (1,)], bias=(CSH if r == 0 else -CSH))
            act(tabA[:, r, n, c0:c0 + 8], b3[:, :, CH - 1], AF.Exp, [K4], [tk + (2,)])
            tt("pool", T1.rearrange("p (c t) -> p c t", t=CH), b3,
               b3[:, :, CH // 2 - 1:CH // 2].to_broadcast([128, 8, CH]), ALU.subtract, [K4, K1], [K1])
            if r == 1:
                tt("dve", T1, T1, T2, ALU.subtract, [K1, K2], [K1])

        def f_s3(idx):
            r, n, g, (T1, T2, T3, T4), (K1, K2, K3, K4) = f_ctx(idx)
            sg = 1.0 if r == 0 else -1.0
            act(T2, T1, AF.Exp, [K1], [K2], scale=sg, bias=-CSH)
            act(T4, T1, AF.Exp, [K1], [K4], scale=-sg, bias=-CSH)
            tt("dve", qp[r][:, n, g * 512:(g + 1) * 512], qo[:, n, g * 512:(g + 1) * 512], T2, ALU.mult,
               [("qo", n, g), K2], [("qp", r, n, g)])
            tt("pool", kp[r][:, n, g * 512:(g + 1) * 512], T3, T4, ALU.mult, [K3, K4], [("kp", r, n, g)])

        NF = len(f_items)
        for idx in range(NF + 1):
            if idx < NF:
                f_s0(idx)
            if idx >= 1:
                f_s2(idx - 1)
            if idx < NF:
                f_s1(idx)
            if idx >= 1:
                f_s3(idx - 1)

        tg = "_%d_%d" % (l, int(is_sample))
        allk = lambda nm, *pre: [(nm,) + pre + (n, g) for n in range(4) for g in range(G)]
        dbg("hT" + tg, hT[:, :, 0:L], [128, 8, L], [("hT", g, c) for g in range(G) for c in range(8)], BF16)
        dbg("qp0" + tg, qp[0], [128, 4, L], allk("qp", 0), BF16)
        dbg("kp0" + tg, kp[0], [128, 4, L], allk("kp", 0), BF16)
        dbg("qp1" + tg, qp[1], [128, 4, L], allk("qp", 1), BF16)
        dbg("kp1" + tg, kp[1], [128, 4, L], allk("kp", 1), BF16)
        dbg("tabS" + tg, tabS, [128, 2, 4, nch], [("tab", r, n, g, k) for r in range(2) for n in range(4) for g in range(G) for k in range(3)])
        dbg("tabG" + tg, tabG, [128, 2, 4, nch], [("tab", r, n, g, k) for r in range(2) for n in range(4) for g in range(G) for k in range(3)])
        dbg("tabA" + tg, tabA, [128, 2, 4, nch], [("tab", r, n, g, k) for r in range(2) for n in range(4) for g in range(G) for k in range(3)])
        if stop <= 3:
            return
        slot = w_get(("rin", j), 2560)
        for c in range(nch):
            bk = gen_bank()
            proj_tm(slot, c * CH, CH, bk)
            cp("act" if c % 2 == 0 else "dve", v_c[:, c, :], PS[0:64, bk, :], [("ps", bk)], [("v_c", c)])

        slot = w_get(("rin", j), 512)
        for n in range(4):
            for g in range(G):
                bk = gen_bank()
                proj_fm(slot, n, g, bk)
                act(ypool[:, n, g * 512:(g + 1) * 512], PS[:, bk, :], AF.Silu, [("ps", bk)], [("yp", n, g)])

        slot = w_get(("rin", j), 0)
        nseq = len(seqs)
        Wd = nseq * (Lseq + 16)
        for n in range(4):
            win = (2, 4, 8, 16)[n]
            UB, WA, WB = sc2(0), sc2(1), sc2(2)
            DB = sc2(3).bitcast(BF16)
            KU, KA, KB, KD = [[("sc", 2 * i), ("sc", 2 * i + 1)] for i in range(4)]
            S.dma("sp", lambda e, n=n: e.dma_start(out=icnt, in_=src_ic[n:n + 1, :].partition_broadcast(128)),
                  writes=["icnt"])
            S.op("pool", lambda e, UB=UB: e.memset(UB[:, 0:Wd], 0.0), [], KU)
            for g in range(G):
                bk = gen_bank()
                proj_fm(slot, n, g, bk)
                for k, (s0, Ls) in enumerate(seqs):
                    a = max(s0, g * 512)
                    b = min(s0 + Ls, (g + 1) * 512)
                    if a >= b:
                        continue
                    dst0 = k * (Ls + 16) + 8 + (a - s0)
                    cp("act", UB[:, dst0:dst0 + (b - a)], PS[:, bk, a - g * 512:b - g * 512], [("ps", bk)], KU)
            tt("dve", WA[:, 1:Wd], UB[:, 0:Wd - 1], UB[:, 1:Wd], ALU.add, KU, KA)
            cur, curk, oth, othk = WA, KA, WB, KB
            lo, hi = 1, Wd
            if win >= 4:
                tt("dve", oth[:, lo + 1:hi - 1], cur[:, lo:hi - 2], cur[:, lo + 2:hi], ALU.add, curk, othk)
                cur, curk, oth, othk = oth, othk, cur, curk
                lo, hi = lo + 1, hi - 1
            if win >= 8:
                tt("dve", oth[:, lo + 2:hi - 2], cur[:, lo:hi - 4], cur[:, lo + 4:hi], ALU.add, curk, othk)
                cur, curk, oth, othk = oth, othk, cur, curk
                lo, hi = lo + 2, hi - 2
            if win >= 16:
                tt("dve", oth[:, lo + 4:hi - 4], cur[:, lo:hi - 8], cur[:, lo + 8:hi], ALU.add, curk, othk)
                cur, curk, oth, othk = oth, othk, cur, curk
                lo, hi = lo + 4, hi - 4
            assert lo <= 8 and hi >= Wd - 8
            for k, (s0, Ls) in enumerate(seqs):
                base = k * (Ls + 16) + 8
                tt("dve", oth[:, base:base + Ls], cur[:, base:base + Ls], icnt, ALU.mult, curk + ["icnt"], othk)
                tt("dve", DB[:, s0:s0 + Ls], oth[:, base:base + Ls], UB[:, base:base + Ls], ALU.subtract,
                   othk + KU, KD)
            for g in range(G):
                bk = gen_bank()
                mm(PS[:, bk, :], poolw[:, j * 4 + n, :], DB[:, g * 512:(g + 1) * 512], True, True,
                   KD + ["poolw"], [("ps", bk)], signal=True)
                yv = ypool[:, n, g * 512:(g + 1) * 512]
                stt("dve", yv, PS[:, bk, :], psc[:, j, n:n + 1], yv, ALU.mult, ALU.mult,
                    [("ps", bk), "psc", ("yp", n, g)], [("yp", n, g)])

        dbg("vc" + tg, v_c, [64, nch, 512], [("v_c", c) for c in range(nch)], BF16)
        dbg("yp" + tg, ypool, [128, 4, L], allk("yp"), BF16)
        if stop <= 4:
            return
        if is_sample:
            ic_bf = icnt.bitcast(BF16)
            kT2 = [kTt, ic_bf[0:64, 0:1024].rearrange("p (a b) -> p a b", b=128)]
            AT2 = [ATm, ic_bf[0:64, 1024:1536].rearrange("p (a b) -> p a b", b=64)]
            alias_k = ["icnt"]
        else:
            kT2 = [kTt, aalloc([64, 8, 128], BF16)]
            AT2 = [ATm, aalloc([64, 8, 64], BF16)]
            alias_k = []
        for si, (s0, Ls) in enumerate(seqs):
            nst = Ls // CH
            cb = s0 // CH
            for bsel in range(2):
                S.op("dve", lambda e, bsel=bsel: e.memset(AT2[bsel], 0.0), [],
                     [("ATm", bsel, 0), ("ATm", bsel, 1)] + alias_k)
            if is_sample:
                S.dma("sp", lambda e: e.dma_start(out=St, in_=st_in[j]), writes=["St"])
            else:
                S.op("dve", lambda e: e.memset(St, 0.0), [], ["St"])

            def geo(i):
                cr = (cb + i, cb + nst - 1 - i)
                t0 = (cr[0] * CH, cr[1] * CH)
                gq = (t0[0] // 512, t0[1] // 512)
                return cr, t0, gq

            def scan_p(i):
                cr, t0, gq = geo(i)
                bsel = i % 2
                kTb, ATb = kT2[bsel], AT2[bsel]
                bT = 2 + bsel
                for r in range(2):
                    for n in range(4):
                        tr(psb(bT)[0:64, (4 * r + n) * 128:(4 * r + n + 1) * 128], kp[r][:, n, t0[r]:t0[r] + CH],
                           [("kp", r, n, gq[r])], [("ps", bT)], signal=(r == 1 and n == 3))
                cp("act", kTb.rearrange("p a b -> p (a b)"), psb(bT)[0:64, :], [("ps", bT)],
                   [("kTt", bsel)] + alias_k)
                bA = (0, 5)[bsel]
                for r in range(2):
                    for n in range(4):
                        mm(PS[0:64, bA, (4 * r + n) * 64:(4 * r + n + 1) * 64], kp[r][:, n, t0[r]:t0[r] + CH],
                           qp[r][:, n, t0[r]:t0[r] + CH], True, True,
                           [("kp", r, n, gq[r]), ("qp", r, n, gq[r])], [("ps", bA)], signal=(r == 1 and n == 3))
                for r in range(2):
                    mku = mkf[:, r, :].bitcast(mybir.dt.uint32)
                    S.op("dve", lambda e, r=r, mku=mku, bA=bA, ATb=ATb: e.copy_predicated(
                        ATb[:, 4 * r:4 * r + 4, :].rearrange("p a b -> p (a b)"), mku,
                        PS[0:64, bA, 256 * r:256 * r + 256]),
                        [("ps", bA), "mkf"], [("ATm", bsel, r)] + alias_k)

            def scan_q(i):
                cr, t0, gq = geo(i)
                bsel = i % 2
                kTb, ATb = kT2[bsel], AT2[bsel]
                first = i < nst // 2
                tabk = lambda r, kind: [("tab", r, n, gq[r], kind) for n in range(4)]
                if deferred:
                    run_deferred(1)
                for r in range(2):
                    for n in range(4):
                        mm(PS[:, 6 + r, n * 128:(n + 1) * 128], kTb[:, 4 * r + n, :],
                           v_c[:, cr[r], n * 128:(n + 1) * 128], True, True,
                           [("kTt", bsel), ("v_c", cr[r])], [("ps", 6 + r)], signal=(n == 3))
                tmpKV = SC[:, 0:2, 0:512].rearrange("p a (h v) -> p a h v", v=128)
                for r in range(2):
                    for n in range(4):
                        act(tmpKV[:, r, n, :], PS[:, 6 + r, n * 128:(n + 1) * 128], AF.Identity,
                            [("ps", 6 + r)] + tabk(r, 0) + tabk(r, 1), [("sc", r, n)],
                            scale=tabG[:, r, n, cr[r]:cr[r] + 1])
                for r in range(2):
                    tt("dve", Stb[:, 4 * r:4 * r + 4, :], St[:, 4 * r:4 * r + 4, :],
                       tabS[:, r, :, cr[r]:cr[r] + 1].to_broadcast([128, 4, 128]), ALU.mult,
                       ["St"] + tabk(r, 0) + tabk(r, 1), [("Stb", r)])
                for r in range(2):
                    tt("dve", St[:, 4 * r:4 * r + 4, :], St[:, 4 * r:4 * r + 4, :],
                       tabA[:, r, :, cr[r]:cr[r] + 1].to_broadcast([128, 4, 128]), ALU.mult,
                       ["St"] + tabk(r, 2), ["St"])
                St4 = St.rearrange("p (a h) v -> p a h v", a=2)
                tt("dve", St4, St4, tmpKV, ALU.add,
                   ["St"] + [("sc", r, n) for r in range(2) for n in range(4)], ["St", ("sc", 0), ("sc", 1)])
                bO = 4
                for r in range(2):
                    for n in range(4):
                        o_ap = PS[:, bO, (4 * r + n) * 64:(4 * r + n + 1) * 64]
                        mm(o_ap, v_c[:, cr[r], n * 128:(n + 1) * 128], ATb[:, 4 * r + n, :], True, False,
                           [("v_c", cr[r]), ("ATm", bsel, r)], [("ps", bO)], signal=False)
                        mm(o_ap, Stb[:, 4 * r + n, :], qp[r][:, n, t0[r]:t0[r] + CH], False, True,
                           [("Stb", r), ("qp", r, n, gq[r])], [("ps", bO)], signal=(r == 1 and n == 3))
                for r in range(2):
                    dst = qo[:, :, t0[r]:t0[r] + CH]
                    src = PS[:, bO, 256 * r:256 * r + 256].rearrange("p (a b) -> p a b", b=64)
                    ok = [("qo", n, gq[r]) for n in range(4)]
                    E2 = float(np.exp(2.0 * CSH))
                    if first:
                        act(dst, src, AF.Identity, [("ps", bO)], ok, scale=E2)
                    else:
                        stt("dve", dst, src, E2, dst, ALU.mult, ALU.add, [("ps", bO)] + ok, ok)

            S.op("dve", lambda e: e.memset(SC[:, 0:2, 0:8], 0.0), [], [("sc", 0), ("sc", 1)] +
                 [("sc", r, n) for r in range(2) for n in range(4)])
            scan_p(0)
            for i in range(nst):
                if i + 1 < nst:
                    scan_p(i + 1)
                scan_q(i)
            if not is_sample:
                dst = st_out[si, j].rearrange("r h d v -> d (r h) v")
                S.dma("sp", lambda e, dst=dst: e.dma_start(out=dst, in_=St), reads=["St"])

        dbg("o" + tg, qo, [128, 4, L], allk("qo"), BF16)
        if stop <= 5:
            return
        ctr = 0
        for n in range(4):
            for g in range(G):
                o4 = 4 * (ctr % 2)
                ctr += 1
                T1, T2 = sc(o4), sc(o4 + 1)
                K1, K2 = ("sc", o4), ("sc", o4 + 1)
                ov = qo[:, n, g * 512:(g + 1) * 512]
                sqb = scb(o4 + 2, 512)
                act(sqb, ov, AF.Square, [("qo", n, g)], [("sc", o4 + 2)])
                bk = gen_bank()
                mm(PS[:, bk, :], onesb[:], sqb, True, True, [("sc", o4 + 2), "onesb"], [("ps", bk)], signal=True)
                act(T1, PS[:, bk, :], AF.Sqrt, [("ps", bk)], [K1], scale=1.0 / 128, bias=EPS)
                S.op("dve", lambda e, T1=T1: e.reciprocal(T1, T1), [K1], [K1])
                stt("dve", qp[0][:, n, g * 512:(g + 1) * 512], ov, hn[:, j:j + 1], T1, ALU.mult, ALU.mult,
                    [("qo", n, g), "hn", K1, ("qp", 0, n, g)], [("qp", 0, n, g)])

        slot = w_get(("rin", j), 3072)
        ctr = 0
        for n in range(4):
            for g in range(G):
                bk = gen_bank()
                proj_fm(slot, n, g, bk)
                o4 = 4 * (ctr % 2) + 3
                ctr += 1
                sgt = scb(o4, 512)
                act(sgt, PS[:, bk, :], AF.Silu, [("ps", bk)], [("sc", o4)])
                zv = qp[0][:, n, g * 512:(g + 1) * 512]
                tt("dve", zv, zv, sgt, ALU.mult, [("qp", 0, n, g), ("sc", o4)], [("qp", 0, n, g)])
        dbg("z" + tg, qp[0], [128, 4, L], allk("qp", 0), BF16)
        dbg("zin%d_%d" % (l, int(is_sample)), ypool, [128, 4, L], [("yp", n, g) for n in range(4) for g in range(G)])

        def zin_fn(kc, it):
            buf = ypool if kc < 4 else qp[0]
            return buf[:, kc % 4, it * 128:(it + 1) * 128]

        def zin_keys(kc, it):
            g = (it * 128) // 512
            return [("yp", kc, g)] if kc < 4 else [("qp", 0, kc - 4, g)]

        out_proj_and_residual(l, tile0, ntiles, v, zin_fn, zin_keys, ("rout", j))
        S.barrier()

    def att_pass(l, j, tile0, ntiles, v, seqs, is_sample):
        L = ntiles * 128
        G = L // 512
        nkt_cache = 4 if is_sample else 0
        arena_off[0] = 0
        QT = aalloc([128, 8, L], BF16)
        KT = aalloc([128, 2, 512 + L], BF16)
        Vt = aalloc([128, nkt_cache + ntiles, 256], BF16)
        gT = aalloc([128, 8, L], BF16)
        pT = aalloc([128, 3, 512], BF16)
        gq = aalloc([128, 128], F32)
        gk = aalloc([128, 128], F32)
        stg = aalloc([128, 2, 512], F32)
        cosT = aalloc([128, 8, 128], F32)
        sinT = aalloc([128, 8, 128], F32)
        ckt = aalloc([128, 4, 256], BF16)
        if is_sample:
            CGq = aalloc([128, 8, 128], F32)
            SGq = aalloc([128, 8, 128], F32)
            CGk = aalloc([128, 8, 128], F32)
            SGk = aalloc([128, 8, 128], F32)

        S.dma("sp", lambda e: e.dma_start(out=gq, in_=qn_row[j:j + 1, :].partition_broadcast(128)), writes=["gq"])
        S.dma("sp", lambda e: e.dma_start(out=gk, in_=kn_row[j:j + 1, :].partition_broadcast(128)), writes=["gk"])
        if is_sample:
            S.dma("sp", lambda e: e.dma_start(out=cosT, in_=cos_in.rearrange("t p f -> p t f")), writes=["cosT"])
            S.dma("sp", lambda e: e.dma_start(out=sinT, in_=sin_in.rearrange("t p f -> p t f")), writes=["sinT"])
            S.dma("pool", lambda e: e.dma_start(out=ckt, in_=ck_in[j].rearrange("(t p) f -> p t f", p=128)),
                  writes=["ckt"])
            S.dma("pool", lambda e: e.dma_start(out=Vt[:, 0:4, :], in_=cv_in[j].rearrange("(t p) f -> p t f", p=128)),
                  writes=[("Vt", t) for t in range(4)])
        prenorm(l, tile0, G, v)
        if is_sample:
            for t in range(4):
                bk = tr_bank()
                for h in range(2):
                    tr(psb(bk)[:, h * 128:(h + 1) * 128], ckt[:, t, h * 128:(h + 1) * 128], ["ckt"], [("ps", bk)],
                       signal=(h == 1))
                cp("act", KT[:, :, t * 128:(t + 1) * 128], psb(bk)[:, 0:256].rearrange("p (h t) -> p h t", t=128),
                   [("ps", bk)], [("KT", t)])

        def normrope(src_ps, pskeys, nh, gain, it, dst_bf, dkeys, ctr):
            o4 = 4 * (ctr % 2)
            A, B, C = sc(o4, nh * 128), sc(o4 + 1, nh * 128), sc(o4 + 2, nh * 128)
            KA, KB, KC = ("sc", o4), ("sc", o4 + 1), ("sc", o4 + 2)
            ssq = small[:, 48 + 4 * (ctr % 2):48 + 4 * (ctr % 2) + nh]
            sk = ("ssq3", ctr % 2)
            v3 = lambda ap: ap.rearrange("p (h d) -> p h d", d=128)
            act(A, src_ps, AF.Square, pskeys, [KA])
            S.op("dve", lambda e: e.tensor_reduce(out=ssq, in_=v3(A), axis=AX.X, op=ALU.add), [KA], [sk])
            act(ssq, ssq, AF.Sqrt, [sk], [sk], scale=1.0 / 128, bias=EPS)
            S.op("dve", lambda e: e.reciprocal(ssq, ssq), [sk], [sk])
            tt("dve", v3(B), v3(src_ps), ssq.unsqueeze(2).to_broadcast([128, nh, 128]), ALU.mult,
               pskeys + [sk], [KB])
            gainb = gain.unsqueeze(1).to_broadcast([128, nh, 128])
            if not is_sample:
                tt("dve", v3(dst_bf), v3(B), gainb, ALU.mult, [KB, "gq", "gk"], dkeys)
                return B
            CG, SG = (CGq, SGq) if nh == 4 else (CGk, SGk)
            c2 = CG[:, it, :].unsqueeze(1).to_broadcast([128, nh, 128])
            tt("dve", v3(A), v3(B), c2, ALU.mult, [KB, "ropetab"], [KA])
            B4 = B.rearrange("p (h i two) -> p h i two", i=64, two=2)
            C4 = C.rearrange("p (h i two) -> p h i two", i=64, two=2)
            s4 = SG[:, it, :].rearrange("p (i two) -> p i two", two=2)
            for e_ in range(2):
                tt("pool" if e_ == 0 else "dve", C4[:, :, :, e_], B4[:, :, :, 1 - e_],
                   s4[:, :, e_].unsqueeze(1).to_broadcast([128, nh, 64]), ALU.mult, [KB, "ropetab", KC], [(KC[0], KC[1], e_)])
            tt("pool", dst_bf, A, C, ALU.add, [KA, (KC[0], KC[1], 0), (KC[0], KC[1], 1)], dkeys + [KC])
            return B

        if is_sample:
            for (CG, SG, gain) in ((CGq, SGq, gq), (CGk, SGk, gk)):
                gb8 = gain.unsqueeze(1).to_broadcast([128, 8, 128])
                tt("dve", CG, cosT, gb8, ALU.mult, ["cosT", "gq", "gk"], ["ropetab"])
                g2 = gain.rearrange("p (i two) -> p i two", two=2)
                S4 = SG.rearrange("p t (i two) -> p t i two", two=2)
                s8 = sinT.rearrange("p t (i two) -> p t i two", two=2)
                for e_ in range(2):
                    tt("dve", S4[:, :, :, e_], s8[:, :, :, e_],
                       g2[:, :, 1 - e_].unsqueeze(1).to_broadcast([128, 8, 64]), ALU.mult,
                       ["sinT", "gq", "gk"], ["ropetab"])
        queue_pass_mod(l, v)
        if stop == 11:
            return
        items = []
        for qb in range(2):
            for it in range(ntiles):
                items.append(("q", qb, it))
        for it in range(ntiles):
            items.append(("kv", 0, it))
        slots = {}
        pend = []

        def stage_a(ctr, item):
            kind, qb, it = item
            key = (kind, qb)
            if key not in slots:
                slots[key] = w_get(("ain", j), qb * 512 if kind == "q" else 1024)
            slot = slots[key]
            bk = gen_bank()
            proj_tm(slot, it * 128, 128, bk)
            pk = [("ps", bk)]
            dk = [("sc", 4 * (ctr % 2) + 3)]
            if kind == "q":
                qr = scb(4 * (ctr % 2) + 3, 512)
                normrope(PS[:, bk, :], pk, 4, gq, it, qr, dk, ctr)
                return (kind, qb, it, qr, dk, bk, None)
            kr = scb(4 * (ctr % 2) + 3, 256)
            Bn = normrope(PS[:, bk, 0:256], pk, 2, gk, it, kr, dk, ctr)
            kt_idx = nkt_cache + it
            cp("dve", Vt[:, kt_idx, :], PS[:, bk, 256:512], pk, [("Vt", kt_idx)])
            if not is_sample:
                sq = it % 2
                tt("dve", stg[:, sq, 0:256].rearrange("p (h d) -> p h d", d=128),
                   Bn.rearrange("p (h d) -> p h d", d=128), gk.unsqueeze(1).to_broadcast([128, 2, 128]), ALU.mult,
                   [("sc", 4 * (ctr % 2) + 1), "gk"], [("stg", sq, 0)])
                cp("dve", stg[:, sq, 256:512], PS[:, bk, 256:512], pk, [("stg", sq, 1)])
                si, tt0 = divmod(it * 128, 256)
                S.dma("sp", lambda e, si=si, tt0=tt0, sq=sq: e.dma_start(out=ck_out[si, j, tt0:tt0 + 128, :],
                                                                         in_=stg[:, sq, 0:256]), reads=[("stg", sq, 0)])
                S.dma("sp", lambda e, si=si, tt0=tt0, sq=sq: e.dma_start(out=cv_out[si, j, tt0:tt0 + 128, :],
                                                                         in_=stg[:, sq, 256:512]), reads=[("stg", sq, 1)])
            return (kind, qb, it, kr, dk, bk, kt_idx)

        def stage_b(st):
            kind, qb, it, rr, dk, bk, kt_idx = st
            bt = tr_bank()
            if kind == "q":
                for h in range(4):
                    tr(psb(bt)[:, h * 128:(h + 1) * 128], rr[:, h * 128:(h + 1) * 128], dk, [("ps", bt)],
                       signal=(h == 3))
                cp("act", QT[:, qb * 4:qb * 4 + 4, it * 128:(it + 1) * 128],
                   psb(bt)[:, 0:512].rearrange("p (h t) -> p h t", t=128), [("ps", bt)], [("QT", qb, it)])
            else:
                for h in range(2):
                    tr(psb(bt)[:, h * 128:(h + 1) * 128], rr[:, h * 128:(h + 1) * 128], dk, [("ps", bt)],
                       signal=(h == 1))
                cp("act", KT[:, :, kt_idx * 128:(kt_idx + 1) * 128],
                   psb(bt)[:, 0:256].rearrange("p (h t) -> p h t", t=128), [("ps", bt)], [("KT", kt_idx)])

        stop_items = len(items)
        if stop == 12:
            stop_items = 2 * ntiles
        for ctr, item in enumerate(items[:stop_items]):
            st = stage_a(ctr, item)
            if pend:
                stage_b(pend.pop(0))
            pend.append(st)
        while pend:
            stage_b(pend.pop(0))
        if stop in (12, 13):
            return
        for gb in range(2):
            slot = w_get(("ain", j), 1536 + gb * 512)
            for n in range(4):
                for g in range(G):
                    bk = gen_bank()
                    proj_fm(slot, n, g, bk)
                    act(gT[:, gb * 4 + n, g * 512:(g + 1) * 512], PS[:, bk, :], AF.Silu, [("ps", bk)],
                        [("gT", gb * 4 + n, g)])
        if stop == 14:
            return
        scale = 128.0 ** -0.5
        units = []
        itc = 0
        for (s0, Ls) in seqs:
            kt_lo = 0 if is_sample else s0 // 128
            nkt = (nkt_cache + ntiles) if is_sample else Ls // 128
            for h in range(8):
                for q0 in range(s0, s0 + Ls, 512):
                    nq = min(512, s0 + Ls - q0)
                    for ki in range(nkt):
                        units.append(dict(h=h, q0=q0, nq=nq, ki=ki, nkt=nkt, kt=kt_lo + ki, itc=itc))
                    itc += 1
        LA = 2

        def emit_S(idx, u):
            h, q0, nq, kt = u["h"], u["q0"], u["nq"], u["kt"]
            bS = idx % 3
            mm(PS[:, bS, 0:nq], KT[:, h // 4, kt * 128:(kt + 1) * 128], QT[:, h, q0:q0 + nq], True, True,
               [("KT", kt)] + [("QT", h // 4, t) for t in range(q0 // 128, (q0 + nq) // 128)],
               [("ps", bS)], signal=True)

        def emit_rest(idx, u):
            h, q0, nq, kt, ki, nkt, ic = u["h"], u["q0"], u["nq"], u["kt"], u["ki"], u["nkt"], u["itc"]
            kvh = h // 4
            bS = idx % 3
            pb = idx % 3
            bO = 4 + (ic % 2)
            bD = 6 + (ic % 2)
            act(pT[:, pb, 0:nq], PS[:, bS, 0:nq], AF.Exp, [("ps", bS)], [("pT", pb)], scale=scale)
            mm(PS[:, bO, 0:nq], Vt[:, kt, kvh * 128:(kvh + 1) * 128], pT[:, pb, 0:nq], ki == 0,
               ki == nkt - 1, [("Vt", kt), ("pT", pb)], [("ps", bO)], signal=False)
            mm(PS[:, bD, 0:nq], onesb[:], pT[:, pb, 0:nq], ki == 0, ki == nkt - 1,
               [("pT", pb), "onesb"], [("ps", bD)], signal=True)
            if ki == nkt - 1:
                gg = q0 // 512
                o4 = 2 * (ic % 2)
                R1, R2 = sc(o4, nq), sc(o4 + 1, nq)
                S.op("dve", lambda e, R1=R1, bD=bD, nq=nq: e.reciprocal(R1, PS[:, bD, 0:nq]),
                     [("ps", bD)], [("sc", o4)])
                tt("dve", R2, PS[:, bO, 0:nq], R1, ALU.mult, [("ps", bO), ("sc", o4)], [("sc", o4 + 1)])
                gv = gT[:, h, q0:q0 + nq]
                tt("pool", gv, gv, R2, ALU.mult, [("gT", h, gg), ("sc", o4 + 1)], [("gT", h, gg)])

        dstep = max(1, len(units) // 10)
        for idx in range(len(units) + LA):
            if deferred and idx % dstep == dstep - 1:
                run_deferred(3)
            if idx < len(units):
                emit_S(idx, units[idx])
            if idx >= LA:
                emit_rest(idx - LA, units[idx - LA])

        if stop == 15:
            return

        def zin_fn(kc, it):
            return gT[:, kc, it * 128:(it + 1) * 128]

        def zin_keys(kc, it):
            return [("gT", kc, (it * 128) // 512)]

        out_proj_and_residual(l, tile0, ntiles, v, zin_fn, zin_keys, ("aout", j))
        S.barrier()

    for l in range(nlayers):
        j = l // 2
        if l == 0:
            queue_mod_ss(0)
            while deferred:
                run_deferred(6)
        if l % 2 == 0:
            rec_pass(l, j, 0, 4, 0, [(0, 256), (256, 256)], False)
            rec_pass(l, j, 4, 8, 1, [(0, 1024)], True)
        else:
            att_pass(l, j, 0, 4, 0, [(0, 256), (256, 256)], False)
            att_pass(l, j, 4, 8, 1, [(0, 1024)], True)

    for t in range(12):
        S.dma("sp", lambda e, t=t: e.dma_start(out=y_out[t], in_=X[:, t, :]), reads=[("X", t)])

    if wlist_in is not None:
        S.emit_all()
    es.close()
    return nc, dbg_out, wcollect


def _consts():
    ident = np.eye(128, dtype=np.float32)
    ones = np.ones((128, 128), np.float32)
    s = np.arange(64)[:, None]
    t = np.arange(64)[None, :]
    mk = np.stack([(s <= t), (s >= t)], axis=1).astype(np.float32)
    mk = np.ascontiguousarray(np.tile(mk, (1, 1, 4)))
    rmask = np.ones((128, 512), np.float32)
    rmask[:, ::CH] = 0.0

    def icnt(L):
        tt_ = np.arange(L)
        out = []
        for win in (2, 4, 8, 16):
            lo = np.clip(tt_ - win // 2, 0, L)
            hi = np.clip(tt_ + win // 2, 0, L)
            out.append(1.0 / (hi - lo).astype(np.float32))
        return np.stack(out).astype(np.float32)

    tpos = np.arange(1024)
    row = (tpos // 64).astype(np.float32)
    col = (tpos % 64).astype(np.float32)
    inv = (10000.0 ** (-np.arange(0, 64, 2, dtype=np.float32) / 64)).astype(np.float32)
    ang = np.concatenate([row[:, None] * inv[None, :], col[:, None] * inv[None, :]], axis=-1).astype(np.float32)
    cos_t = np.repeat(np.cos(ang).astype(np.float32), 2, axis=-1).reshape(8, 128, 128)
    sn = np.sin(ang).astype(np.float32)
    sin_t = np.stack([-sn, sn], axis=-1).reshape(8, 128, 128)
    return dict(ident=ident, ones=ones, mk=mk, rmask=rmask, icnt_s=icnt(1024), icnt_p=icnt(256),
                cos_t=cos_t, sin_t=sin_t)


def _prep_inputs(x_prompt, x_sample, c, state_hgrn, cache_k, cache_v, c_ctx, ada_w, ada_b,
                 norm_pre, norm_post, rec_w_in, rec_lb_logits, rec_head_norm, pool_w, pool_scale,
                 rec_w_out, att_w_in, att_q_norm, att_k_norm, att_w_out):
    f = lambda a: np.ascontiguousarray(np.asarray(a, dtype=np.float32))
    shared = dict(
        ada_w=f(ada_w),
        ada_bT=f(np.asarray(ada_b).reshape(4, 24, 128).transpose(2, 0, 1)),
        ada_bg=f(np.asarray(ada_b)[:, 2048:3072]),
        npreT=f(np.asarray(norm_pre).reshape(4, 8, 128).transpose(2, 0, 1)),
        npost=f(norm_post),
        rec_w_in=f(rec_w_in), rec_w_out=f(rec_w_out), att_w_in=f(att_w_in), att_w_out=f(att_w_out),
        lbT=f(np.asarray(rec_lb_logits).reshape(2, 2, 4, 128).transpose(3, 0, 1, 2)),
        hnT=f(np.asarray(rec_head_norm).T),
        pool_w=f(pool_w),
        pscT=f(np.asarray(pool_scale).reshape(2, 4, 128).transpose(2, 0, 1)),
        qn_row=f(att_q_norm), kn_row=f(att_k_norm),
    )
    shared.update(_consts())
    maps = []
    for i in range(NCORES):
        xin = np.concatenate([np.asarray(x_prompt[2 * i]).reshape(2, 128, 1024),
                              np.asarray(x_prompt[2 * i + 1]).reshape(2, 128, 1024),
                              np.asarray(x_sample[i]).reshape(8, 128, 1024)], axis=0)
        cvec = np.stack([np.asarray(c_ctx), np.asarray(c[i])], axis=0)
        cT = cvec.reshape(2, 8, 128).transpose(2, 1, 0)
        st = np.asarray(state_hgrn[i]).transpose(0, 3, 1, 2, 4).reshape(2, 128, 8, 128)
        m = dict(shared)
        m.update(x_in=f(xin), cT=f(cT), st_in=f(st),
                 ck_in=f(np.asarray(cache_k[i]).reshape(2, 512, 256)),
                 cv_in=f(np.asarray(cache_v[i]).reshape(2, 512, 256)))
        maps.append(m)
    return maps


_CACHE = {}


def kernel(**inputs):
    maps = _prep_inputs(**inputs)
    if "nc" not in _CACHE:
        _CACHE["nc"] = build_program()[0]
    nc = _CACHE["nc"]
    res = run_bass_kernel_spmd(nc, maps, core_ids=list(range(NCORES)))
    R = res.results
    y_prompt = np.zeros((16, 256, 1024), np.float32)
    y_sample = np.zeros((8, 1024, 1024), np.float32)
    new_state = np.zeros((16, 2, 2, 4, 128, 128), np.float32)
    new_k = np.zeros((16, 2, 256, 2, 128), np.float32)
    new_v = np.zeros((16, 2, 256, 2, 128), np.float32)
    for i in range(NCORES):
        y = np.asarray(R[i]["y_out"])
        y_prompt[2 * i] = y[0:2].reshape(256, 1024)
        y_prompt[2 * i + 1] = y[2:4].reshape(256, 1024)
        y_sample[i] = y[4:12].reshape(1024, 1024)
        new_state[2 * i:2 * i + 2] = np.asarray(R[i]["st_out"])
        new_k[2 * i:2 * i + 2] = np.asarray(R[i]["ck_out"]).reshape(2, 2, 256, 2, 128)
        new_v[2 * i:2 * i + 2] = np.asarray(R[i]["cv_out"]).reshape(2, 2, 256, 2, 128)
    return (y_prompt, y_sample, new_state, new_k, new_v)
```

```python
import numpy as np
from contextlib import ExitStack
import concourse.bass as bass
import concourse.mybir as mybir
from concourse.bass_utils import run_bass_kernel_spmd

F32 = mybir.dt.float32
BF16 = mybir.dt.bfloat16
AF = mybir.ActivationFunctionType
ALU = mybir.AluOpType
AX = mybir.AxisListType

ENGS = ["pe", "act", "dve", "pool", "sp"]
EPS = 1e-6
F_MIN = 1e-6
NCORES = 8
CH = 64
CSH = 20.0
NRING = 3


class Sched:
    def __init__(self, nc, n_dma_sems=16):
        self.nc = nc
        self.ops = {e: [] for e in ENGS}
        self.last_w = {}
        self.readers = {}
        self.n_dma_sems = n_dma_sems
        self.dma_slot_val = {}
        self.dma_rr = {"sp": 0, "pool": 0}
        self.all_dma_tokens = []
        self.pending = {e: set() for e in ENGS}
        self.trace = None

    def _deps(self, reads, writes):
        deps = set()
        for k in reads:
            t = self.last_w.get(k)
            if t is not None:
                deps.add(t)
        for k in writes:
            t = self.last_w.get(k)
            if t is not None:
                deps.add(t)
            for r in self.readers.get(k, ()):
                deps.add(r)
        return deps

    def _commit(self, tok, reads, writes):
        for k in writes:
            self.last_w[k] = tok
            self.readers[k] = []
        for k in reads:
            self.readers.setdefault(k, []).append(tok)

    def op(self, eng, emit, reads=(), writes=(), signal=True):
        deps = self._deps(reads, writes)
        deps |= self.pending[eng]
        self.pending[eng] = set()
        idx = len(self.ops[eng])
        tok = ("e", eng, idx)
        if eng == "pe":
            deps = {d for d in deps if not (d[0] == "e" and d[1] == "pe")}
        self.ops[eng].append(dict(emit=emit, deps=deps, signal=signal, tok=tok, dma=None))
        self._commit(tok, reads, writes)
        return tok

    def dma(self, q, emit, reads=(), writes=()):
        deps = self._deps(reads, writes)
        deps |= self.pending[q]
        self.pending[q] = set()
        slot = self.dma_rr[q]
        self.dma_rr[q] = (slot + 1) % self.n_dma_sems
        prev = self.dma_slot_val.get((q, slot), 0)
        if prev > 0:
            deps.add(("d", (q, slot), prev))
        val = prev + 16
        self.dma_slot_val[(q, slot)] = val
        tok = ("d", (q, slot), val)
        self.ops[q].append(dict(emit=emit, deps=deps, signal=False, tok=tok, dma=(q, slot)))
        self._commit(tok, reads, writes)
        self.all_dma_tokens.append(tok)
        return tok

    def barrier(self):
        toks = set()
        for e in ENGS:
            for i in range(len(self.ops[e]) - 1, -1, -1):
                if self.ops[e][i]["dma"] is None:
                    toks.add(self.ops[e][i]["tok"])
                    break
        for k, v in self.dma_slot_val.items():
            toks.add(("d", k, v))
        for e in ENGS:
            self.pending[e] |= toks

    def emit_all(self):
        nc = self.nc
        with ExitStack() as es:
            esem = {e: es.enter_context(nc.semaphore("sem_" + e)) for e in ENGS}
            dsem = {}
            for k in self.dma_slot_val:
                dsem[k] = es.enter_context(nc.semaphore("dsem_%s_%d" % k))
            counts = {}
            for e in ENGS:
                c = 0
                arr = []
                for o in self.ops[e]:
                    if o["signal"] and o["dma"] is None:
                        c += 1
                    arr.append(c)
                res = [None] * len(arr)
                nxt = None
                for i in range(len(arr) - 1, -1, -1):
                    o = self.ops[e][i]
                    if o["signal"] and o["dma"] is None:
                        nxt = arr[i]
                    res[i] = nxt
                counts[e] = res

            def resolve(tok):
                if tok[0] == "e":
                    v = counts[tok[1]][tok[2]]
                    assert v is not None, ("dep on op with no later signal", tok)
                    return ("e", tok[1]), esem[tok[1]], v
                return tok[1], dsem[tok[1]], tok[2]

            block = es.enter_context(nc.Block())

            def run(ename, eobj):
                seen = {}
                for o in self.ops[ename]:
                    waits = {}
                    for d in o["deps"]:
                        key, sem, v = resolve(d)
                        if v > waits.get(key, (None, 0))[1]:
                            waits[key] = (sem, v)
                    wl = []
                    for key, (sem, v) in waits.items():
                        if seen.get(key, 0) >= v:
                            continue
                        eobj.wait_ge(sem, v)
                        seen[key] = v
                        wl.append((key, v))
                    if self.trace is not None:
                        self.trace.append((ename, o["tok"], wl, o["signal"], o.get("tag")))
                    ins = o["emit"](eobj)
                    if o["dma"] is not None:
                        ins.then_inc(dsem[o["dma"]], 16)
                    elif o["signal"]:
                        ins.then_inc(esem[ename], 1)
                if ename == "sp":
                    for k, v in self.dma_slot_val.items():
                        eobj.wait_ge(dsem[k], v)

            @block.sync
            def _(e):
                run("sp", e)

            @block.tensor
            def _(e):
                run("pe", e)

            @block.scalar
            def _(e):
                run("act", e)

            @block.vector
            def _(e):
                run("dve", e)

            @block.gpsimd
            def _(e):
                run("pool", e)


def build_program(nlayers=4, dbg_names=(), stop=99):
    wl = _build(nlayers, (), stop, None)[2]
    nc, dbg_out, _ = _build(nlayers, dbg_names, stop, wl)
    return nc, dbg_out


def _build(nlayers, dbg_names, stop, wlist_in):
    nc = bass.Bass("TRN2", target_bir_lowering=False)
    es = ExitStack()

    def din(name, shape):
        return nc.dram_tensor(name, list(shape), F32, kind="ExternalInput").ap()

    def dout(name, shape):
        return nc.dram_tensor(name, list(shape), F32, kind="ExternalOutput").ap()

    x_in = din("x_in", [12, 128, 1024])
    cT_in = din("cT", [128, 8, 2])
    st_in = din("st_in", [2, 128, 8, 128])
    ck_in = din("ck_in", [2, 512, 256])
    cv_in = din("cv_in", [2, 512, 256])
    ada_w = din("ada_w", [4, 1024, 3072])
    ada_bT = din("ada_bT", [128, 4, 24])
    ada_bg = din("ada_bg", [4, 1024])
    npreT = din("npreT", [128, 4, 8])
    npost = din("npost", [4, 1024])
    rec_w_in = din("rec_w_in", [2, 1024, 3584])
    rec_w_out = din("rec_w_out", [2, 1024, 1024])
    att_w_in = din("att_w_in", [2, 1024, 2560])
    att_w_out = din("att_w_out", [2, 1024, 1024])
    lbT_in = din("lbT", [128, 2, 2, 4])
    hnT_in = din("hnT", [128, 2])
    pool_w = din("pool_w", [2, 4, 128, 128])
    pscT_in = din("pscT", [128, 2, 4])
    qn_row = din("qn_row", [2, 128])
    kn_row = din("kn_row", [2, 128])
    ident_in = din("ident", [128, 128])
    ones_in = din("ones", [128, 128])
    mk_in = din("mk", [64, 2, 256])
    rmask_in = din("rmask", [128, 512])
    icnt_s = din("icnt_s", [4, 1024])
    icnt_p = din("icnt_p", [4, 256])
    cos_in = din("cos_t", [8, 128, 128])
    sin_in = din("sin_t", [8, 128, 128])

    y_out = dout("y_out", [12, 128, 1024])
    st_out = dout("st_out", [2, 2, 2, 4, 128, 128])
    ck_out = dout("ck_out", [2, 2, 256, 256])
    cv_out = dout("cv_out", [2, 2, 256, 256])
    dbg_out = {}

    def sb(name, shape, dt, stack=None):
        return (stack or es).enter_context(nc.sbuf_tensor("sb_" + name, list(shape), dt))

    S = Sched(nc)

    X = sb("X", [128, 12, 1024], F32)
    ring = sb("ring", [128, NRING, 8, 512], BF16)
    hT = sb("hT", [128, 8, 1024], BF16)
    GT = sb("GT", [128, 2, 1024], F32)
    SC = sb("SC", [128, 8, 520], F32)
    idb = sb("idb", [128, 128], BF16)
    onesb = sb("onesb", [128, 128], BF16)
    onesf = sb("onesf", [128, 128], F32)
    rmask = sb("rmask", [128, 512], F32)
    mkf = sb("mkf", [64, 2, 256], F32)
    cT = sb("cTs", [128, 8, 2], F32)
    scT = sb("scT", [128, 8, 2], BF16)
    sc_rep = sb("sc_rep", [128, 8, 2, 128], BF16)
    adabT = sb("adabT", [128, 4, 24], F32)
    npre = sb("npre", [128, 4, 8], F32)
    lbl = sb("lbl", [128, 2, 2, 4], F32)
    lb = sb("lb", [128, 2, 2, 4], F32)
    oml = sb("oml", [128, 2, 2, 4], F32)
    hn = sb("hn", [128, 2], F32)
    psc = sb("psc", [128, 2, 4], F32)
    poolw = sb("poolw", [128, 8, 128], BF16)
    modT2 = sb("modT", [128, 2, 16, 2], F32)
    Gcol2 = sb("Gcol", [128, 2, 8, 2], F32)
    small = sb("small", [128, 64], F32)
    PS = es.enter_context(nc.psum_tensor("PS", [128, 8, 512], F32))
    ARENA_BYTES = 80 * 1024
    arena = sb("arena", [128, ARENA_BYTES // 4], F32)
    arena_off = [0]

    def aalloc(shape, dt):
        esz = 2 if dt == BF16 else 4
        n = int(np.prod(shape[1:])) * esz
        n = (n + 63) // 64 * 64
        off = arena_off[0]
        assert off + n <= ARENA_BYTES, ("arena overflow", off, n)
        arena_off[0] = off + n
        ap = arena[0:shape[0], off // 4:(off + n) // 4]
        if dt == BF16:
            ap = ap.bitcast(BF16)
        ap = ap[:, 0:int(np.prod(shape[1:]))]
        if len(shape) == 3:
            ap = ap.rearrange("p (a b) -> p a b", b=shape[2])
        elif len(shape) == 4:
            ap = ap.rearrange("p (a b c) -> p a b c", b=shape[2], c=shape[3])
        return ap

    def psb(b):
        return PS[:, b, :].bitcast(BF16)

    def sc(i, n=512):
        return SC[:, i, 0:n]

    def scb(i, n=1024):
        return SC[:, i, :].bitcast(BF16)[:, 0:n]

    def sc2(i):
        return SC[:, 2 * i:2 * i + 2, :].rearrange("p a b -> p (a b)")

    def act(out, in_, func, reads, writes, **kw):
        S.op("act", lambda e: e.activation(out=out, in_=in_, func=func, **kw), reads, writes)

    def tt(eng, out, a, b, op, reads, writes):
        S.op(eng, lambda e: e.tensor_tensor(out, a, b, op), reads, writes)

    def ts(eng, out, a, s1, s2, op0, op1, reads, writes):
        if s2 is None:
            S.op(eng, lambda e: e.tensor_scalar(out, a, s1, None, op0), reads, writes)
        else:
            S.op(eng, lambda e: e.tensor_scalar(out, a, s1, s2, op0, op1), reads, writes)

    def stt(eng, out, in0, scalar, in1, op0, op1, reads, writes):
        S.op(eng, lambda e: e.scalar_tensor_tensor(out, in0, scalar, in1, op0, op1), reads, writes)

    def cp(eng, out, in_, reads, writes):
        if eng == "act":
            S.op("act", lambda e: e.activation(out=out, in_=in_, func=AF.Copy), reads, writes)
        else:
            S.op(eng, lambda e: e.tensor_copy(out, in_), reads, writes)

    def mm(out, lhsT, rhs, start, stop, reads, writes, signal):
        S.op("pe", lambda e: e.matmul(out, lhsT, rhs, start=start, stop=stop), reads, writes, signal=signal)

    def tr(out, in_, reads, writes, signal):
        n = in_.shape[0]
        S.op("pe", lambda e: e.transpose(out, in_, idb[0:n, 0:n]), list(reads) + ["idb"], writes, signal=signal)

    def dbg(name, ap, shape, reads, dt=F32):
        if name not in dbg_names:
            return
        d = nc.dram_tensor("dbg_" + name, list(shape), dt, kind="ExternalOutput").ap()
        dbg_out[name] = d
        S.dma("sp", lambda e: e.dma_start(out=d, in_=ap), reads=reads)

    bank_rr = {"gen": 0, "tr": 0}

    def gen_bank():
        b = bank_rr["gen"]
        bank_rr["gen"] ^= 1
        return b

    def tr_bank():
        b = 2 + bank_rr["tr"]
        bank_rr["tr"] ^= 1
        return b

    wlist = list(wlist_in) if wlist_in is not None else []
    wcollect = []
    wstate = {"issued": 0, "next": 0}

    def w_issue_upto(k):
        while wstate["issued"] <= k and wstate["issued"] < len(wlist):
            i = wstate["issued"]
            wap, c0, ncol = wlist[i]
            slot = i % NRING
            src = wap[:, c0:c0 + ncol].rearrange("(c p) n -> p c n", p=128)
            S.dma("pool", lambda e, slot=slot, src=src, ncol=ncol: e.dma_start(out=ring[:, slot, :, 0:ncol], in_=src),
                  writes=[("ring", slot)])
            wstate["issued"] += 1

    def w_get(wkey, c0, ahead=NRING - 1):
        k = wstate["next"]
        wstate["next"] += 1
        wcollect.append((wkey, c0))
        if wlist_in is not None:
            assert wlist_in[k][3] == (wkey, c0), ("weight stream order mismatch", k, wlist_in[k][3], (wkey, c0))
        w_issue_upto(k + ahead)
        return k % NRING

    WSRC = {}
    for l_ in range(4):
        WSRC[("ada", l_)] = ada_w[l_]
    for j_ in range(2):
        WSRC[("rin", j_)] = rec_w_in[j_]
        WSRC[("rout", j_)] = rec_w_out[j_]
        WSRC[("ain", j_)] = att_w_in[j_]
        WSRC[("aout", j_)] = att_w_out[j_]
    if wlist_in is not None:
        wlist = [(WSRC[wk], c0, 512) for (wk, c0) in wlist_in]
        wlist_in = [(WSRC[wk], c0, 512, (wk, c0)) for (wk, c0) in wlist_in]

    for t in range(12):
        S.dma("sp", lambda e, t=t: e.dma_start(out=X[:, t, :], in_=x_in[t]), writes=[("X", t)])
    S.dma("pool", lambda e: e.dma_start(out=idb[:], in_=ident_in), writes=["idb"])
    S.dma("pool", lambda e: e.dma_start(out=onesb[:], in_=ones_in), writes=["onesb"])
    S.dma("sp", lambda e: e.dma_start(out=onesf[:], in_=ones_in), writes=["onesf"])
    S.dma("pool", lambda e: e.dma_start(out=poolw[:], in_=pool_w.rearrange("j g c d -> c (j g) d")), writes=["poolw"])
    S.dma("sp", lambda e: e.dma_start(out=rmask[:], in_=rmask_in), writes=["rmask"])
    S.dma("sp", lambda e: e.dma_start(out=mkf[:], in_=mk_in), writes=["mkf"])
    S.dma("sp", lambda e: e.dma_start(out=cT[:], in_=cT_in), writes=["cT"])
    S.dma("sp", lambda e: e.dma_start(out=adabT[:], in_=ada_bT), writes=["adabT"])
    S.dma("sp", lambda e: e.dma_start(out=npre[:], in_=npreT), writes=["npre"])
    S.dma("sp", lambda e: e.dma_start(out=lbl[:], in_=lbT_in), writes=["lbl"])
    S.dma("sp", lambda e: e.dma_start(out=hn[:], in_=hnT_in), writes=["hn"])
    S.dma("sp", lambda e: e.dma_start(out=psc[:], in_=pscT_in), writes=["psc"])
    w_issue_upto(NRING - 2)

    act(scT[:], cT[:], AF.Silu, ["cT"], ["scT"])
    cp("dve", sc_rep[:], scT[:].unsqueeze(3).to_broadcast([128, 8, 2, 128]), ["scT"], ["sc_rep"])
    S.op("dve", lambda e: e.memset(lb[:, 0, :, :], 0.0), [], ["lb0"])
    tt("dve", small[:, 0:8], lbl[:, 1, :, :].rearrange("p a b -> p (a b)"),
       lbl[:, 0, :, :].rearrange("p a b -> p (a b)"), ALU.subtract, ["lbl"], ["small"])
    act(lb[:, 1, :, :].rearrange("p a b -> p (a b)"), small[:, 0:8], AF.Sigmoid, ["small"], ["lb1"])
    ts("dve", oml[:].rearrange("p j a b -> p (j a b)"), lb[:].rearrange("p j a b -> p (j a b)"),
       -1.0, 1.0, ALU.mult, ALU.add, ["lb0", "lb1"], ["oml"])
    LBK = ["lb0", "lb1", "oml"]

    deferred = []

    def run_deferred(bk, n=1):
        for _ in range(n):
            if deferred:
                deferred.pop(0)(bk)

    def queue_mod_ss(l):
        modT = modT2[:, l % 2]
        Gcol = Gcol2[:, l % 2]

        def blk(b):
            def f(bk):
                slot = w_get(("ada", l), b * 512)
                for n in range(4):
                    for kc in range(8):
                        mm(PS[:, bk, n * 2:n * 2 + 2], ring[:, slot, kc, n * 128:(n + 1) * 128], scT[:, kc, :],
                           kc == 0, kc == 7, [("ring", slot), "scT"], [("ps", bk)], signal=(kc == 7 and n == 3))
                tt("dve", modT[:, 4 * b:4 * b + 4, :], PS[:, bk, 0:8].rearrange("p (c v) -> p c v", v=2),
                   adabT[:, l, 4 * b:4 * b + 4].unsqueeze(2).to_broadcast([128, 4, 2]), ALU.add,
                   [("ps", bk), "adabT"], [("modT", l % 2, b)])
                if b == 3:
                    stt("dve", Gcol, modT[:, 8:16, :], 1.0, npre[:, l, :].unsqueeze(2).to_broadcast([128, 8, 2]),
                        ALU.add, ALU.mult, [("modT", l % 2, 2), ("modT", l % 2, 3), "npre"], [("Gcol", l % 2)])
            return f
        for b in range(4):
            deferred.append(blk(b))

    def queue_mod_gate(l, v):
        def blk(b):
            def f(bk):
                if b == 0:
                    S.dma("sp", lambda e: e.dma_start(out=sc2(2)[:, 0:1024],
                                                      in_=ada_bg[l:l + 1, :].partition_broadcast(128)),
                          writes=[("sc", 4), ("sc", 5)])
                    S.dma("sp", lambda e: e.dma_start(out=sc2(3)[:, 0:1024],
                                                      in_=npost[l:l + 1, :].partition_broadcast(128)),
                          writes=[("sc", 6), ("sc", 7)])
                slot = w_get(("ada", l), (4 + b) * 512)
                for kc in range(8):
                    mm(PS[:, bk, :], sc_rep[:, kc, v, :], ring[:, slot, kc, :], kc == 0, kc == 7,
                       [("ring", slot), "sc_rep"], [("ps", bk)], signal=(kc == 7))
                tt("dve", GT[:, v, b * 512:(b + 1) * 512], PS[:, bk, :], sc2(2)[:, b * 512:(b + 1) * 512], ALU.add,
                   [("ps", bk), ("sc", 4), ("sc", 5)], [("GT", v, b)])
                tt("dve", GT[:, v, b * 512:(b + 1) * 512], GT[:, v, b * 512:(b + 1) * 512],
                   sc2(3)[:, b * 512:(b + 1) * 512], ALU.mult,
                   [("GT", v, b), ("sc", 6), ("sc", 7)], [("GT", v, b)])
            return f
        for b in range(2):
            deferred.append(blk(b))

    def queue_pass_mod(l, v):
        if v == 1 and l + 1 < nlayers:
            queue_mod_ss(l + 1)
        queue_mod_gate(l, v)

    def prenorm(l, tile0, ngroups, v):
        modT = modT2[:, l % 2]
        Gcol = Gcol2[:, l % 2]
        MK = [("Gcol", l % 2), ("modT", l % 2, 0), ("modT", l % 2, 1)]
        for g in range(ngroups):
            for jt in range(4):
                t = tile0 + g * 4 + jt
                ssq = small[:, 16 + jt:17 + jt]
                rs = small[:, 20 + jt:21 + jt]
                act(scb(4 + jt), X[:, t, :], AF.Square, [("X", t)], [("sc", 4 + jt), ("ssq", jt)], accum_out=ssq)
                act(rs, ssq, AF.Sqrt, [("ssq", jt)], [("rs", jt)], scale=1.0 / 1024, bias=EPS)
                S.op("dve", lambda e, rs=rs: e.reciprocal(rs, rs), [("rs", jt)], [("rs", jt)])
                ts("dve", scb(jt), X[:, t, :], rs, None, ALU.mult, None, [("X", t), ("rs", jt)], [("sc", jt)])
            for cpair in range(4):
                bk = tr_bank()
                for cc in range(2):
                    c = 2 * cpair + cc
                    for jt in range(4):
                        tr(psb(bk)[:, cc * 512 + jt * 128: cc * 512 + (jt + 1) * 128],
                           scb(jt)[:, c * 128:(c + 1) * 128], [("sc", jt)], [("ps", bk)],
                           signal=(cc == 1 and jt == 3))
                for cc in range(2):
                    c = 2 * cpair + cc
                    dst = hT[:, c, g * 512:(g + 1) * 512]
                    src = psb(bk)[:, cc * 512:(cc + 1) * 512]
                    if cpair % 2 == 0:
                        act(dst, src, AF.Identity, [("ps", bk)] + MK, [("hT", g, c)],
                            scale=Gcol[:, c, v:v + 1], bias=modT[:, c, v:v + 1])
                    else:
                        ts("dve", dst, src, Gcol[:, c, v:v + 1], modT[:, c, v:v + 1], ALU.mult, ALU.add,
                           [("ps", bk)] + MK, [("hT", g, c)])

    def hT_keys(g):
        return [("hT", g, c) for c in range(8)]

    def proj_fm(slot, n, g, bk):
        for kc in range(8):
            mm(PS[:, bk, :], ring[:, slot, kc, n * 128:(n + 1) * 128], hT[:, kc, g * 512:(g + 1) * 512],
               kc == 0, kc == 7, [("ring", slot)] + hT_keys(g), [("ps", bk)], signal=(kc == 7))

    def proj_tm(slot, tok0, ntok, bk):
        g = tok0 // 512
        for kc in range(8):
            mm(PS[0:ntok, bk, :], hT[:, kc, tok0:tok0 + ntok], ring[:, slot, kc, :],
               kc == 0, kc == 7, [("ring", slot)] + hT_keys(g), [("ps", bk)], signal=(kc == 7))

    def out_proj_and_residual(l, tile0, ntiles, v, zin_fn, zin_keys_fn, wkey):
        while deferred:
            run_deferred(gen_bank())
        s0 = w_get(wkey, 0)
        s1 = w_get(wkey, 512, NRING - 2)
        slots = (s0, s1)
        for it in range(ntiles):
            t = tile0 + it
            b0 = 4 + 2 * (it % 2)
            for nb in range(2):
                for kc in range(8):
                    mm(PS[:, b0 + nb, :], zin_fn(kc, it), ring[:, slots[nb], kc, :], kc == 0, kc == 7,
                       [("ring", slots[nb])] + zin_keys_fn(kc, it), [("ps", b0 + nb)], signal=(kc == 7))
            po = PS[:, b0:b0 + 2, :].rearrange("p a b -> p (a b)")
            pk = [("ps", b0), ("ps", b0 + 1)]
            ssq = small[:, 24 + (it % 2):25 + (it % 2)]
            rs = small[:, 26 + (it % 2):27 + (it % 2)]
            jk = 6 + (it % 2)
            act(scb(jk), po, AF.Square, pk, [("sc", jk), ("ssq2", it % 2)], accum_out=ssq)
            act(rs, ssq, AF.Sqrt, [("ssq2", it % 2)], [("rs2", it % 2)], scale=1.0 / 1024, bias=EPS)
            S.op("dve", lambda e, rs=rs: e.reciprocal(rs, rs), [("rs2", it % 2)], [("rs2", it % 2)])
            tmpk = 2 * (it % 2)
            tmp = sc2(it % 2)[:, 0:1024]
            stt("dve", tmp, po, rs, GT[:, v, :], ALU.mult, ALU.mult,
                pk + [("rs2", it % 2), ("GT", v, 0), ("GT", v, 1)], [("sc", tmpk), ("sc", tmpk + 1)])
            tt("pool", X[:, t, :], X[:, t, :], tmp, ALU.add,
               [("X", t), ("sc", tmpk), ("sc", tmpk + 1)], [("X", t)])

    def rec_pass(l, j, tile0, ntiles, v, seqs, is_sample):
        L = ntiles * 128
        G = L // 512
        nch = L // CH
        arena_off[0] = 0
        qo = aalloc([128, 4, L], BF16)
        qp = [aalloc([128, 4, L], BF16) for r in range(2)]
        kp = [aalloc([128, 4, L], BF16) for r in range(2)]
        v_c = aalloc([64, nch, 512], BF16)
        ypool = aalloc([128, 4, L], BF16)
        tabS = aalloc([128, 2, 4, nch], F32)
        tabG = aalloc([128, 2, 4, nch], F32)
        tabA = aalloc([128, 2, 4, nch], F32)
        St = aalloc([128, 8, 128], F32)
        Stb = aalloc([128, 8, 128], BF16)
        kTt = aalloc([64, 8, 128], BF16)
        ATm = aalloc([64, 8, 64], BF16)
        Lseq = seqs[0][1]
        icnt = aalloc([128, Lseq], F32)

        if stop <= 1:
            return
        prenorm(l, tile0, G, v)
        src_ic = icnt_s if is_sample else icnt_p
        if stop <= 2:
            return

        queue_pass_mod(l, v)
        slot = w_get(("rin", j), 1024)
        for n in range(4):
            for g in range(G):
                bk = gen_bank()
                proj_fm(slot, n, g, bk)
                act(qo[:, n, g * 512:(g + 1) * 512], PS[:, bk, :], AF.Silu, [("ps", bk)], [("qo", n, g)])

        f_items = [(r, n, g) for r in range(2) for n in range(4) for g in range(G)]
        f_slots = {}

        def f_ctx(idx):
            r, n, g = f_items[idx]
            o4 = 4 * (idx % 2)
            Ts = [sc(o4 + i) for i in range(4)]
            Ks = [("sc", o4 + i) for i in range(4)]
            return r, n, g, Ts, Ks

        def f_s0(idx):
            r, n, g, (T1, T2, T3, T4), (K1, K2, K3, K4) = f_ctx(idx)
            if r not in f_slots:
                f_slots[r] = w_get(("rin", j), 1536 + 512 * r)
            bk = gen_bank()
            proj_fm(f_slots[r], n, g, bk)
            act(T1, PS[:, bk, :], AF.Exp, [("ps", bk)], [K1], scale=-1.0)

        def f_s1(idx):
            r, n, g, (T1, T2, T3, T4), (K1, K2, K3, K4) = f_ctx(idx)
            act(T1, T1, AF.Ln, [K1], [K1], bias=1.0)
            act(T1, T1, AF.Exp, [K1], [K1], scale=-1.0)
            ts("dve", T1, T1, oml[:, j, r, n:n + 1], lb[:, j, r, n:n + 1], ALU.mult, ALU.add, [K1] + LBK, [K1])
            ts("dve", T1, T1, F_MIN, 1.0, ALU.max, ALU.min, [K1], [K1])
            act(T2, T1, AF.Ln, [K1], [K2])
            ts("pool", T3, T1, -1.0, 1.0, ALU.mult, ALU.add, [K1], [K3])

        def f_s2(idx):
            r, n, g, (T1, T2, T3, T4), (K1, K2, K3, K4) = f_ctx(idx)
            S.op("dve", lambda e: e.tensor_tensor_scan(T4, rmask[:], T2, 0.0, ALU.mult, ALU.add),
                 [K2, "rmask"], [K4])
            b3 = T4.rearrange("p (c t) -> p c t", t=CH)
            c0 = g * 8
            tk = ("tab", r, n, g)
            smt = small[:, 32 + 8 * (idx % 2):40 + 8 * (idx % 2)]
            smk = ("smt", idx % 2)
            tt("dve", smt, b3[:, :, CH - 1], b3[:, :, CH // 2 - 1], ALU.subtract, [K4], [smk])
            e_mid = tabS if r == 0 else tabG
            e_dif = tabG if r == 0 else tabS
            act(e_mid[:, r, n, c0:c0 + 8], b3[:, :, CH // 2 - 1], AF.Exp, [K4], [tk + (0,)],
                bias=(-CSH if r == 0 else CSH))
            act(e_dif[:, r, n, c0:c0 + 8], smt, AF.Exp, [smk], [tk + (1,)], bias=(CSH if r == 0 else -CSH))
            act(tabA[:, r, n, c0:c0 + 8], b3[:, :, CH - 1], AF.Exp, [K4], [tk + (2,)])
            tt("pool", T1.rearrange("p (c t) -> p c t", t=CH), b3,
               b3[:, :, CH // 2 - 1:CH // 2].to_broadcast([128, 8, CH]), ALU.subtract, [K4, K1], [K1])
            if r == 1:
                tt("dve", T1, T1, T2, ALU.subtract, [K1, K2], [K1])

        def f_s3(idx):
            r, n, g, (T1, T2, T3, T4), (K1, K2, K3, K4) = f_ctx(idx)
            sg = 1.0 if r == 0 else -1.0
            act(T2, T1, AF.Exp, [K1], [K2], scale=sg, bias=-CSH)
            act(T4, T1, AF.Exp, [K1], [K4], scale=-sg, bias=-CSH)
            tt("dve", qp[r][:, n, g * 512:(g + 1) * 512], qo[:, n, g * 512:(g + 1) * 512], T2, ALU.mult,
               [("qo", n, g), K2], [("qp", r, n, g)])
            tt("pool", kp[r][:, n, g * 512:(g + 1) * 512], T3, T4, ALU.mult, [K3, K4], [("kp", r, n, g)])

        NF = len(f_items)
        for idx in range(NF + 1):
            if idx < NF:
                f_s0(idx)
            if idx >= 1:
                f_s2(idx - 1)
            if idx < NF:
                f_s1(idx)
            if idx >= 1:
                f_s3(idx - 1)

        tg = "_%d_%d" % (l, int(is_sample))
        allk = lambda nm, *pre: [(nm,) + pre + (n, g) for n in range(4) for g in range(G)]
        dbg("hT" + tg, hT[:, :, 0:L], [128, 8, L], [("hT", g, c) for g in range(G) for c in range(8)], BF16)
        dbg("qp0" + tg, qp[0], [128, 4, L], allk("qp", 0), BF16)
        dbg("kp0" + tg, kp[0], [128, 4, L], allk("kp", 0), BF16)
        dbg("qp1" + tg, qp[1], [128, 4, L], allk("qp", 1), BF16)
        dbg("kp1" + tg, kp[1], [128, 4, L], allk("kp", 1), BF16)
        dbg("tabS" + tg, tabS, [128, 2, 4, nch], [("tab", r, n, g, k) for r in range(2) for n in range(4) for g in range(G) for k in range(3)])
        dbg("tabG" + tg, tabG, [128, 2, 4, nch], [("tab", r, n, g, k) for r in range(2) for n in range(4) for g in range(G) for k in range(3)])
        dbg("tabA" + tg, tabA, [128, 2, 4, nch], [("tab", r, n, g, k) for r in range(2) for n in range(4) for g in range(G) for k in range(3)])
        if stop <= 3:
            return
        slot = w_get(("rin", j), 2560)
        for c in range(nch):
            bk = gen_bank()
            proj_tm(slot, c * CH, CH, bk)
            cp("act" if c % 2 == 0 else "dve", v_c[:, c, :], PS[0:64, bk, :], [("ps", bk)], [("v_c", c)])

        slot = w_get(("rin", j), 512)
        for n in range(4):
            for g in range(G):
                bk = gen_bank()
                proj_fm(slot, n, g, bk)
                act(ypool[:, n, g * 512:(g + 1) * 512], PS[:, bk, :], AF.Silu, [("ps", bk)], [("yp", n, g)])

        slot = w_get(("rin", j), 0)
        nseq = len(seqs)
        Wd = nseq * (Lseq + 16)
        for n in range(4):
            win = (2, 4, 8, 16)[n]
            UB, WA, WB = sc2(0), sc2(1), sc2(2)
            DB = sc2(3).bitcast(BF16)
            KU, KA, KB, KD = [[("sc", 2 * i), ("sc", 2 * i + 1)] for i in range(4)]
            S.dma("sp", lambda e, n=n: e.dma_start(out=icnt, in_=src_ic[n:n + 1, :].partition_broadcast(128)),
                  writes=["icnt"])
            S.op("pool", lambda e, UB=UB: e.memset(UB[:, 0:Wd], 0.0), [], KU)
            for g in range(G):
                bk = gen_bank()
                proj_fm(slot, n, g, bk)
                for k, (s0, Ls) in enumerate(seqs):
                    a = max(s0, g * 512)
                    b = min(s0 + Ls, (g + 1) * 512)
                    if a >= b:
                        continue
                    dst0 = k * (Ls + 16) + 8 + (a - s0)
                    cp("act", UB[:, dst0:dst0 + (b - a)], PS[:, bk, a - g * 512:b - g * 512], [("ps", bk)], KU)
            tt("dve", WA[:, 1:Wd], UB[:, 0:Wd - 1], UB[:, 1:Wd], ALU.add, KU, KA)
            cur, curk, oth, othk = WA, KA, WB, KB
            lo, hi = 1, Wd
            if win >= 4:
                tt("dve", oth[:, lo + 1:hi - 1], cur[:, lo:hi - 2], cur[:, lo + 2:hi], ALU.add, curk, othk)
                cur, curk, oth, othk = oth, othk, cur, curk
                lo, hi = lo + 1, hi - 1
            if win >= 8:
                tt("dve", oth[:, lo + 2:hi - 2], cur[:, lo:hi - 4], cur[:, lo + 4:hi], ALU.add, curk, othk)
                cur, curk, oth, othk = oth, othk, cur, curk
                lo, hi = lo + 2, hi - 2
            if win >= 16:
                tt("dve", oth[:, lo + 4:hi - 4], cur[:, lo:hi - 8], cur[:, lo + 8:hi], ALU.add, curk, othk)
                cur, curk, oth, othk = oth, othk, cur, curk
                lo, hi = lo + 4, hi - 4
            assert lo <= 8 and hi >= Wd - 8
            for k, (s0, Ls) in enumerate(seqs):
                base = k * (Ls + 16) + 8
                tt("dve", oth[:, base:base + Ls], cur[:, base:base + Ls], icnt, ALU.mult, curk + ["icnt"], othk)
                tt("dve", DB[:, s0:s0 + Ls], oth[:, base:base + Ls], UB[:, base:base + Ls], ALU.subtract,
                   othk + KU, KD)
            for g in range(G):
                bk = gen_bank()
                mm(PS[:, bk, :], poolw[:, j * 4 + n, :], DB[:, g * 512:(g + 1) * 512], True, True,
                   KD + ["poolw"], [("ps", bk)], signal=True)
                yv = ypool[:, n, g * 512:(g + 1) * 512]
                stt("dve", yv, PS[:, bk, :], psc[:, j, n:n + 1], yv, ALU.mult, ALU.mult,
                    [("ps", bk), "psc", ("yp", n, g)], [("yp", n, g)])

        dbg("vc" + tg, v_c, [64, nch, 512], [("v_c", c) for c in range(nch)], BF16)
        dbg("yp" + tg, ypool, [128, 4, L], allk("yp"), BF16)
        if stop <= 4:
            return
        if is_sample:
            ic_bf = icnt.bitcast(BF16)
            kT2 = [kTt, ic_bf[0:64, 0:1024].rearrange("p (a b) -> p a b", b=128)]
            AT2 = [ATm, ic_bf[0:64, 1024:1536].rearrange("p (a b) -> p a b", b=64)]
            alias_k = ["icnt"]
        else:
            kT2 = [kTt, aalloc([64, 8, 128], BF16)]
            AT2 = [ATm, aalloc([64, 8, 64], BF16)]
            alias_k = []
        for bsel in range(2):
            S.op("dve", lambda e, bsel=bsel: e.memset(AT2[bsel], 0.0), [],
                 [("ATm", bsel, 0), ("ATm", bsel, 1)] + alias_k)
        ctxs = []
        for si, (s0, Ls) in enumerate(seqs):
            if si == 0:
                St_s, Stb_s = St, Stb
            else:
                St_s, Stb_s = aalloc([128, 8, 128], F32), aalloc([128, 8, 128], BF16)
            ctxs.append(dict(si=si, s0=s0, nst=Ls // CH, cb=s0 // CH, St=St_s, Stb=Stb_s))
            if is_sample:
                S.dma("sp", lambda e: e.dma_start(out=St_s, in_=st_in[j]), writes=[("St", si)])
            else:
                S.op("dve", lambda e, St_s=St_s: e.memset(St_s, 0.0), [], [("St", si)])
        gsteps = []
        for i in range(max(c["nst"] for c in ctxs)):
            for c in ctxs:
                if i < c["nst"]:
                    gsteps.append((c, i))

        def geo(c, i):
            cr = (c["cb"] + i, c["cb"] + c["nst"] - 1 - i)
            t0 = (cr[0] * CH, cr[1] * CH)
            gq = (t0[0] // 512, t0[1] // 512)
            return cr, t0, gq

        def scan_p(g):
            c, i = gsteps[g]
            cr, t0, gq = geo(c, i)
            bsel = g % 2
            kTb, ATb = kT2[bsel], AT2[bsel]
            bT = 2 + bsel
            for r in range(2):
                for n in range(4):
                    tr(psb(bT)[0:64, (4 * r + n) * 128:(4 * r + n + 1) * 128], kp[r][:, n, t0[r]:t0[r] + CH],
                       [("kp", r, n, gq[r])], [("ps", bT)], signal=(r == 1 and n == 3))
            cp("act", kTb.rearrange("p a b -> p (a b)"), psb(bT)[0:64, :], [("ps", bT)],
               [("kTt", bsel)] + alias_k)
            bA = (0, 5)[bsel]
            for r in range(2):
                for n in range(4):
                    mm(PS[0:64, bA, (4 * r + n) * 64:(4 * r + n + 1) * 64], kp[r][:, n, t0[r]:t0[r] + CH],
                       qp[r][:, n, t0[r]:t0[r] + CH], True, True,
                       [("kp", r, n, gq[r]), ("qp", r, n, gq[r])], [("ps", bA)], signal=(r == 1 and n == 3))
            for r in range(2):
                mku = mkf[:, r, :].bitcast(mybir.dt.uint32)
                S.op("dve", lambda e, r=r, mku=mku, bA=bA, ATb=ATb: e.copy_predicated(
                    ATb[:, 4 * r:4 * r + 4, :].rearrange("p a b -> p (a b)"), mku,
                    PS[0:64, bA, 256 * r:256 * r + 256]),
                    [("ps", bA), "mkf"], [("ATm", bsel, r)] + alias_k)

        def scan_q(g):
            c, i = gsteps[g]
            cr, t0, gq = geo(c, i)
            bsel = g % 2
            kTb, ATb = kT2[bsel], AT2[bsel]
            St_s, Stb_s, si = c["St"], c["Stb"], c["si"]
            SK = ("St", si)
            first = i < c["nst"] // 2
            tabk = lambda r, kind: [("tab", r, n, gq[r], kind) for n in range(4)]
            if deferred:
                run_deferred(1)
            for r in range(2):
                for n in range(4):
                    mm(PS[:, 6 + r, n * 128:(n + 1) * 128], kTb[:, 4 * r + n, :],
                       v_c[:, cr[r], n * 128:(n + 1) * 128], True, True,
                       [("kTt", bsel), ("v_c", cr[r])], [("ps", 6 + r)], signal=(n == 3))
            ts0 = 2 * bsel
            tmpKV = SC[:, ts0:ts0 + 2, 0:512].rearrange("p a (h v) -> p a h v", v=128)
            for r in range(2):
                for n in range(4):
                    act(tmpKV[:, r, n, :], PS[:, 6 + r, n * 128:(n + 1) * 128], AF.Identity,
                        [("ps", 6 + r)] + tabk(r, 0) + tabk(r, 1), [("sc", ts0 + r, n)],
                        scale=tabG[:, r, n, cr[r]:cr[r] + 1])
            for r in range(2):
                tt("dve", Stb_s[:, 4 * r:4 * r + 4, :], St_s[:, 4 * r:4 * r + 4, :],
                   tabS[:, r, :, cr[r]:cr[r] + 1].to_broadcast([128, 4, 128]), ALU.mult,
                   [SK] + tabk(r, 0) + tabk(r, 1), [("Stb", si, r)])
            for r in range(2):
                tt("dve", St_s[:, 4 * r:4 * r + 4, :], St_s[:, 4 * r:4 * r + 4, :],
                   tabA[:, r, :, cr[r]:cr[r] + 1].to_broadcast([128, 4, 128]), ALU.mult,
                   [SK] + tabk(r, 2), [SK])
            St4 = St_s.rearrange("p (a h) v -> p a h v", a=2)
            tt("dve", St4, St4, tmpKV, ALU.add,
               [SK] + [("sc", ts0 + r, n) for r in range(2) for n in range(4)],
               [SK, ("sc", ts0), ("sc", ts0 + 1)])
            bO = 4
            for r in range(2):
                for n in range(4):
                    o_ap = PS[:, bO, (4 * r + n) * 64:(4 * r + n + 1) * 64]
                    mm(o_ap, v_c[:, cr[r], n * 128:(n + 1) * 128], ATb[:, 4 * r + n, :], True, False,
                       [("v_c", cr[r]), ("ATm", bsel, r)], [("ps", bO)], signal=False)
                    mm(o_ap, Stb_s[:, 4 * r + n, :], qp[r][:, n, t0[r]:t0[r] + CH], False, True,
                       [("Stb", si, r), ("qp", r, n, gq[r])], [("ps", bO)], signal=(r == 1 and n == 3))
            for r in range(2):
                dst = qo[:, :, t0[r]:t0[r] + CH]
                src = PS[:, bO, 256 * r:256 * r + 256].rearrange("p (a b) -> p a b", b=64)
                ok = [("qo", n, gq[r]) for n in range(4)]
                E2 = float(np.exp(2.0 * CSH))
                if first:
                    act(dst, src, AF.Identity, [("ps", bO)], ok, scale=E2)
                else:
                    stt("dve", dst, src, E2, dst, ALU.mult, ALU.add, [("ps", bO)] + ok, ok)

        S.op("dve", lambda e: e.memset(SC[:, 0:4, 0:8], 0.0), [], [("sc", q) for q in range(4)] +
             [("sc", q, n) for q in range(4) for n in range(4)])
        scan_p(0)
        for g in range(len(gsteps)):
            if g + 1 < len(gsteps):
                scan_p(g + 1)
            scan_q(g)
        if not is_sample:
            for c in ctxs:
                dst = st_out[c["si"], j].rearrange("r h d v -> d (r h) v")
                S.dma("sp", lambda e, dst=dst, St_s=c["St"]: e.dma_start(out=dst, in_=St_s), reads=[("St", c["si"])])

        dbg("o" + tg, qo, [128, 4, L], allk("qo"), BF16)
        if stop <= 5:
            return
        ctr = 0
        for n in range(4):
            for g in range(G):
                o4 = 4 * (ctr % 2)
                ctr += 1
                T1, T2 = sc(o4), sc(o4 + 1)
                K1, K2 = ("sc", o4), ("sc", o4 + 1)
                ov = qo[:, n, g * 512:(g + 1) * 512]
                sqb = scb(o4 + 2, 512)
                act(sqb, ov, AF.Square, [("qo", n, g)], [("sc", o4 + 2)])
                bk = gen_bank()
                mm(PS[:, bk, :], onesb[:], sqb, True, True, [("sc", o4 + 2), "onesb"], [("ps", bk)], signal=True)
                act(T1, PS[:, bk, :], AF.Sqrt, [("ps", bk)], [K1], scale=1.0 / 128, bias=EPS)
                S.op("dve", lambda e, T1=T1: e.reciprocal(T1, T1), [K1], [K1])
                stt("dve", qp[0][:, n, g * 512:(g + 1) * 512], ov, hn[:, j:j + 1], T1, ALU.mult, ALU.mult,
                    [("qo", n, g), "hn", K1, ("qp", 0, n, g)], [("qp", 0, n, g)])

        slot = w_get(("rin", j), 3072)
        ctr = 0
        for n in range(4):
            for g in range(G):
                bk = gen_bank()
                proj_fm(slot, n, g, bk)
                o4 = 4 * (ctr % 2) + 3
                ctr += 1
                sgt = scb(o4, 512)
                act(sgt, PS[:, bk, :], AF.Silu, [("ps", bk)], [("sc", o4)])
                zv = qp[0][:, n, g * 512:(g + 1) * 512]
                tt("dve", zv, zv, sgt, ALU.mult, [("qp", 0, n, g), ("sc", o4)], [("qp", 0, n, g)])
        dbg("z" + tg, qp[0], [128, 4, L], allk("qp", 0), BF16)
        dbg("zin%d_%d" % (l, int(is_sample)), ypool, [128, 4, L], [("yp", n, g) for n in range(4) for g in range(G)])

        def zin_fn(kc, it):
            buf = ypool if kc < 4 else qp[0]
            return buf[:, kc % 4, it * 128:(it + 1) * 128]

        def zin_keys(kc, it):
            g = (it * 128) // 512
            return [("yp", kc, g)] if kc < 4 else [("qp", 0, kc - 4, g)]

        out_proj_and_residual(l, tile0, ntiles, v, zin_fn, zin_keys, ("rout", j))
        S.barrier()

    def att_pass(l, j, tile0, ntiles, v, seqs, is_sample):
        L = ntiles * 128
        G = L // 512
        nkt_cache = 4 if is_sample else 0
        arena_off[0] = 0
        QT = aalloc([128, 8, L], BF16)
        KT = aalloc([128, 2, 512 + L], BF16)
        Vt = aalloc([128, nkt_cache + ntiles, 256], BF16)
        gT = aalloc([128, 8, L], BF16)
        pT = aalloc([128, 3, 512], BF16)
        gq = aalloc([128, 128], F32)
        gk = aalloc([128, 128], F32)
        stg = aalloc([128, 2, 512], F32)
        cosT = aalloc([128, 8, 128], F32)
        sinT = aalloc([128, 8, 128], F32)
        ckt = aalloc([128, 4, 256], BF16)
        if is_sample:
            CGq = aalloc([128, 8, 128], F32)
            SGq = aalloc([128, 8, 128], F32)
            CGk = aalloc([128, 8, 128], F32)
            SGk = aalloc([128, 8, 128], F32)

        S.dma("sp", lambda e: e.dma_start(out=gq, in_=qn_row[j:j + 1, :].partition_broadcast(128)), writes=["gq"])
        S.dma("sp", lambda e: e.dma_start(out=gk, in_=kn_row[j:j + 1, :].partition_broadcast(128)), writes=["gk"])
        if is_sample:
            S.dma("sp", lambda e: e.dma_start(out=cosT, in_=cos_in.rearrange("t p f -> p t f")), writes=["cosT"])
            S.dma("sp", lambda e: e.dma_start(out=sinT, in_=sin_in.rearrange("t p f -> p t f")), writes=["sinT"])
            S.dma("pool", lambda e: e.dma_start(out=ckt, in_=ck_in[j].rearrange("(t p) f -> p t f", p=128)),
                  writes=["ckt"])
            S.dma("pool", lambda e: e.dma_start(out=Vt[:, 0:4, :], in_=cv_in[j].rearrange("(t p) f -> p t f", p=128)),
                  writes=[("Vt", t) for t in range(4)])
        prenorm(l, tile0, G, v)
        if is_sample:
            for t in range(4):
                bk = tr_bank()
                for h in range(2):
                    tr(psb(bk)[:, h * 128:(h + 1) * 128], ckt[:, t, h * 128:(h + 1) * 128], ["ckt"], [("ps", bk)],
                       signal=(h == 1))
                cp("act", KT[:, :, t * 128:(t + 1) * 128], psb(bk)[:, 0:256].rearrange("p (h t) -> p h t", t=128),
                   [("ps", bk)], [("KT", t)])

        def normrope(src_ps, pskeys, nh, gain, it, dst_bf, dkeys, ctr):
            o4 = 4 * (ctr % 2)
            A, B, C = sc(o4, nh * 128), sc(o4 + 1, nh * 128), sc(o4 + 2, nh * 128)
            KA, KB, KC = ("sc", o4), ("sc", o4 + 1), ("sc", o4 + 2)
            ssq = small[:, 48 + 4 * (ctr % 2):48 + 4 * (ctr % 2) + nh]
            sk = ("ssq3", ctr % 2)
            v3 = lambda ap: ap.rearrange("p (h d) -> p h d", d=128)
            act(A, src_ps, AF.Square, pskeys, [KA])
            S.op("dve", lambda e: e.tensor_reduce(out=ssq, in_=v3(A), axis=AX.X, op=ALU.add), [KA], [sk])
            act(ssq, ssq, AF.Sqrt, [sk], [sk], scale=1.0 / 128, bias=EPS)
            S.op("dve", lambda e: e.reciprocal(ssq, ssq), [sk], [sk])
            tt("dve", v3(B), v3(src_ps), ssq.unsqueeze(2).to_broadcast([128, nh, 128]), ALU.mult,
               pskeys + [sk], [KB])
            gainb = gain.unsqueeze(1).to_broadcast([128, nh, 128])
            if not is_sample:
                tt("dve", v3(dst_bf), v3(B), gainb, ALU.mult, [KB, "gq", "gk"], dkeys)
                return B
            CG, SG = (CGq, SGq) if nh == 4 else (CGk, SGk)
            c2 = CG[:, it, :].unsqueeze(1).to_broadcast([128, nh, 128])
            tt("dve", v3(A), v3(B), c2, ALU.mult, [KB, "ropetab"], [KA])
            B4 = B.rearrange("p (h i two) -> p h i two", i=64, two=2)
            C4 = C.rearrange("p (h i two) -> p h i two", i=64, two=2)
            s4 = SG[:, it, :].rearrange("p (i two) -> p i two", two=2)
            for e_ in range(2):
                tt("pool" if e_ == 0 else "dve", C4[:, :, :, e_], B4[:, :, :, 1 - e_],
                   s4[:, :, e_].unsqueeze(1).to_broadcast([128, nh, 64]), ALU.mult, [KB, "ropetab", KC], [(KC[0], KC[1], e_)])
            tt("pool", dst_bf, A, C, ALU.add, [KA, (KC[0], KC[1], 0), (KC[0], KC[1], 1)], dkeys + [KC])
            return B

        if is_sample:
            for (CG, SG, gain) in ((CGq, SGq, gq), (CGk, SGk, gk)):
                gb8 = gain.unsqueeze(1).to_broadcast([128, 8, 128])
                tt("dve", CG, cosT, gb8, ALU.mult, ["cosT", "gq", "gk"], ["ropetab"])
                g2 = gain.rearrange("p (i two) -> p i two", two=2)
                S4 = SG.rearrange("p t (i two) -> p t i two", two=2)
                s8 = sinT.rearrange("p t (i two) -> p t i two", two=2)
                for e_ in range(2):
                    tt("dve", S4[:, :, :, e_], s8[:, :, :, e_],
                       g2[:, :, 1 - e_].unsqueeze(1).to_broadcast([128, 8, 64]), ALU.mult,
                       ["sinT", "gq", "gk"], ["ropetab"])
        queue_pass_mod(l, v)
        if stop == 11:
            return
        items = []
        for qb in range(2):
            for it in range(ntiles):
                items.append(("q", qb, it))
        for it in range(ntiles):
            items.append(("kv", 0, it))
        slots = {}
        pend = []

        def stage_a(ctr, item):
            kind, qb, it = item
            key = (kind, qb)
            if key not in slots:
                slots[key] = w_get(("ain", j), qb * 512 if kind == "q" else 1024)
            slot = slots[key]
            bk = gen_bank()
            proj_tm(slot, it * 128, 128, bk)
            pk = [("ps", bk)]
            dk = [("sc", 4 * (ctr % 2) + 3)]
            if kind == "q":
                qr = scb(4 * (ctr % 2) + 3, 512)
                normrope(PS[:, bk, :], pk, 4, gq, it, qr, dk, ctr)
                return (kind, qb, it, qr, dk, bk, None)
            kr = scb(4 * (ctr % 2) + 3, 256)
            Bn = normrope(PS[:, bk, 0:256], pk, 2, gk, it, kr, dk, ctr)
            kt_idx = nkt_cache + it
            cp("dve", Vt[:, kt_idx, :], PS[:, bk, 256:512], pk, [("Vt", kt_idx)])
            if not is_sample:
                sq = it % 2
                tt("dve", stg[:, sq, 0:256].rearrange("p (h d) -> p h d", d=128),
                   Bn.rearrange("p (h d) -> p h d", d=128), gk.unsqueeze(1).to_broadcast([128, 2, 128]), ALU.mult,
                   [("sc", 4 * (ctr % 2) + 1), "gk"], [("stg", sq, 0)])
                cp("dve", stg[:, sq, 256:512], PS[:, bk, 256:512], pk, [("stg", sq, 1)])
                si, tt0 = divmod(it * 128, 256)
                S.dma("sp", lambda e, si=si, tt0=tt0, sq=sq: e.dma_start(out=ck_out[si, j, tt0:tt0 + 128, :],
                                                                         in_=stg[:, sq, 0:256]), reads=[("stg", sq, 0)])
                S.dma("sp", lambda e, si=si, tt0=tt0, sq=sq: e.dma_start(out=cv_out[si, j, tt0:tt0 + 128, :],
                                                                         in_=stg[:, sq, 256:512]), reads=[("stg", sq, 1)])
            return (kind, qb, it, kr, dk, bk, kt_idx)

        def stage_b(st):
            kind, qb, it, rr, dk, bk, kt_idx = st
            bt = tr_bank()
            if kind == "q":
                for h in range(4):
                    tr(psb(bt)[:, h * 128:(h + 1) * 128], rr[:, h * 128:(h + 1) * 128], dk, [("ps", bt)],
                       signal=(h == 3))
                cp("act", QT[:, qb * 4:qb * 4 + 4, it * 128:(it + 1) * 128],
                   psb(bt)[:, 0:512].rearrange("p (h t) -> p h t", t=128), [("ps", bt)], [("QT", qb, it)])
            else:
                for h in range(2):
                    tr(psb(bt)[:, h * 128:(h + 1) * 128], rr[:, h * 128:(h + 1) * 128], dk, [("ps", bt)],
                       signal=(h == 1))
                cp("act", KT[:, :, kt_idx * 128:(kt_idx + 1) * 128],
                   psb(bt)[:, 0:256].rearrange("p (h t) -> p h t", t=128), [("ps", bt)], [("KT", kt_idx)])

        stop_items = len(items)
        if stop == 12:
            stop_items = 2 * ntiles
        for ctr, item in enumerate(items[:stop_items]):
            st = stage_a(ctr, item)
            if pend:
                stage_b(pend.pop(0))
            pend.append(st)
        while pend:
            stage_b(pend.pop(0))
        if stop in (12, 13):
            return
        for gb in range(2):
            slot = w_get(("ain", j), 1536 + gb * 512)
            for n in range(4):
                for g in range(G):
                    bk = gen_bank()
                    proj_fm(slot, n, g, bk)
                    act(gT[:, gb * 4 + n, g * 512:(g + 1) * 512], PS[:, bk, :], AF.Silu, [("ps", bk)],
                        [("gT", gb * 4 + n, g)])
        if stop == 14:
            return
        scale = 128.0 ** -0.5
        units = []
        itc = 0
        for (s0, Ls) in seqs:
            kt_lo = 0 if is_sample else s0 // 128
            nkt = (nkt_cache + ntiles) if is_sample else Ls // 128
            for h in range(8):
                for q0 in range(s0, s0 + Ls, 512):
                    nq = min(512, s0 + Ls - q0)
                    for ki in range(nkt):
                        units.append(dict(h=h, q0=q0, nq=nq, ki=ki, nkt=nkt, kt=kt_lo + ki, itc=itc))
                    itc += 1
        LA = 2

        def emit_S(idx, u):
            h, q0, nq, kt = u["h"], u["q0"], u["nq"], u["kt"]
            bS = idx % 3
            mm(PS[:, bS, 0:nq], KT[:, h // 4, kt * 128:(kt + 1) * 128], QT[:, h, q0:q0 + nq], True, True,
               [("KT", kt)] + [("QT", h // 4, t) for t in range(q0 // 128, (q0 + nq) // 128)],
               [("ps", bS)], signal=True)

        def emit_rest(idx, u):
            h, q0, nq, kt, ki, nkt, ic = u["h"], u["q0"], u["nq"], u["kt"], u["ki"], u["nkt"], u["itc"]
            kvh = h // 4
            bS = idx % 3
            pb = idx % 3
            bO = 4 + (ic % 2)
            bD = 6 + (ic % 2)
            act(pT[:, pb, 0:nq], PS[:, bS, 0:nq], AF.Exp, [("ps", bS)], [("pT", pb)], scale=scale)
            mm(PS[:, bO, 0:nq], Vt[:, kt, kvh * 128:(kvh + 1) * 128], pT[:, pb, 0:nq], ki == 0,
               ki == nkt - 1, [("Vt", kt), ("pT", pb)], [("ps", bO)], signal=False)
            mm(PS[:, bD, 0:nq], onesb[:], pT[:, pb, 0:nq], ki == 0, ki == nkt - 1,
               [("pT", pb), "onesb"], [("ps", bD)], signal=True)
            if ki == nkt - 1:
                gg = q0 // 512
                o4 = 2 * (ic % 2)
                R1, R2 = sc(o4, nq), sc(o4 + 1, nq)
                S.op("dve", lambda e, R1=R1, bD=bD, nq=nq: e.reciprocal(R1, PS[:, bD, 0:nq]),
                     [("ps", bD)], [("sc", o4)])
                tt("dve", R2, PS[:, bO, 0:nq], R1, ALU.mult, [("ps", bO), ("sc", o4)], [("sc", o4 + 1)])
                gv = gT[:, h, q0:q0 + nq]
                tt("pool", gv, gv, R2, ALU.mult, [("gT", h, gg), ("sc", o4 + 1)], [("gT", h, gg)])

        dstep = max(1, len(units) // 10)
        for idx in range(len(units) + LA):
            if deferred and idx % dstep == dstep - 1:
                run_deferred(3)
            if idx < len(units):
                emit_S(idx, units[idx])
            if idx >= LA:
                emit_rest(idx - LA, units[idx - LA])

        if stop == 15:
            return

        def zin_fn(kc, it):
            return gT[:, kc, it * 128:(it + 1) * 128]

        def zin_keys(kc, it):
            return [("gT", kc, (it * 128) // 512)]

        out_proj_and_residual(l, tile0, ntiles, v, zin_fn, zin_keys, ("aout", j))
        S.barrier()

    for l in range(nlayers):
        j = l // 2
        if l == 0:
            queue_mod_ss(0)
            while deferred:
                run_deferred(6)
        if l % 2 == 0:
            rec_pass(l, j, 0, 4, 0, [(0, 256), (256, 256)], False)
            rec_pass(l, j, 4, 8, 1, [(0, 1024)], True)
        else:
            att_pass(l, j, 0, 4, 0, [(0, 256), (256, 256)], False)
            att_pass(l, j, 4, 8, 1, [(0, 1024)], True)

    for t in range(12):
        S.dma("sp", lambda e, t=t: e.dma_start(out=y_out[t], in_=X[:, t, :]), reads=[("X", t)])

    if wlist_in is not None:
        S.emit_all()
    es.close()
    return nc, dbg_out, wcollect


def _consts():
    ident = np.eye(128, dtype=np.float32)
    ones = np.ones((128, 128), np.float32)
    s = np.arange(64)[:, None]
    t = np.arange(64)[None, :]
    mk = np.stack([(s <= t), (s >= t)], axis=1).astype(np.float32)
    mk = np.ascontiguousarray(np.tile(mk, (1, 1, 4)))
    rmask = np.ones((128, 512), np.float32)
    rmask[:, ::CH] = 0.0

    def icnt(L):
        tt_ = np.arange(L)
        out = []
        for win in (2, 4, 8, 16):
            lo = np.clip(tt_ - win // 2, 0, L)
            hi = np.clip(tt_ + win // 2, 0, L)
            out.append(1.0 / (hi - lo).astype(np.float32))
        return np.stack(out).astype(np.float32)

    tpos = np.arange(1024)
    row = (tpos // 64).astype(np.float32)
    col = (tpos % 64).astype(np.float32)
    inv = (10000.0 ** (-np.arange(0, 64, 2, dtype=np.float32) / 64)).astype(np.float32)
    ang = np.concatenate([row[:, None] * inv[None, :], col[:, None] * inv[None, :]], axis=-1).astype(np.float32)
    cos_t = np.repeat(np.cos(ang).astype(np.float32), 2, axis=-1).reshape(8, 128, 128)
    sn = np.sin(ang).astype(np.float32)
    sin_t = np.stack([-sn, sn], axis=-1).reshape(8, 128, 128)
    return dict(ident=ident, ones=ones, mk=mk, rmask=rmask, icnt_s=icnt(1024), icnt_p=icnt(256),
                cos_t=cos_t, sin_t=sin_t)


def _prep_inputs(x_prompt, x_sample, c, state_hgrn, cache_k, cache_v, c_ctx, ada_w, ada_b,
                 norm_pre, norm_post, rec_w_in, rec_lb_logits, rec_head_norm, pool_w, pool_scale,
                 rec_w_out, att_w_in, att_q_norm, att_k_norm, att_w_out):
    f = lambda a: np.ascontiguousarray(np.asarray(a, dtype=np.float32))
    shared = dict(
        ada_w=f(ada_w),
        ada_bT=f(np.asarray(ada_b).reshape(4, 24, 128).transpose(2, 0, 1)),
        ada_bg=f(np.asarray(ada_b)[:, 2048:3072]),
        npreT=f(np.asarray(norm_pre).reshape(4, 8, 128).transpose(2, 0, 1)),
        npost=f(norm_post),
        rec_w_in=f(rec_w_in), rec_w_out=f(rec_w_out), att_w_in=f(att_w_in), att_w_out=f(att_w_out),
        lbT=f(np.asarray(rec_lb_logits).reshape(2, 2, 4, 128).transpose(3, 0, 1, 2)),
        hnT=f(np.asarray(rec_head_norm).T),
        pool_w=f(pool_w),
        pscT=f(np.asarray(pool_scale).reshape(2, 4, 128).transpose(2, 0, 1)),
        qn_row=f(att_q_norm), kn_row=f(att_k_norm),
    )
    shared.update(_consts())
    maps = []
    for i in range(NCORES):
        xin = np.concatenate([np.asarray(x_prompt[2 * i]).reshape(2, 128, 1024),
                              np.asarray(x_prompt[2 * i + 1]).reshape(2, 128, 1024),
                              np.asarray(x_sample[i]).reshape(8, 128, 1024)], axis=0)
        cvec = np.stack([np.asarray(c_ctx), np.asarray(c[i])], axis=0)
        cT = cvec.reshape(2, 8, 128).transpose(2, 1, 0)
        st = np.asarray(state_hgrn[i]).transpose(0, 3, 1, 2, 4).reshape(2, 128, 8, 128)
        m = dict(shared)
        m.update(x_in=f(xin), cT=f(cT), st_in=f(st),
                 ck_in=f(np.asarray(cache_k[i]).reshape(2, 512, 256)),
                 cv_in=f(np.asarray(cache_v[i]).reshape(2, 512, 256)))
        maps.append(m)
    return maps


_CACHE = {}


def kernel(**inputs):
    maps = _prep_inputs(**inputs)
    if "nc" not in _CACHE:
        _CACHE["nc"] = build_program()[0]
    nc = _CACHE["nc"]
    res = run_bass_kernel_spmd(nc, maps, core_ids=list(range(NCORES)))
    R = res.results
    y_prompt = np.zeros((16, 256, 1024), np.float32)
    y_sample = np.zeros((8, 1024, 1024), np.float32)
    new_state = np.zeros((16, 2, 2, 4, 128, 128), np.float32)
    new_k = np.zeros((16, 2, 256, 2, 128), np.float32)
    new_v = np.zeros((16, 2, 256, 2, 128), np.float32)
    for i in range(NCORES):
        y = np.asarray(R[i]["y_out"])
        y_prompt[2 * i] = y[0:2].reshape(256, 1024)
        y_prompt[2 * i + 1] = y[2:4].reshape(256, 1024)
        y_sample[i] = y[4:12].reshape(1024, 1024)
        new_state[2 * i:2 * i + 2] = np.asarray(R[i]["st_out"])
        new_k[2 * i:2 * i + 2] = np.asarray(R[i]["ck_out"]).reshape(2, 2, 256, 2, 128)
        new_v[2 * i:2 * i + 2] = np.asarray(R[i]["cv_out"]).reshape(2, 2, 256, 2, 128)
    return (y_prompt, y_sample, new_state, new_k, new_v)
```

```python
import numpy as np
from contextlib import ExitStack
import concourse.bass as bass
import concourse.mybir as mybir
from concourse.bass_utils import run_bass_kernel_spmd

F32 = mybir.dt.float32
BF16 = mybir.dt.bfloat16
AF = mybir.ActivationFunctionType
ALU = mybir.AluOpType
AX = mybir.AxisListType

ENGS = ["pe", "act", "dve", "pool", "sp"]
EPS = 1e-6
F_MIN = 1e-6
NCORES = 8
CH = 64
CSH = 20.0
NRING = 3


class Sched:
    def __init__(self, nc, n_dma_sems=16):
        self.nc = nc
        self.ops = {e: [] for e in ENGS}
        self.last_w = {}
        self.readers = {}
        self.n_dma_sems = n_dma_sems
        self.dma_slot_val = {}
        self.dma_rr = {"sp": 0, "pool": 0}
        self.all_dma_tokens = []
        self.pending = {e: set() for e in ENGS}
        self.trace = None

    def _deps(self, reads, writes):
        deps = set()
        for k in reads:
            t = self.last_w.get(k)
            if t is not None:
                deps.add(t)
        for k in writes:
            t = self.last_w.get(k)
            if t is not None:
                deps.add(t)
            for r in self.readers.get(k, ()):
                deps.add(r)
        return deps

    def _commit(self, tok, reads, writes):
        for k in writes:
            self.last_w[k] = tok
            self.readers[k] = []
        for k in reads:
            self.readers.setdefault(k, []).append(tok)

    def op(self, eng, emit, reads=(), writes=(), signal=True):
        deps = self._deps(reads, writes)
        deps |= self.pending[eng]
        self.pending[eng] = set()
        idx = len(self.ops[eng])
        tok = ("e", eng, idx)
        if eng == "pe":
            deps = {d for d in deps if not (d[0] == "e" and d[1] == "pe")}
        self.ops[eng].append(dict(emit=emit, deps=deps, signal=signal, tok=tok, dma=None))
        self._commit(tok, reads, writes)
        return tok

    def dma(self, q, emit, reads=(), writes=()):
        deps = self._deps(reads, writes)
        deps |= self.pending[q]
        self.pending[q] = set()
        slot = self.dma_rr[q]
        self.dma_rr[q] = (slot + 1) % self.n_dma_sems
        prev = self.dma_slot_val.get((q, slot), 0)
        if prev > 0:
            deps.add(("d", (q, slot), prev))
        val = prev + 16
        self.dma_slot_val[(q, slot)] = val
        tok = ("d", (q, slot), val)
        self.ops[q].append(dict(emit=emit, deps=deps, signal=False, tok=tok, dma=(q, slot)))
        self._commit(tok, reads, writes)
        self.all_dma_tokens.append(tok)
        return tok

    def barrier(self):
        toks = set()
        for e in ENGS:
            for i in range(len(self.ops[e]) - 1, -1, -1):
                if self.ops[e][i]["dma"] is None:
                    toks.add(self.ops[e][i]["tok"])
                    break
        for k, v in self.dma_slot_val.items():
            toks.add(("d", k, v))
        for e in ENGS:
            self.pending[e] |= toks

    def emit_all(self):
        nc = self.nc
        with ExitStack() as es:
            esem = {e: es.enter_context(nc.semaphore("sem_" + e)) for e in ENGS}
            dsem = {}
            for k in self.dma_slot_val:
                dsem[k] = es.enter_context(nc.semaphore("dsem_%s_%d" % k))
            counts = {}
            for e in ENGS:
                c = 0
                arr = []
                for o in self.ops[e]:
                    if o["signal"] and o["dma"] is None:
                        c += 1
                    arr.append(c)
                res = [None] * len(arr)
                nxt = None
                for i in range(len(arr) - 1, -1, -1):
                    o = self.ops[e][i]
                    if o["signal"] and o["dma"] is None:
                        nxt = arr[i]
                    res[i] = nxt
                counts[e] = res

            def resolve(tok):
                if tok[0] == "e":
                    v = counts[tok[1]][tok[2]]
                    assert v is not None, ("dep on op with no later signal", tok)
                    return ("e", tok[1]), esem[tok[1]], v
                return tok[1], dsem[tok[1]], tok[2]

            block = es.enter_context(nc.Block())

            def run(ename, eobj):
                seen = {}
                for o in self.ops[ename]:
                    waits = {}
                    for d in o["deps"]:
                        key, sem, v = resolve(d)
                        if v > waits.get(key, (None, 0))[1]:
                            waits[key] = (sem, v)
                    wl = []
                    for key, (sem, v) in waits.items():
                        if seen.get(key, 0) >= v:
                            continue
                        eobj.wait_ge(sem, v)
                        seen[key] = v
                        wl.append((key, v))
                    if self.trace is not None:
                        self.trace.append((ename, o["tok"], wl, o["signal"], o.get("tag")))
                    ins = o["emit"](eobj)
                    if o["dma"] is not None:
                        ins.then_inc(dsem[o["dma"]], 16)
                    elif o["signal"]:
                        ins.then_inc(esem[ename], 1)
                if ename == "sp":
                    for k, v in self.dma_slot_val.items():
                        eobj.wait_ge(dsem[k], v)

            @block.sync
            def _(e):
                run("sp", e)

            @block.tensor
            def _(e):
                run("pe", e)

            @block.scalar
            def _(e):
                run("act", e)

            @block.vector
            def _(e):
                run("dve", e)

            @block.gpsimd
            def _(e):
                run("pool", e)


def build_program(nlayers=4, dbg_names=(), stop=99):
    wl = _build(nlayers, (), stop, None)[2]
    nc, dbg_out, _ = _build(nlayers, dbg_names, stop, wl)
    return nc, dbg_out


def _build(nlayers, dbg_names, stop, wlist_in):
    nc = bass.Bass("TRN2", target_bir_lowering=False)
    es = ExitStack()

    def din(name, shape):
        return nc.dram_tensor(name, list(shape), F32, kind="ExternalInput").ap()

    def dout(name, shape):
        return nc.dram_tensor(name, list(shape), F32, kind="ExternalOutput").ap()

    x_in = din("x_in", [12, 128, 1024])
    cT_in = din("cT", [128, 8, 2])
    st_in = din("st_in", [2, 128, 8, 128])
    ck_in = din("ck_in", [2, 512, 256])
    cv_in = din("cv_in", [2, 512, 256])
    ada_w = din("ada_w", [4, 1024, 3072])
    ada_bT = din("ada_bT", [128, 4, 24])
    ada_bg = din("ada_bg", [4, 1024])
    npreT = din("npreT", [128, 4, 8])
    npost = din("npost", [4, 1024])
    rec_w_in = din("rec_w_in", [2, 1024, 3584])
    rec_w_out = din("rec_w_out", [2, 1024, 1024])
    att_w_in = din("att_w_in", [2, 1024, 2560])
    att_w_out = din("att_w_out", [2, 1024, 1024])
    lbT_in = din("lbT", [128, 2, 2, 4])
    hnT_in = din("hnT", [128, 2])
    pool_w = din("pool_w", [2, 4, 128, 128])
    pscT_in = din("pscT", [128, 2, 4])
    qn_row = din("qn_row", [2, 128])
    kn_row = din("kn_row", [2, 128])
    ident_in = din("ident", [128, 128])
    ones_in = din("ones", [128, 128])
    mk_in = din("mk", [64, 2, 256])
    rmask_in = din("rmask", [128, 512])
    icnt_s = din("icnt_s", [4, 1024])
    icnt_p = din("icnt_p", [4, 256])
    cos_in = din("cos_t", [8, 128, 128])
    sin_in = din("sin_t", [8, 128, 128])

    y_out = dout("y_out", [12, 128, 1024])
    st_out = dout("st_out", [2, 2, 2, 4, 128, 128])
    ck_out = dout("ck_out", [2, 2, 256, 256])
    cv_out = dout("cv_out", [2, 2, 256, 256])
    dbg_out = {}

    def sb(name, shape, dt, stack=None):
        return (stack or es).enter_context(nc.sbuf_tensor("sb_" + name, list(shape), dt))

    S = Sched(nc)

    X = sb("X", [128, 12, 1024], F32)
    ring = sb("ring", [128, NRING, 8, 512], BF16)
    hT = sb("hT", [128, 8, 1024], BF16)
    GT = sb("GT", [128, 2, 1024], F32)
    SC = sb("SC", [128, 8, 520], F32)
    idb = sb("idb", [128, 128], BF16)
    onesb = sb("onesb", [128, 128], BF16)
    onesf = sb("onesf", [128, 128], F32)
    rmask = sb("rmask", [128, 512], F32)
    mkf = sb("mkf", [64, 2, 256], F32)
    cT = sb("cTs", [128, 8, 2], F32)
    scT = sb("scT", [128, 8, 2], BF16)
    sc_rep = sb("sc_rep", [128, 8, 2, 128], BF16)
    adabT = sb("adabT", [128, 4, 24], F32)
    npre = sb("npre", [128, 4, 8], F32)
    lbl = sb("lbl", [128, 2, 2, 4], F32)
    lb = sb("lb", [128, 2, 2, 4], F32)
    oml = sb("oml", [128, 2, 2, 4], F32)
    hn = sb("hn", [128, 2], F32)
    psc = sb("psc", [128, 2, 4], F32)
    poolw = sb("poolw", [128, 8, 128], BF16)
    modT2 = sb("modT", [128, 2, 16, 2], F32)
    Gcol2 = sb("Gcol", [128, 2, 8, 2], F32)
    small = sb("small", [128, 64], F32)
    PS = es.enter_context(nc.psum_tensor("PS", [128, 8, 512], F32))
    ARENA_BYTES = 80 * 1024
    arena = sb("arena", [128, ARENA_BYTES // 4], F32)
    arena_off = [0]

    def aalloc(shape, dt):
        esz = 2 if dt == BF16 else 4
        n = int(np.prod(shape[1:])) * esz
        n = (n + 63) // 64 * 64
        off = arena_off[0]
        assert off + n <= ARENA_BYTES, ("arena overflow", off, n)
        arena_off[0] = off + n
        ap = arena[0:shape[0], off // 4:(off + n) // 4]
        if dt == BF16:
            ap = ap.bitcast(BF16)
        ap = ap[:, 0:int(np.prod(shape[1:]))]
        if len(shape) == 3:
            ap = ap.rearrange("p (a b) -> p a b", b=shape[2])
        elif len(shape) == 4:
            ap = ap.rearrange("p (a b c) -> p a b c", b=shape[2], c=shape[3])
        return ap

    def psb(b):
        return PS[:, b, :].bitcast(BF16)

    def sc(i, n=512):
        return SC[:, i, 0:n]

    def scb(i, n=1024):
        return SC[:, i, :].bitcast(BF16)[:, 0:n]

    def sc2(i):
        return SC[:, 2 * i:2 * i + 2, :].rearrange("p a b -> p (a b)")

    def act(out, in_, func, reads, writes, **kw):
        S.op("act", lambda e: e.activation(out=out, in_=in_, func=func, **kw), reads, writes)

    def tt(eng, out, a, b, op, reads, writes):
        S.op(eng, lambda e: e.tensor_tensor(out, a, b, op), reads, writes)

    def ts(eng, out, a, s1, s2, op0, op1, reads, writes):
        if s2 is None:
            S.op(eng, lambda e: e.tensor_scalar(out, a, s1, None, op0), reads, writes)
        else:
            S.op(eng, lambda e: e.tensor_scalar(out, a, s1, s2, op0, op1), reads, writes)

    def stt(eng, out, in0, scalar, in1, op0, op1, reads, writes):
        S.op(eng, lambda e: e.scalar_tensor_tensor(out, in0, scalar, in1, op0, op1), reads, writes)

    def cp(eng, out, in_, reads, writes):
        if eng == "act":
            S.op("act", lambda e: e.activation(out=out, in_=in_, func=AF.Copy), reads, writes)
        else:
            S.op(eng, lambda e: e.tensor_copy(out, in_), reads, writes)

    def mm(out, lhsT, rhs, start, stop, reads, writes, signal):
        S.op("pe", lambda e: e.matmul(out, lhsT, rhs, start=start, stop=stop), reads, writes, signal=signal)

    def tr(out, in_, reads, writes, signal):
        n = in_.shape[0]
        S.op("pe", lambda e: e.transpose(out, in_, idb[0:n, 0:n]), list(reads) + ["idb"], writes, signal=signal)

    def dbg(name, ap, shape, reads, dt=F32):
        if name not in dbg_names:
            return
        d = nc.dram_tensor("dbg_" + name, list(shape), dt, kind="ExternalOutput").ap()
        dbg_out[name] = d
        S.dma("sp", lambda e: e.dma_start(out=d, in_=ap), reads=reads)

    bank_rr = {"gen": 0, "tr": 0}

    def gen_bank():
        b = bank_rr["gen"]
        bank_rr["gen"] ^= 1
        return b

    def tr_bank():
        b = 2 + bank_rr["tr"]
        bank_rr["tr"] ^= 1
        return b

    wlist = list(wlist_in) if wlist_in is not None else []
    wcollect = []
    wstate = {"issued": 0, "next": 0}

    def w_issue_upto(k):
        while wstate["issued"] <= k and wstate["issued"] < len(wlist):
            i = wstate["issued"]
            wap, c0, ncol = wlist[i]
            slot = i % NRING
            src = wap[:, c0:c0 + ncol].rearrange("(c p) n -> p c n", p=128)
            S.dma("pool", lambda e, slot=slot, src=src, ncol=ncol: e.dma_start(out=ring[:, slot, :, 0:ncol], in_=src),
                  writes=[("ring", slot)])
            wstate["issued"] += 1

    def w_get(wkey, c0, ahead=NRING - 1):
        k = wstate["next"]
        wstate["next"] += 1
        wcollect.append((wkey, c0))
        if wlist_in is not None:
            assert wlist_in[k][3] == (wkey, c0), ("weight stream order mismatch", k, wlist_in[k][3], (wkey, c0))
        w_issue_upto(k + ahead)
        return k % NRING

    WSRC = {}
    for l_ in range(4):
        WSRC[("ada", l_)] = ada_w[l_]
    for j_ in range(2):
        WSRC[("rin", j_)] = rec_w_in[j_]
        WSRC[("rout", j_)] = rec_w_out[j_]
        WSRC[("ain", j_)] = att_w_in[j_]
        WSRC[("aout", j_)] = att_w_out[j_]
    if wlist_in is not None:
        wlist = [(WSRC[wk], c0, 512) for (wk, c0) in wlist_in]
        wlist_in = [(WSRC[wk], c0, 512, (wk, c0)) for (wk, c0) in wlist_in]

    for t in range(12):
        S.dma("sp", lambda e, t=t: e.dma_start(out=X[:, t, :], in_=x_in[t]), writes=[("X", t)])
    S.dma("pool", lambda e: e.dma_start(out=idb[:], in_=ident_in), writes=["idb"])
    S.dma("pool", lambda e: e.dma_start(out=onesb[:], in_=ones_in), writes=["onesb"])
    S.dma("sp", lambda e: e.dma_start(out=onesf[:], in_=ones_in), writes=["onesf"])
    S.dma("pool", lambda e: e.dma_start(out=poolw[:], in_=pool_w.rearrange("j g c d -> c (j g) d")), writes=["poolw"])
    S.dma("sp", lambda e: e.dma_start(out=rmask[:], in_=rmask_in), writes=["rmask"])
    S.dma("sp", lambda e: e.dma_start(out=mkf[:], in_=mk_in), writes=["mkf"])
    S.dma("sp", lambda e: e.dma_start(out=cT[:], in_=cT_in), writes=["cT"])
    S.dma("sp", lambda e: e.dma_start(out=adabT[:], in_=ada_bT), writes=["adabT"])
    S.dma("sp", lambda e: e.dma_start(out=npre[:], in_=npreT), writes=["npre"])
    S.dma("sp", lambda e: e.dma_start(out=lbl[:], in_=lbT_in), writes=["lbl"])
    S.dma("sp", lambda e: e.dma_start(out=hn[:], in_=hnT_in), writes=["hn"])
    S.dma("sp", lambda e: e.dma_start(out=psc[:], in_=pscT_in), writes=["psc"])
    w_issue_upto(NRING - 2)

    act(scT[:], cT[:], AF.Silu, ["cT"], ["scT"])
    cp("dve", sc_rep[:], scT[:].unsqueeze(3).to_broadcast([128, 8, 2, 128]), ["scT"], ["sc_rep"])
    S.op("dve", lambda e: e.memset(lb[:, 0, :, :], 0.0), [], ["lb0"])
    tt("dve", small[:, 0:8], lbl[:, 1, :, :].rearrange("p a b -> p (a b)"),
       lbl[:, 0, :, :].rearrange("p a b -> p (a b)"), ALU.subtract, ["lbl"], ["small"])
    act(lb[:, 1, :, :].rearrange("p a b -> p (a b)"), small[:, 0:8], AF.Sigmoid, ["small"], ["lb1"])
    ts("dve", oml[:].rearrange("p j a b -> p (j a b)"), lb[:].rearrange("p j a b -> p (j a b)"),
       -1.0, 1.0, ALU.mult, ALU.add, ["lb0", "lb1"], ["oml"])
    LBK = ["lb0", "lb1", "oml"]

    deferred = []

    def run_deferred(bk, n=1):
        for _ in range(n):
            if deferred:
                deferred.pop(0)(bk)

    def queue_mod_ss(l):
        modT = modT2[:, l % 2]
        Gcol = Gcol2[:, l % 2]

        def blk(b):
            def f(bk):
                slot = w_get(("ada", l), b * 512)
                for n in range(4):
                    for kc in range(8):
                        mm(PS[:, bk, n * 2:n * 2 + 2], ring[:, slot, kc, n * 128:(n + 1) * 128], scT[:, kc, :],
                           kc == 0, kc == 7, [("ring", slot), "scT"], [("ps", bk)], signal=(kc == 7 and n == 3))
                tt("dve", modT[:, 4 * b:4 * b + 4, :], PS[:, bk, 0:8].rearrange("p (c v) -> p c v", v=2),
                   adabT[:, l, 4 * b:4 * b + 4].unsqueeze(2).to_broadcast([128, 4, 2]), ALU.add,
                   [("ps", bk), "adabT"], [("modT", l % 2, b)])
                if b == 3:
                    stt("dve", Gcol, modT[:, 8:16, :], 1.0, npre[:, l, :].unsqueeze(2).to_broadcast([128, 8, 2]),
                        ALU.add, ALU.mult, [("modT", l % 2, 2), ("modT", l % 2, 3), "npre"], [("Gcol", l % 2)])
            return f
        for b in range(4):
            deferred.append(blk(b))

    def queue_mod_gate(l, v):
        def blk(b):
            def f(bk):
                if b == 0:
                    S.dma("sp", lambda e: e.dma_start(out=sc2(2)[:, 0:1024],
                                                      in_=ada_bg[l:l + 1, :].partition_broadcast(128)),
                          writes=[("sc", 4), ("sc", 5)])
                    S.dma("sp", lambda e: e.dma_start(out=sc2(3)[:, 0:1024],
                                                      in_=npost[l:l + 1, :].partition_broadcast(128)),
                          writes=[("sc", 6), ("sc", 7)])
                slot = w_get(("ada", l), (4 + b) * 512)
                for kc in range(8):
                    mm(PS[:, bk, :], sc_rep[:, kc, v, :], ring[:, slot, kc, :], kc == 0, kc == 7,
                       [("ring", slot), "sc_rep"], [("ps", bk)], signal=(kc == 7))
                tt("dve", GT[:, v, b * 512:(b + 1) * 512], PS[:, bk, :], sc2(2)[:, b * 512:(b + 1) * 512], ALU.add,
                   [("ps", bk), ("sc", 4), ("sc", 5)], [("GT", v, b)])
                tt("dve", GT[:, v, b * 512:(b + 1) * 512], GT[:, v, b * 512:(b + 1) * 512],
                   sc2(3)[:, b * 512:(b + 1) * 512], ALU.mult,
                   [("GT", v, b), ("sc", 6), ("sc", 7)], [("GT", v, b)])
            return f
        for b in range(2):
            deferred.append(blk(b))

    def queue_pass_mod(l, v):
        if v == 1 and l + 1 < nlayers:
            queue_mod_ss(l + 1)
        queue_mod_gate(l, v)

    def prenorm(l, tile0, ngroups, v):
        modT = modT2[:, l % 2]
        Gcol = Gcol2[:, l % 2]
        MK = [("Gcol", l % 2), ("modT", l % 2, 0), ("modT", l % 2, 1)]
        for g in range(ngroups):
            for jt in range(4):
                t = tile0 + g * 4 + jt
                ssq = small[:, 16 + jt:17 + jt]
                rs = small[:, 20 + jt:21 + jt]
                act(scb(4 + jt), X[:, t, :], AF.Square, [("X", t)], [("sc", 4 + jt), ("ssq", jt)], accum_out=ssq)
                act(rs, ssq, AF.Sqrt, [("ssq", jt)], [("rs", jt)], scale=1.0 / 1024, bias=EPS)
                S.op("dve", lambda e, rs=rs: e.reciprocal(rs, rs), [("rs", jt)], [("rs", jt)])
                ts("dve", scb(jt), X[:, t, :], rs, None, ALU.mult, None, [("X", t), ("rs", jt)], [("sc", jt)])
            for cpair in range(4):
                bk = tr_bank()
                for cc in range(2):
                    c = 2 * cpair + cc
                    for jt in range(4):
                        tr(psb(bk)[:, cc * 512 + jt * 128: cc * 512 + (jt + 1) * 128],
                           scb(jt)[:, c * 128:(c + 1) * 128], [("sc", jt)], [("ps", bk)],
                           signal=(cc == 1 and jt == 3))
                for cc in range(2):
                    c = 2 * cpair + cc
                    dst = hT[:, c, g * 512:(g + 1) * 512]
                    src = psb(bk)[:, cc * 512:(cc + 1) * 512]
                    if cpair % 2 == 0:
                        act(dst, src, AF.Identity, [("ps", bk)] + MK, [("hT", g, c)],
                            scale=Gcol[:, c, v:v + 1], bias=modT[:, c, v:v + 1])
                    else:
                        ts("dve", dst, src, Gcol[:, c, v:v + 1], modT[:, c, v:v + 1], ALU.mult, ALU.add,
                           [("ps", bk)] + MK, [("hT", g, c)])

    def hT_keys(g):
        return [("hT", g, c) for c in range(8)]

    def proj_fm(slot, n, g, bk):
        for kc in range(8):
            mm(PS[:, bk, :], ring[:, slot, kc, n * 128:(n + 1) * 128], hT[:, kc, g * 512:(g + 1) * 512],
               kc == 0, kc == 7, [("ring", slot)] + hT_keys(g), [("ps", bk)], signal=(kc == 7))

    def proj_tm(slot, tok0, ntok, bk):
        g = tok0 // 512
        for kc in range(8):
            mm(PS[0:ntok, bk, :], hT[:, kc, tok0:tok0 + ntok], ring[:, slot, kc, :],
               kc == 0, kc == 7, [("ring", slot)] + hT_keys(g), [("ps", bk)], signal=(kc == 7))

    def out_proj_and_residual(l, tile0, ntiles, v, zin_fn, zin_keys_fn, wkey):
        while deferred:
            run_deferred(gen_bank())
        s0 = w_get(wkey, 0)
        s1 = w_get(wkey, 512, NRING - 2)
        slots = (s0, s1)
        for it in range(ntiles):
            t = tile0 + it
            b0 = 4 + 2 * (it % 2)
            for nb in range(2):
                for kc in range(8):
                    mm(PS[:, b0 + nb, :], zin_fn(kc, it), ring[:, slots[nb], kc, :], kc == 0, kc == 7,
                       [("ring", slots[nb])] + zin_keys_fn(kc, it), [("ps", b0 + nb)], signal=(kc == 7))
            po = PS[:, b0:b0 + 2, :].rearrange("p a b -> p (a b)")
            pk = [("ps", b0), ("ps", b0 + 1)]
            ssq = small[:, 24 + (it % 2):25 + (it % 2)]
            rs = small[:, 26 + (it % 2):27 + (it % 2)]
            jk = 6 + (it % 2)
            act(scb(jk), po, AF.Square, pk, [("sc", jk), ("ssq2", it % 2)], accum_out=ssq)
            act(rs, ssq, AF.Sqrt, [("ssq2", it % 2)], [("rs2", it % 2)], scale=1.0 / 1024, bias=EPS)
            S.op("dve", lambda e, rs=rs: e.reciprocal(rs, rs), [("rs2", it % 2)], [("rs2", it % 2)])
            tmpk = 2 * (it % 2)
            tmp = sc2(it % 2)[:, 0:1024]
            stt("dve", tmp, po, rs, GT[:, v, :], ALU.mult, ALU.mult,
                pk + [("rs2", it % 2), ("GT", v, 0), ("GT", v, 1)], [("sc", tmpk), ("sc", tmpk + 1)])
            tt("dve", X[:, t, :], X[:, t, :], tmp, ALU.add,
               [("X", t), ("sc", tmpk), ("sc", tmpk + 1)], [("X", t)])

    def rec_pass(l, j, tile0, ntiles, v, seqs, is_sample):
        L = ntiles * 128
        G = L // 512
        nch = L // CH
        arena_off[0] = 0
        qo = aalloc([128, 4, L], BF16)
        qp = [aalloc([128, 4, L], BF16) for r in range(2)]
        kp = [aalloc([128, 4, L], BF16) for r in range(2)]
        v_c = aalloc([64, nch, 512], BF16)
        ypool = aalloc([128, 4, L], BF16)
        tabS = aalloc([128, 2, 4, nch], F32)
        tabG = aalloc([128, 2, 4, nch], F32)
        tabA = aalloc([128, 2, 4, nch], F32)
        St = aalloc([128, 8, 128], F32)
        Stb = aalloc([128, 8, 128], BF16)
        kTt = aalloc([64, 8, 128], BF16)
        ATm = aalloc([64, 8, 64], BF16)
        Lseq = seqs[0][1]
        icnt = aalloc([128, Lseq], F32)

        if stop <= 1:
            return
        prenorm(l, tile0, G, v)
        src_ic = icnt_s if is_sample else icnt_p
        if stop <= 2:
            return

        queue_pass_mod(l, v)
        slot = w_get(("rin", j), 1024)
        for n in range(4):
            for g in range(G):
                bk = gen_bank()
                proj_fm(slot, n, g, bk)
                act(qo[:, n, g * 512:(g + 1) * 512], PS[:, bk, :], AF.Silu, [("ps", bk)], [("qo", n, g)])

        f_items = [(r, n, g) for r in range(2) for n in range(4) for g in range(G)]
        f_slots = {}

        def f_ctx(idx):
            r, n, g = f_items[idx]
            o4 = 4 * (idx % 2)
            Ts = [sc(o4 + i) for i in range(4)]
            Ks = [("sc", o4 + i) for i in range(4)]
            return r, n, g, Ts, Ks

        def f_s0(idx):
            r, n, g, (T1, T2, T3, T4), (K1, K2, K3, K4) = f_ctx(idx)
            if r not in f_slots:
                f_slots[r] = w_get(("rin", j), 1536 + 512 * r)
            bk = gen_bank()
            proj_fm(f_slots[r], n, g, bk)
            act(T1, PS[:, bk, :], AF.Exp, [("ps", bk)], [K1], scale=-1.0)

        def f_s1(idx):
            r, n, g, (T1, T2, T3, T4), (K1, K2, K3, K4) = f_ctx(idx)
            act(T1, T1, AF.Ln, [K1], [K1], bias=1.0)
            act(T1, T1, AF.Exp, [K1], [K1], scale=-1.0)
            ts("dve", T1, T1, oml[:, j, r, n:n + 1], lb[:, j, r, n:n + 1], ALU.mult, ALU.add, [K1] + LBK, [K1])
            ts("dve", T1, T1, F_MIN, 1.0, ALU.max, ALU.min, [K1], [K1])
            act(T2, T1, AF.Ln, [K1], [K2])
            ts("pool", T3, T1, -1.0, 1.0, ALU.mult, ALU.add, [K1], [K3])

        def f_s2(idx):
            r, n, g, (T1, T2, T3, T4), (K1, K2, K3, K4) = f_ctx(idx)
            S.op("dve", lambda e: e.tensor_tensor_scan(T4, rmask[:], T2, 0.0, ALU.mult, ALU.add),
                 [K2, "rmask"], [K4])
            b3 = T4.rearrange("p (c t) -> p c t", t=CH)
            c0 = g * 8
            tk = ("tab", r, n, g)
            smt = small[:, 32 + 8 * (idx % 2):40 + 8 * (idx % 2)]
            smk = ("smt", idx % 2)
            tt("dve", smt, b3[:, :, CH - 1], b3[:, :, CH // 2 - 1], ALU.subtract, [K4], [smk])
            e_mid = tabS if r == 0 else tabG
            e_dif = tabG if r == 0 else tabS
            act(e_mid[:, r, n, c0:c0 + 8], b3[:, :, CH // 2 - 1], AF.Exp, [K4], [tk + (0,)],
                bias=(-CSH if r == 0 else CSH))
            act(e_dif[:, r, n, c0:c0 + 8], smt, AF.Exp, [smk], [tk + (1,)], bias=(CSH if r == 0 else -CSH))
            act(tabA[:, r, n, c0:c0 + 8], b3[:, :, CH - 1], AF.Exp, [K4], [tk + (2,)])
            tt("pool", T1.rearrange("p (c t) -> p c t", t=CH), b3,
               b3[:, :, CH // 2 - 1:CH // 2].to_broadcast([128, 8, CH]), ALU.subtract, [K4, K1], [K1])
            if r == 1:
                tt("dve", T1, T1, T2, ALU.subtract, [K1, K2], [K1])

        def f_s3(idx):
            r, n, g, (T1, T2, T3, T4), (K1, K2, K3, K4) = f_ctx(idx)
            sg = 1.0 if r == 0 else -1.0
            act(T2, T1, AF.Exp, [K1], [K2], scale=sg, bias=-CSH)
            act(T4, T1, AF.Exp, [K1], [K4], scale=-sg, bias=-CSH)
            tt("dve", qp[r][:, n, g * 512:(g + 1) * 512], qo[:, n, g * 512:(g + 1) * 512], T2, ALU.mult,
               [("qo", n, g), K2], [("qp", r, n, g)])
            tt("pool", kp[r][:, n, g * 512:(g + 1) * 512], T3, T4, ALU.mult, [K3, K4], [("kp", r, n, g)])

        NF = len(f_items)
        for idx in range(NF + 1):
            if idx < NF:
                f_s0(idx)
            if idx >= 1:
                f_s2(idx - 1)
            if idx < NF:
                f_s1(idx)
            if idx >= 1:
                f_s3(idx - 1)

        tg = "_%d_%d" % (l, int(is_sample))
        allk = lambda nm, *pre: [(nm,) + pre + (n, g) for n in range(4) for g in range(G)]
        dbg("hT" + tg, hT[:, :, 0:L], [128, 8, L], [("hT", g, c) for g in range(G) for c in range(8)], BF16)
        dbg("qp0" + tg, qp[0], [128, 4, L], allk("qp", 0), BF16)
        dbg("kp0" + tg, kp[0], [128, 4, L], allk("kp", 0), BF16)
        dbg("qp1" + tg, qp[1], [128, 4, L], allk("qp", 1), BF16)
        dbg("kp1" + tg, kp[1], [128, 4, L], allk("kp", 1), BF16)
        dbg("tabS" + tg, tabS, [128, 2, 4, nch], [("tab", r, n, g, k) for r in range(2) for n in range(4) for g in range(G) for k in range(3)])
        dbg("tabG" + tg, tabG, [128, 2, 4, nch], [("tab", r, n, g, k) for r in range(2) for n in range(4) for g in range(G) for k in range(3)])
        dbg("tabA" + tg, tabA, [128, 2, 4, nch], [("tab", r, n, g, k) for r in range(2) for n in range(4) for g in range(G) for k in range(3)])
        if stop <= 3:
            return
        slot = w_get(("rin", j), 2560)
        for c in range(nch):
            bk = gen_bank()
            proj_tm(slot, c * CH, CH, bk)
            cp("act" if c % 2 == 0 else "dve", v_c[:, c, :], PS[0:64, bk, :], [("ps", bk)], [("v_c", c)])

        slot = w_get(("rin", j), 512)
        for n in range(4):
            for g in range(G):
                bk = gen_bank()
                proj_fm(slot, n, g, bk)
                act(ypool[:, n, g * 512:(g + 1) * 512], PS[:, bk, :], AF.Silu, [("ps", bk)], [("yp", n, g)])

        slot = w_get(("rin", j), 0)
        nseq = len(seqs)
        Wd = nseq * (Lseq + 16)
        for n in range(4):
            win = (2, 4, 8, 16)[n]
            UB, WA, WB = sc2(0), sc2(1), sc2(2)
            DB = sc2(3).bitcast(BF16)
            KU, KA, KB, KD = [[("sc", 2 * i), ("sc", 2 * i + 1)] for i in range(4)]
            S.dma("sp", lambda e, n=n: e.dma_start(out=icnt, in_=src_ic[n:n + 1, :].partition_broadcast(128)),
                  writes=["icnt"])
            S.op("pool", lambda e, UB=UB: e.memset(UB[:, 0:Wd], 0.0), [], KU)
            for g in range(G):
                bk = gen_bank()
                proj_fm(slot, n, g, bk)
                for k, (s0, Ls) in enumerate(seqs):
                    a = max(s0, g * 512)
                    b = min(s0 + Ls, (g + 1) * 512)
                    if a >= b:
                        continue
                    dst0 = k * (Ls + 16) + 8 + (a - s0)
                    cp("act", UB[:, dst0:dst0 + (b - a)], PS[:, bk, a - g * 512:b - g * 512], [("ps", bk)], KU)
            tt("dve", WA[:, 1:Wd], UB[:, 0:Wd - 1], UB[:, 1:Wd], ALU.add, KU, KA)
            cur, curk, oth, othk = WA, KA, WB, KB
            lo, hi = 1, Wd
            if win >= 4:
                tt("dve", oth[:, lo + 1:hi - 1], cur[:, lo:hi - 2], cur[:, lo + 2:hi], ALU.add, curk, othk)
                cur, curk, oth, othk = oth, othk, cur, curk
                lo, hi = lo + 1, hi - 1
            if win >= 8:
                tt("dve", oth[:, lo + 2:hi - 2], cur[:, lo:hi - 4], cur[:, lo + 4:hi], ALU.add, curk, othk)
                cur, curk, oth, othk = oth, othk, cur, curk
                lo, hi = lo + 2, hi - 2
            if win >= 16:
                tt("dve", oth[:, lo + 4:hi - 4], cur[:, lo:hi - 8], cur[:, lo + 8:hi], ALU.add, curk, othk)
                cur, curk, oth, othk = oth, othk, cur, curk
                lo, hi = lo + 4, hi - 4
            assert lo <= 8 and hi >= Wd - 8
            for k, (s0, Ls) in enumerate(seqs):
                base = k * (Ls + 16) + 8
                tt("dve", oth[:, base:base + Ls], cur[:, base:base + Ls], icnt, ALU.mult, curk + ["icnt"], othk)
                tt("dve", DB[:, s0:s0 + Ls], oth[:, base:base + Ls], UB[:, base:base + Ls], ALU.subtract,
                   othk + KU, KD)
            for g in range(G):
                bk = gen_bank()
                mm(PS[:, bk, :], poolw[:, j * 4 + n, :], DB[:, g * 512:(g + 1) * 512], True, True,
                   KD + ["poolw"], [("ps", bk)], signal=True)
                yv = ypool[:, n, g * 512:(g + 1) * 512]
                stt("dve", yv, PS[:, bk, :], psc[:, j, n:n + 1], yv, ALU.mult, ALU.mult,
                    [("ps", bk), "psc", ("yp", n, g)], [("yp", n, g)])

        dbg("vc" + tg, v_c, [64, nch, 512], [("v_c", c) for c in range(nch)], BF16)
        dbg("yp" + tg, ypool, [128, 4, L], allk("yp"), BF16)
        if stop <= 4:
            return
        if is_sample:
            ic_bf = icnt.bitcast(BF16)
            kT2 = [kTt, ic_bf[0:64, 0:1024].rearrange("p (a b) -> p a b", b=128)]
            AT2 = [ATm, ic_bf[0:64, 1024:1536].rearrange("p (a b) -> p a b", b=64)]
            alias_k = ["icnt"]
        else:
            kT2 = [kTt, aalloc([64, 8, 128], BF16)]
            AT2 = [ATm, aalloc([64, 8, 64], BF16)]
            alias_k = []
        for bsel in range(2):
            S.op("dve", lambda e, bsel=bsel: e.memset(AT2[bsel], 0.0), [],
                 [("ATm", bsel, 0), ("ATm", bsel, 1)] + alias_k)
        ctxs = []
        for si, (s0, Ls) in enumerate(seqs):
            if si == 0:
                St_s, Stb_s = St, Stb
            else:
                St_s, Stb_s = aalloc([128, 8, 128], F32), aalloc([128, 8, 128], BF16)
            ctxs.append(dict(si=si, s0=s0, nst=Ls // CH, cb=s0 // CH, St=St_s, Stb=Stb_s))
            if is_sample:
                S.dma("sp", lambda e: e.dma_start(out=St_s, in_=st_in[j]), writes=[("St", si)])
            else:
                S.op("dve", lambda e, St_s=St_s: e.memset(St_s, 0.0), [], [("St", si)])
        gsteps = []
        for i in range(max(c["nst"] for c in ctxs)):
            for c in ctxs:
                if i < c["nst"]:
                    gsteps.append((c, i))

        def geo(c, i):
            cr = (c["cb"] + i, c["cb"] + c["nst"] - 1 - i)
            t0 = (cr[0] * CH, cr[1] * CH)
            gq = (t0[0] // 512, t0[1] // 512)
            return cr, t0, gq

        def scan_p(g):
            c, i = gsteps[g]
            cr, t0, gq = geo(c, i)
            bsel = g % 2
            kTb, ATb = kT2[bsel], AT2[bsel]
            bT = 2 + bsel
            for r in range(2):
                for n in range(4):
                    tr(psb(bT)[0:64, (4 * r + n) * 128:(4 * r + n + 1) * 128], kp[r][:, n, t0[r]:t0[r] + CH],
                       [("kp", r, n, gq[r])], [("ps", bT)], signal=(r == 1 and n == 3))
            cp("act", kTb.rearrange("p a b -> p (a b)"), psb(bT)[0:64, :], [("ps", bT)],
               [("kTt", bsel)] + alias_k)
            bA = (0, 5)[bsel]
            for r in range(2):
                for n in range(4):
                    mm(PS[0:64, bA, (4 * r + n) * 64:(4 * r + n + 1) * 64], kp[r][:, n, t0[r]:t0[r] + CH],
                       qp[r][:, n, t0[r]:t0[r] + CH], True, True,
                       [("kp", r, n, gq[r]), ("qp", r, n, gq[r])], [("ps", bA)], signal=(r == 1 and n == 3))
            for r in range(2):
                mku = mkf[:, r, :].bitcast(mybir.dt.uint32)
                S.op("dve", lambda e, r=r, mku=mku, bA=bA, ATb=ATb: e.copy_predicated(
                    ATb[:, 4 * r:4 * r + 4, :].rearrange("p a b -> p (a b)"), mku,
                    PS[0:64, bA, 256 * r:256 * r + 256]),
                    [("ps", bA), "mkf"], [("ATm", bsel, r)] + alias_k)

        def scan_q(g):
            c, i = gsteps[g]
            cr, t0, gq = geo(c, i)
            bsel = g % 2
            kTb, ATb = kT2[bsel], AT2[bsel]
            St_s, Stb_s, si = c["St"], c["Stb"], c["si"]
            SK = ("St", si)
            first = i < c["nst"] // 2
            tabk = lambda r, kind: [("tab", r, n, gq[r], kind) for n in range(4)]
            if deferred:
                run_deferred(1)
            for r in range(2):
                for n in range(4):
                    mm(PS[:, 6 + r, n * 128:(n + 1) * 128], kTb[:, 4 * r + n, :],
                       v_c[:, cr[r], n * 128:(n + 1) * 128], True, True,
                       [("kTt", bsel), ("v_c", cr[r])], [("ps", 6 + r)], signal=(n == 3))
            ts0 = 2 * bsel
            tmpKV = SC[:, ts0:ts0 + 2, 0:512].rearrange("p a (h v) -> p a h v", v=128)
            for r in range(2):
                for n in range(4):
                    act(tmpKV[:, r, n, :], PS[:, 6 + r, n * 128:(n + 1) * 128], AF.Identity,
                        [("ps", 6 + r)] + tabk(r, 0) + tabk(r, 1), [("sc", ts0 + r, n)],
                        scale=tabG[:, r, n, cr[r]:cr[r] + 1])
            for r in range(2):
                tt("dve", Stb_s[:, 4 * r:4 * r + 4, :], St_s[:, 4 * r:4 * r + 4, :],
                   tabS[:, r, :, cr[r]:cr[r] + 1].to_broadcast([128, 4, 128]), ALU.mult,
                   [SK] + tabk(r, 0) + tabk(r, 1), [("Stb", si, r)])
            for r in range(2):
                tt("dve", St_s[:, 4 * r:4 * r + 4, :], St_s[:, 4 * r:4 * r + 4, :],
                   tabA[:, r, :, cr[r]:cr[r] + 1].to_broadcast([128, 4, 128]), ALU.mult,
                   [SK] + tabk(r, 2), [SK])
            St4 = St_s.rearrange("p (a h) v -> p a h v", a=2)
            tt("dve", St4, St4, tmpKV, ALU.add,
               [SK] + [("sc", ts0 + r, n) for r in range(2) for n in range(4)],
               [SK, ("sc", ts0), ("sc", ts0 + 1)])
            bO = 4
            for r in range(2):
                for n in range(4):
                    o_ap = PS[:, bO, (4 * r + n) * 64:(4 * r + n + 1) * 64]
                    mm(o_ap, v_c[:, cr[r], n * 128:(n + 1) * 128], ATb[:, 4 * r + n, :], True, False,
                       [("v_c", cr[r]), ("ATm", bsel, r)], [("ps", bO)], signal=False)
                    mm(o_ap, Stb_s[:, 4 * r + n, :], qp[r][:, n, t0[r]:t0[r] + CH], False, True,
                       [("Stb", si, r), ("qp", r, n, gq[r])], [("ps", bO)], signal=(r == 1 and n == 3))
            for r in range(2):
                dst = qo[:, :, t0[r]:t0[r] + CH]
                src = PS[:, bO, 256 * r:256 * r + 256].rearrange("p (a b) -> p a b", b=64)
                ok = [("qo", n, gq[r]) for n in range(4)]
                E2 = float(np.exp(2.0 * CSH))
                if first:
                    act(dst, src, AF.Identity, [("ps", bO)], ok, scale=E2)
                else:
                    stt("dve", dst, src, E2, dst, ALU.mult, ALU.add, [("ps", bO)] + ok, ok)

        S.op("dve", lambda e: e.memset(SC[:, 0:4, 0:8], 0.0), [], [("sc", q) for q in range(4)] +
             [("sc", q, n) for q in range(4) for n in range(4)])
        scan_p(0)
        for g in range(len(gsteps)):
            if g + 1 < len(gsteps):
                scan_p(g + 1)
            scan_q(g)
        if not is_sample:
            for c in ctxs:
                dst = st_out[c["si"], j].rearrange("r h d v -> d (r h) v")
                S.dma("sp", lambda e, dst=dst, St_s=c["St"]: e.dma_start(out=dst, in_=St_s), reads=[("St", c["si"])])

        dbg("o" + tg, qo, [128, 4, L], allk("qo"), BF16)
        if stop <= 5:
            return
        ctr = 0
        for n in range(4):
            for g in range(G):
                o4 = 4 * (ctr % 2)
                ctr += 1
                T1, T2 = sc(o4), sc(o4 + 1)
                K1, K2 = ("sc", o4), ("sc", o4 + 1)
                ov = qo[:, n, g * 512:(g + 1) * 512]
                sqb = scb(o4 + 2, 512)
                act(sqb, ov, AF.Square, [("qo", n, g)], [("sc", o4 + 2)])
                bk = gen_bank()
                mm(PS[:, bk, :], onesb[:], sqb, True, True, [("sc", o4 + 2), "onesb"], [("ps", bk)], signal=True)
                act(T1, PS[:, bk, :], AF.Sqrt, [("ps", bk)], [K1], scale=1.0 / 128, bias=EPS)
                S.op("dve", lambda e, T1=T1: e.reciprocal(T1, T1), [K1], [K1])
                stt("dve", qp[0][:, n, g * 512:(g + 1) * 512], ov, hn[:, j:j + 1], T1, ALU.mult, ALU.mult,
                    [("qo", n, g), "hn", K1, ("qp", 0, n, g)], [("qp", 0, n, g)])

        slot = w_get(("rin", j), 3072)
        ctr = 0
        for n in range(4):
            for g in range(G):
                bk = gen_bank()
                proj_fm(slot, n, g, bk)
                o4 = 4 * (ctr % 2) + 3
                ctr += 1
                sgt = scb(o4, 512)
                act(sgt, PS[:, bk, :], AF.Silu, [("ps", bk)], [("sc", o4)])
                zv = qp[0][:, n, g * 512:(g + 1) * 512]
                tt("dve", zv, zv, sgt, ALU.mult, [("qp", 0, n, g), ("sc", o4)], [("qp", 0, n, g)])
        dbg("z" + tg, qp[0], [128, 4, L], allk("qp", 0), BF16)
        dbg("zin%d_%d" % (l, int(is_sample)), ypool, [128, 4, L], [("yp", n, g) for n in range(4) for g in range(G)])

        def zin_fn(kc, it):
            buf = ypool if kc < 4 else qp[0]
            return buf[:, kc % 4, it * 128:(it + 1) * 128]

        def zin_keys(kc, it):
            g = (it * 128) // 512
            return [("yp", kc, g)] if kc < 4 else [("qp", 0, kc - 4, g)]

        out_proj_and_residual(l, tile0, ntiles, v, zin_fn, zin_keys, ("rout", j))
        S.barrier()

    def att_pass(l, j, tile0, ntiles, v, seqs, is_sample):
        L = ntiles * 128
        G = L // 512
        nkt_cache = 4 if is_sample else 0
        arena_off[0] = 0
        QT = aalloc([128, 8, L], BF16)
        KT = aalloc([128, 2, 512 + L], BF16)
        Vt = aalloc([128, nkt_cache + ntiles, 256], BF16)
        gT = aalloc([128, 8, L], BF16)
        pT = aalloc([128, 3, 512], BF16)
        gq = aalloc([128, 128], F32)
        gk = aalloc([128, 128], F32)
        stg = aalloc([128, 2, 512], F32)
        cosT = aalloc([128, 8, 128], F32)
        sinT = aalloc([128, 8, 128], F32)
        ckt = aalloc([128, 4, 256], BF16)
        if is_sample:
            CGq = aalloc([128, 8, 128], F32)
            SGq = aalloc([128, 8, 128], F32)
            CGk = aalloc([128, 8, 128], F32)
            SGk = aalloc([128, 8, 128], F32)

        S.dma("sp", lambda e: e.dma_start(out=gq, in_=qn_row[j:j + 1, :].partition_broadcast(128)), writes=["gq"])
        S.dma("sp", lambda e: e.dma_start(out=gk, in_=kn_row[j:j + 1, :].partition_broadcast(128)), writes=["gk"])
        if is_sample:
            S.dma("sp", lambda e: e.dma_start(out=cosT, in_=cos_in.rearrange("t p f -> p t f")), writes=["cosT"])
            S.dma("sp", lambda e: e.dma_start(out=sinT, in_=sin_in.rearrange("t p f -> p t f")), writes=["sinT"])
            S.dma("pool", lambda e: e.dma_start(out=ckt, in_=ck_in[j].rearrange("(t p) f -> p t f", p=128)),
                  writes=["ckt"])
            S.dma("pool", lambda e: e.dma_start(out=Vt[:, 0:4, :], in_=cv_in[j].rearrange("(t p) f -> p t f", p=128)),
                  writes=[("Vt", t) for t in range(4)])
        prenorm(l, tile0, G, v)
        if is_sample:
            for t in range(4):
                bk = tr_bank()
                for h in range(2):
                    tr(psb(bk)[:, h * 128:(h + 1) * 128], ckt[:, t, h * 128:(h + 1) * 128], ["ckt"], [("ps", bk)],
                       signal=(h == 1))
                cp("act", KT[:, :, t * 128:(t + 1) * 128], psb(bk)[:, 0:256].rearrange("p (h t) -> p h t", t=128),
                   [("ps", bk)], [("KT", t)])

        def normrope(src_ps, pskeys, nh, gain, it, dst_bf, dkeys, ctr):
            o4 = 4 * (ctr % 2)
            A, B, C = sc(o4, nh * 128), sc(o4 + 1, nh * 128), sc(o4 + 2, nh * 128)
            KA, KB, KC = ("sc", o4), ("sc", o4 + 1), ("sc", o4 + 2)
            ssq = small[:, 48 + 4 * (ctr % 2):48 + 4 * (ctr % 2) + nh]
            sk = ("ssq3", ctr % 2)
            v3 = lambda ap: ap.rearrange("p (h d) -> p h d", d=128)
            act(A, src_ps, AF.Square, pskeys, [KA])
            S.op("dve", lambda e: e.tensor_reduce(out=ssq, in_=v3(A), axis=AX.X, op=ALU.add), [KA], [sk])
            act(ssq, ssq, AF.Sqrt, [sk], [sk], scale=1.0 / 128, bias=EPS)
            S.op("dve", lambda e: e.reciprocal(ssq, ssq), [sk], [sk])
            tt("dve", v3(B), v3(src_ps), ssq.unsqueeze(2).to_broadcast([128, nh, 128]), ALU.mult,
               pskeys + [sk], [KB])
            gainb = gain.unsqueeze(1).to_broadcast([128, nh, 128])
            if not is_sample:
                tt("dve", v3(dst_bf), v3(B), gainb, ALU.mult, [KB, "gq", "gk"], dkeys)
                return B
            CG, SG = (CGq, SGq) if nh == 4 else (CGk, SGk)
            c2 = CG[:, it, :].unsqueeze(1).to_broadcast([128, nh, 128])
            tt("dve", v3(A), v3(B), c2, ALU.mult, [KB, "ropetab"], [KA])
            B4 = B.rearrange("p (h i two) -> p h i two", i=64, two=2)
            C4 = C.rearrange("p (h i two) -> p h i two", i=64, two=2)
            s4 = SG[:, it, :].rearrange("p (i two) -> p i two", two=2)
            for e_ in range(2):
                tt("pool" if e_ == 0 else "dve", C4[:, :, :, e_], B4[:, :, :, 1 - e_],
                   s4[:, :, e_].unsqueeze(1).to_broadcast([128, nh, 64]), ALU.mult, [KB, "ropetab", KC], [(KC[0], KC[1], e_)])
            tt("pool", dst_bf, A, C, ALU.add, [KA, (KC[0], KC[1], 0), (KC[0], KC[1], 1)], dkeys + [KC])
            return B

        if is_sample:
            for (CG, SG, gain) in ((CGq, SGq, gq), (CGk, SGk, gk)):
                gb8 = gain.unsqueeze(1).to_broadcast([128, 8, 128])
                tt("dve", CG, cosT, gb8, ALU.mult, ["cosT", "gq", "gk"], ["ropetab"])
                g2 = gain.rearrange("p (i two) -> p i two", two=2)
                S4 = SG.rearrange("p t (i two) -> p t i two", two=2)
                s8 = sinT.rearrange("p t (i two) -> p t i two", two=2)
                for e_ in range(2):
                    tt("dve", S4[:, :, :, e_], s8[:, :, :, e_],
                       g2[:, :, 1 - e_].unsqueeze(1).to_broadcast([128, 8, 64]), ALU.mult,
                       ["sinT", "gq", "gk"], ["ropetab"])
        queue_pass_mod(l, v)
        if stop == 11:
            return
        items = []
        for qb in range(2):
            for it in range(ntiles):
                items.append(("q", qb, it))
        for it in range(ntiles):
            items.append(("kv", 0, it))
        slots = {}
        pend = []

        def stage_a(ctr, item):
            kind, qb, it = item
            key = (kind, qb)
            if key not in slots:
                slots[key] = w_get(("ain", j), qb * 512 if kind == "q" else 1024)
            slot = slots[key]
            bk = gen_bank()
            proj_tm(slot, it * 128, 128, bk)
            pk = [("ps", bk)]
            dk = [("sc", 4 * (ctr % 2) + 3)]
            if kind == "q":
                qr = scb(4 * (ctr % 2) + 3, 512)
                normrope(PS[:, bk, :], pk, 4, gq, it, qr, dk, ctr)
                return (kind, qb, it, qr, dk, bk, None)
            kr = scb(4 * (ctr % 2) + 3, 256)
            Bn = normrope(PS[:, bk, 0:256], pk, 2, gk, it, kr, dk, ctr)
            kt_idx = nkt_cache + it
            cp("dve", Vt[:, kt_idx, :], PS[:, bk, 256:512], pk, [("Vt", kt_idx)])
            if not is_sample:
                sq = it % 2
                tt("dve", stg[:, sq, 0:256].rearrange("p (h d) -> p h d", d=128),
                   Bn.rearrange("p (h d) -> p h d", d=128), gk.unsqueeze(1).to_broadcast([128, 2, 128]), ALU.mult,
                   [("sc", 4 * (ctr % 2) + 1), "gk"], [("stg", sq, 0)])
                cp("dve", stg[:, sq, 256:512], PS[:, bk, 256:512], pk, [("stg", sq, 1)])
                si, tt0 = divmod(it * 128, 256)
                S.dma("sp", lambda e, si=si, tt0=tt0, sq=sq: e.dma_start(out=ck_out[si, j, tt0:tt0 + 128, :],
                                                                         in_=stg[:, sq, 0:256]), reads=[("stg", sq, 0)])
                S.dma("sp", lambda e, si=si, tt0=tt0, sq=sq: e.dma_start(out=cv_out[si, j, tt0:tt0 + 128, :],
                                                                         in_=stg[:, sq, 256:512]), reads=[("stg", sq, 1)])
            return (kind, qb, it, kr, dk, bk, kt_idx)

        def stage_b(st):
            kind, qb, it, rr, dk, bk, kt_idx = st
            bt = tr_bank()
            if kind == "q":
                for h in range(4):
                    tr(psb(bt)[:, h * 128:(h + 1) * 128], rr[:, h * 128:(h + 1) * 128], dk, [("ps", bt)],
                       signal=(h == 3))
                cp("act", QT[:, qb * 4:qb * 4 + 4, it * 128:(it + 1) * 128],
                   psb(bt)[:, 0:512].rearrange("p (h t) -> p h t", t=128), [("ps", bt)], [("QT", qb, it)])
            else:
                for h in range(2):
                    tr(psb(bt)[:, h * 128:(h + 1) * 128], rr[:, h * 128:(h + 1) * 128], dk, [("ps", bt)],
                       signal=(h == 1))
                cp("act", KT[:, :, kt_idx * 128:(kt_idx + 1) * 128],
                   psb(bt)[:, 0:256].rearrange("p (h t) -> p h t", t=128), [("ps", bt)], [("KT", kt_idx)])

        stop_items = len(items)
        if stop == 12:
            stop_items = 2 * ntiles
        for ctr, item in enumerate(items[:stop_items]):
            st = stage_a(ctr, item)
            if pend:
                stage_b(pend.pop(0))
            pend.append(st)
        while pend:
            stage_b(pend.pop(0))
        if stop in (12, 13):
            return
        for gb in range(2):
            slot = w_get(("ain", j), 1536 + gb * 512)
            for n in range(4):
                for g in range(G):
                    bk = gen_bank()
                    proj_fm(slot, n, g, bk)
                    act(gT[:, gb * 4 + n, g * 512:(g + 1) * 512], PS[:, bk, :], AF.Silu, [("ps", bk)],
                        [("gT", gb * 4 + n, g)])
        if stop == 14:
            return
        scale = 128.0 ** -0.5
        units = []
        itc = 0
        for (s0, Ls) in seqs:
            kt_lo = 0 if is_sample else s0 // 128
            nkt = (nkt_cache + ntiles) if is_sample else Ls // 128
            for h in range(8):
                for q0 in range(s0, s0 + Ls, 512):
                    nq = min(512, s0 + Ls - q0)
                    for ki in range(nkt):
                        units.append(dict(h=h, q0=q0, nq=nq, ki=ki, nkt=nkt, kt=kt_lo + ki, itc=itc))
                    itc += 1
        LA = 2

        def emit_S(idx, u):
            h, q0, nq, kt = u["h"], u["q0"], u["nq"], u["kt"]
            bS = idx % 3
            mm(PS[:, bS, 0:nq], KT[:, h // 4, kt * 128:(kt + 1) * 128], QT[:, h, q0:q0 + nq], True, True,
               [("KT", kt)] + [("QT", h // 4, t) for t in range(q0 // 128, (q0 + nq) // 128)],
               [("ps", bS)], signal=True)

        def emit_rest(idx, u):
            h, q0, nq, kt, ki, nkt, ic = u["h"], u["q0"], u["nq"], u["kt"], u["ki"], u["nkt"], u["itc"]
            kvh = h // 4
            bS = idx % 3
            pb = idx % 3
            bO = 4 + (ic % 2)
            bD = 6 + (ic % 2)
            act(pT[:, pb, 0:nq], PS[:, bS, 0:nq], AF.Exp, [("ps", bS)], [("pT", pb)], scale=scale)
            mm(PS[:, bO, 0:nq], Vt[:, kt, kvh * 128:(kvh + 1) * 128], pT[:, pb, 0:nq], ki == 0,
               ki == nkt - 1, [("Vt", kt), ("pT", pb)], [("ps", bO)], signal=False)
            mm(PS[:, bD, 0:nq], onesb[:], pT[:, pb, 0:nq], ki == 0, ki == nkt - 1,
               [("pT", pb), "onesb"], [("ps", bD)], signal=True)
            if ki == nkt - 1:
                gg = q0 // 512
                o4 = 2 * (ic % 2)
                R1, R2 = sc(o4, nq), sc(o4 + 1, nq)
                S.op("dve", lambda e, R1=R1, bD=bD, nq=nq: e.reciprocal(R1, PS[:, bD, 0:nq]),
                     [("ps", bD)], [("sc", o4)])
                tt("dve", R2, PS[:, bO, 0:nq], R1, ALU.mult, [("ps", bO), ("sc", o4)], [("sc", o4 + 1)])
                gv = gT[:, h, q0:q0 + nq]
                tt("pool", gv, gv, R2, ALU.mult, [("gT", h, gg), ("sc", o4 + 1)], [("gT", h, gg)])

        dstep = max(1, len(units) // 10)
        for idx in range(len(units) + LA):
            if deferred and idx % dstep == dstep - 1:
                run_deferred(3)
            if idx < len(units):
                emit_S(idx, units[idx])
            if idx >= LA:
                emit_rest(idx - LA, units[idx - LA])

        if stop == 15:
            return

        def zin_fn(kc, it):
            return gT[:, kc, it * 128:(it + 1) * 128]

        def zin_keys(kc, it):
            return [("gT", kc, (it * 128) // 512)]

        out_proj_and_residual(l, tile0, ntiles, v, zin_fn, zin_keys, ("aout", j))
        S.barrier()

    for l in range(nlayers):
        j = l // 2
        if l == 0:
            queue_mod_ss(0)
            while deferred:
                run_deferred(6)
        if l % 2 == 0:
            rec_pass(l, j, 0, 4, 0, [(0, 256), (256, 256)], False)
            rec_pass(l, j, 4, 8, 1, [(0, 1024)], True)
        else:
            att_pass(l, j, 0, 4, 0, [(0, 256), (256, 256)], False)
            att_pass(l, j, 4, 8, 1, [(0, 1024)], True)

    for t in range(12):
        S.dma("sp", lambda e, t=t: e.dma_start(out=y_out[t], in_=X[:, t, :]), reads=[("X", t)])

    if wlist_in is not None:
        S.emit_all()
    es.close()
    return nc, dbg_out, wcollect


def _consts():
    ident = np.eye(128, dtype=np.float32)
    ones = np.ones((128, 128), np.float32)
    s = np.arange(64)[:, None]
    t = np.arange(64)[None, :]
    mk = np.stack([(s <= t), (s >= t)], axis=1).astype(np.float32)
    mk = np.ascontiguousarray(np.tile(mk, (1, 1, 4)))
    rmask = np.ones((128, 512), np.float32)
    rmask[:, ::CH] = 0.0

    def icnt(L):
        tt_ = np.arange(L)
        out = []
        for win in (2, 4, 8, 16):
            lo = np.clip(tt_ - win // 2, 0, L)
            hi = np.clip(tt_ + win // 2, 0, L)
            out.append(1.0 / (hi - lo).astype(np.float32))
        return np.stack(out).astype(np.float32)

    tpos = np.arange(1024)
    row = (tpos // 64).astype(np.float32)
    col = (tpos % 64).astype(np.float32)
    inv = (10000.0 ** (-np.arange(0, 64, 2, dtype=np.float32) / 64)).astype(np.float32)
    ang = np.concatenate([row[:, None] * inv[None, :], col[:, None] * inv[None, :]], axis=-1).astype(np.float32)
    cos_t = np.repeat(np.cos(ang).astype(np.float32), 2, axis=-1).reshape(8, 128, 128)
    sn = np.sin(ang).astype(np.float32)
    sin_t = np.stack([-sn, sn], axis=-1).reshape(8, 128, 128)
    return dict(ident=ident, ones=ones, mk=mk, rmask=rmask, icnt_s=icnt(1024), icnt_p=icnt(256),
                cos_t=cos_t, sin_t=sin_t)


def _prep_inputs(x_prompt, x_sample, c, state_hgrn, cache_k, cache_v, c_ctx, ada_w, ada_b,
                 norm_pre, norm_post, rec_w_in, rec_lb_logits, rec_head_norm, pool_w, pool_scale,
                 rec_w_out, att_w_in, att_q_norm, att_k_norm, att_w_out):
    f = lambda a: np.ascontiguousarray(np.asarray(a, dtype=np.float32))
    shared = dict(
        ada_w=f(ada_w),
        ada_bT=f(np.asarray(ada_b).reshape(4, 24, 128).transpose(2, 0, 1)),
        ada_bg=f(np.asarray(ada_b)[:, 2048:3072]),
        npreT=f(np.asarray(norm_pre).reshape(4, 8, 128).transpose(2, 0, 1)),
        npost=f(norm_post),
        rec_w_in=f(rec_w_in), rec_w_out=f(rec_w_out), att_w_in=f(att_w_in), att_w_out=f(att_w_out),
        lbT=f(np.asarray(rec_lb_logits).reshape(2, 2, 4, 128).transpose(3, 0, 1, 2)),
        hnT=f(np.asarray(rec_head_norm).T),
        pool_w=f(pool_w),
        pscT=f(np.asarray(pool_scale).reshape(2, 4, 128).transpose(2, 0, 1)),
        qn_row=f(att_q_norm), kn_row=f(att_k_norm),
    )
    shared.update(_consts())
    maps = []
    for i in range(NCORES):
        xin = np.concatenate([np.asarray(x_prompt[2 * i]).reshape(2, 128, 1024),
                              np.asarray(x_prompt[2 * i + 1]).reshape(2, 128, 1024),
                              np.asarray(x_sample[i]).reshape(8, 128, 1024)], axis=0)
        cvec = np.stack([np.asarray(c_ctx), np.asarray(c[i])], axis=0)
        cT = cvec.reshape(2, 8, 128).transpose(2, 1, 0)
        st = np.asarray(state_hgrn[i]).transpose(0, 3, 1, 2, 4).reshape(2, 128, 8, 128)
        m = dict(shared)
        m.update(x_in=f(xin), cT=f(cT), st_in=f(st),
                 ck_in=f(np.asarray(cache_k[i]).reshape(2, 512, 256)),
                 cv_in=f(np.asarray(cache_v[i]).reshape(2, 512, 256)))
        maps.append(m)
    return maps


_CACHE = {}


def kernel(**inputs):
    maps = _prep_inputs(**inputs)
    if "nc" not in _CACHE:
        _CACHE["nc"] = build_program()[0]
    nc = _CACHE["nc"]
    res = run_bass_kernel_spmd(nc, maps, core_ids=list(range(NCORES)))
    R = res.results
    y_prompt = np.zeros((16, 256, 1024), np.float32)
    y_sample = np.zeros((8, 1024, 1024), np.float32)
    new_state = np.zeros((16, 2, 2, 4, 128, 128), np.float32)
    new_k = np.zeros((16, 2, 256, 2, 128), np.float32)
    new_v = np.zeros((16, 2, 256, 2, 128), np.float32)
    for i in range(NCORES):
        y = np.asarray(R[i]["y_out"])
        y_prompt[2 * i] = y[0:2].reshape(256, 1024)
        y_prompt[2 * i + 1] = y[2:4].reshape(256, 1024)
        y_sample[i] = y[4:12].reshape(1024, 1024)
        new_state[2 * i:2 * i + 2] = np.asarray(R[i]["st_out"])
        new_k[2 * i:2 * i + 2] = np.asarray(R[i]["ck_out"]).reshape(2, 2, 256, 2, 128)
        new_v[2 * i:2 * i + 2] = np.asarray(R[i]["cv_out"]).reshape(2, 2, 256, 2, 128)
    return (y_prompt, y_sample, new_state, new_k, new_v)
```

```python
import numpy as np
from contextlib import ExitStack
import concourse.bass as bass
import concourse.mybir as mybir
from concourse.bass_utils import run_bass_kernel_spmd

F32 = mybir.dt.float32
BF16 = mybir.dt.bfloat16
AF = mybir.ActivationFunctionType
ALU = mybir.AluOpType
AX = mybir.AxisListType

ENGS = ["pe", "act", "dve", "pool", "sp"]
EPS = 1e-6
F_MIN = 1e-6
NCORES = 8
CH = 64
CSH = 20.0
NRING = 3


class Sched:
    def __init__(self, nc, n_dma_sems=16):
        self.nc = nc
        self.ops = {e: [] for e in ENGS}
        self.last_w = {}
        self.readers = {}
        self.n_dma_sems = n_dma_sems
        self.dma_slot_val = {}
        self.dma_rr = {"sp": 0, "pool": 0}
        self.all_dma_tokens = []
        self.pending = {e: set() for e in ENGS}
        self.trace = None

    def _deps(self, reads, writes):
        deps = set()
        for k in reads:
            t = self.last_w.get(k)
            if t is not None:
                deps.add(t)
        for k in writes:
            t = self.last_w.get(k)
            if t is not None:
                deps.add(t)
            for r in self.readers.get(k, ()):
                deps.add(r)
        return deps

    def _commit(self, tok, reads, writes):
        for k in writes:
            self.last_w[k] = tok
            self.readers[k] = []
        for k in reads:
            self.readers.setdefault(k, []).append(tok)

    def op(self, eng, emit, reads=(), writes=(), signal=True):
        deps = self._deps(reads, writes)
        deps |= self.pending[eng]
        self.pending[eng] = set()
        idx = len(self.ops[eng])
        tok = ("e", eng, idx)
        if eng == "pe":
            deps = {d for d in deps if not (d[0] == "e" and d[1] == "pe")}
        self.ops[eng].append(dict(emit=emit, deps=deps, signal=signal, tok=tok, dma=None))
        self._commit(tok, reads, writes)
        return tok

    def dma(self, q, emit, reads=(), writes=()):
        deps = self._deps(reads, writes)
        deps |= self.pending[q]
        self.pending[q] = set()
        slot = self.dma_rr[q]
        self.dma_rr[q] = (slot + 1) % self.n_dma_sems
        prev = self.dma_slot_val.get((q, slot), 0)
        if prev > 0:
            deps.add(("d", (q, slot), prev))
        val = prev + 16
        self.dma_slot_val[(q, slot)] = val
        tok = ("d", (q, slot), val)
        self.ops[q].append(dict(emit=emit, deps=deps, signal=False, tok=tok, dma=(q, slot)))
        self._commit(tok, reads, writes)
        self.all_dma_tokens.append(tok)
        return tok

    def barrier(self):
        toks = set()
        for e in ENGS:
            for i in range(len(self.ops[e]) - 1, -1, -1):
                if self.ops[e][i]["dma"] is None:
                    toks.add(self.ops[e][i]["tok"])
                    break
        for k, v in self.dma_slot_val.items():
            toks.add(("d", k, v))
        for e in ENGS:
            self.pending[e] |= toks

    def emit_all(self):
        nc = self.nc
        with ExitStack() as es:
            esem = {e: es.enter_context(nc.semaphore("sem_" + e)) for e in ENGS}
            dsem = {}
            for k in self.dma_slot_val:
                dsem[k] = es.enter_context(nc.semaphore("dsem_%s_%d" % k))
            counts = {}
            for e in ENGS:
                c = 0
                arr = []
                for o in self.ops[e]:
                    if o["signal"] and o["dma"] is None:
                        c += 1
                    arr.append(c)
                res = [None] * len(arr)
                nxt = None
                for i in range(len(arr) - 1, -1, -1):
                    o = self.ops[e][i]
                    if o["signal"] and o["dma"] is None:
                        nxt = arr[i]
                    res[i] = nxt
                counts[e] = res

            def resolve(tok):
                if tok[0] == "e":
                    v = counts[tok[1]][tok[2]]
                    assert v is not None, ("dep on op with no later signal", tok)
                    return ("e", tok[1]), esem[tok[1]], v
                return tok[1], dsem[tok[1]], tok[2]

            block = es.enter_context(nc.Block())

            def run(ename, eobj):
                seen = {}
                for o in self.ops[ename]:
                    waits = {}
                    for d in o["deps"]:
                        key, sem, v = resolve(d)
                        if v > waits.get(key, (None, 0))[1]:
                            waits[key] = (sem, v)
                    wl = []
                    for key, (sem, v) in waits.items():
                        if seen.get(key, 0) >= v:
                            continue
                        eobj.wait_ge(sem, v)
                        seen[key] = v
                        wl.append((key, v))
                    if self.trace is not None:
                        self.trace.append((ename, o["tok"], wl, o["signal"], o.get("tag")))
                    ins = o["emit"](eobj)
                    if o["dma"] is not None:
                        ins.then_inc(dsem[o["dma"]], 16)
                    elif o["signal"]:
                        ins.then_inc(esem[ename], 1)
                if ename == "sp":
                    for k, v in self.dma_slot_val.items():
                        eobj.wait_ge(dsem[k], v)

            @block.sync
            def _(e):
                run("sp", e)

            @block.tensor
            def _(e):
                run("pe", e)

            @block.scalar
            def _(e):
                run("act", e)

            @block.vector
            def _(e):
                run("dve", e)

            @block.gpsimd
            def _(e):
                run("pool", e)


def build_program(nlayers=4, dbg_names=(), stop=99):
    wl = _build(nlayers, (), stop, None)[2]
    nc, dbg_out, _ = _build(nlayers, dbg_names, stop, wl)
    return nc, dbg_out


def _build(nlayers, dbg_names, stop, wlist_in):
    nc = bass.Bass("TRN2", target_bir_lowering=False)
    es = ExitStack()

    def din(name, shape):
        return nc.dram_tensor(name, list(shape), F32, kind="ExternalInput").ap()

    def dout(name, shape):
        return nc.dram_tensor(name, list(shape), F32, kind="ExternalOutput").ap()

    x_in = din("x_in", [12, 128, 1024])
    cT_in = din("cT", [128, 8, 2])
    st_in = din("st_in", [2, 128, 8, 128])
    ck_in = din("ck_in", [2, 512, 256])
    cv_in = din("cv_in", [2, 512, 256])
    ada_w = din("ada_w", [4, 1024, 3072])
    ada_bT = din("ada_bT", [128, 4, 24])
    ada_bg = din("ada_bg", [4, 1024])
    npreT = din("npreT", [128, 4, 8])
    npost = din("npost", [4, 1024])
    rec_w_in = din("rec_w_in", [2, 1024, 3584])
    rec_w_out = din("rec_w_out", [2, 1024, 1024])
    att_w_in = din("att_w_in", [2, 1024, 2560])
    att_w_out = din("att_w_out", [2, 1024, 1024])
    lbT_in = din("lbT", [128, 2, 2, 4])
    hnT_in = din("hnT", [128, 2])
    pool_w = din("pool_w", [2, 4, 128, 128])
    pscT_in = din("pscT", [128, 2, 4])
    qn_row = din("qn_row", [2, 128])
    kn_row = din("kn_row", [2, 128])
    ident_in = din("ident", [128, 128])
    ones_in = din("ones", [128, 128])
    mk_in = din("mk", [64, 2, 256])
    rmask_in = din("rmask", [128, 512])
    icnt_s = din("icnt_s", [4, 1024])
    icnt_p = din("icnt_p", [4, 256])
    cos_in = din("cos_t", [8, 128, 128])
    sin_in = din("sin_t", [8, 128, 128])

    y_out = dout("y_out", [12, 128, 1024])
    st_out = dout("st_out", [2, 2, 2, 4, 128, 128])
    ck_out = dout("ck_out", [2, 2, 256, 256])
    cv_out = dout("cv_out", [2, 2, 256, 256])
    dbg_out = {}

    def sb(name, shape, dt, stack=None):
        return (stack or es).enter_context(nc.sbuf_tensor("sb_" + name, list(shape), dt))

    S = Sched(nc)

    X = sb("X", [128, 12, 1024], F32)
    ring = sb("ring", [128, NRING, 8, 512], BF16)
    hT = sb("hT", [128, 8, 1024], BF16)
    GT = sb("GT", [128, 2, 1024], F32)
    SC = sb("SC", [128, 8, 520], F32)
    idb = sb("idb", [128, 128], BF16)
    onesb = sb("onesb", [128, 128], BF16)
    onesf = sb("onesf", [128, 128], F32)
    rmask = sb("rmask", [128, 512], F32)
    mkf = sb("mkf", [64, 2, 256], F32)
    cT = sb("cTs", [128, 8, 2], F32)
    scT = sb("scT", [128, 8, 2], BF16)
    sc_rep = sb("sc_rep", [128, 8, 2, 128], BF16)
    adabT = sb("adabT", [128, 4, 24], F32)
    npre = sb("npre", [128, 4, 8], F32)
    lbl = sb("lbl", [128, 2, 2, 4], F32)
    lb = sb("lb", [128, 2, 2, 4], F32)
    oml = sb("oml", [128, 2, 2, 4], F32)
    hn = sb("hn", [128, 2], F32)
    psc = sb("psc", [128, 2, 4], F32)
    poolw = sb("poolw", [128, 8, 128], BF16)
    modT2 = sb("modT", [128, 2, 16, 2], F32)
    Gcol2 = sb("Gcol", [128, 2, 8, 2], F32)
    small = sb("small", [128, 64], F32)
    PS = es.enter_context(nc.psum_tensor("PS", [128, 8, 512], F32))
    ARENA_BYTES = 80 * 1024
    arena = sb("arena", [128, ARENA_BYTES // 4], F32)
    arena_off = [0]

    def aalloc(shape, dt):
        esz = 2 if dt == BF16 else 4
        n = int(np.prod(shape[1:])) * esz
        n = (n + 63) // 64 * 64
        off = arena_off[0]
        assert off + n <= ARENA_BYTES, ("arena overflow", off, n)
        arena_off[0] = off + n
        ap = arena[0:shape[0], off // 4:(off + n) // 4]
        if dt == BF16:
            ap = ap.bitcast(BF16)
        ap = ap[:, 0:int(np.prod(shape[1:]))]
        if len(shape) == 3:
            ap = ap.rearrange("p (a b) -> p a b", b=shape[2])
        elif len(shape) == 4:
            ap = ap.rearrange("p (a b c) -> p a b c", b=shape[2], c=shape[3])
        return ap

    def psb(b):
        return PS[:, b, :].bitcast(BF16)

    def sc(i, n=512):
        return SC[:, i, 0:n]

    def scb(i, n=1024):
        return SC[:, i, :].bitcast(BF16)[:, 0:n]

    def sc2(i):
        return SC[:, 2 * i:2 * i + 2, :].rearrange("p a b -> p (a b)")

    def act(out, in_, func, reads, writes, **kw):
        S.op("act", lambda e: e.activation(out=out, in_=in_, func=func, **kw), reads, writes)

    def tt(eng, out, a, b, op, reads, writes):
        S.op(eng, lambda e: e.tensor_tensor(out, a, b, op), reads, writes)

    def ts(eng, out, a, s1, s2, op0, op1, reads, writes):
        if s2 is None:
            S.op(eng, lambda e: e.tensor_scalar(out, a, s1, None, op0), reads, writes)
        else:
            S.op(eng, lambda e: e.tensor_scalar(out, a, s1, s2, op0, op1), reads, writes)

    def stt(eng, out, in0, scalar, in1, op0, op1, reads, writes):
        S.op(eng, lambda e: e.scalar_tensor_tensor(out, in0, scalar, in1, op0, op1), reads, writes)

    def cp(eng, out, in_, reads, writes):
        if eng == "act":
            S.op("act", lambda e: e.activation(out=out, in_=in_, func=AF.Copy), reads, writes)
        else:
            S.op(eng, lambda e: e.tensor_copy(out, in_), reads, writes)

    def mm(out, lhsT, rhs, start, stop, reads, writes, signal):
        S.op("pe", lambda e: e.matmul(out, lhsT, rhs, start=start, stop=stop), reads, writes, signal=signal)

    def tr(out, in_, reads, writes, signal):
        n = in_.shape[0]
        S.op("pe", lambda e: e.transpose(out, in_, idb[0:n, 0:n]), list(reads) + ["idb"], writes, signal=signal)

    def dbg(name, ap, shape, reads, dt=F32):
        if name not in dbg_names:
            return
        d = nc.dram_tensor("dbg_" + name, list(shape), dt, kind="ExternalOutput").ap()
        dbg_out[name] = d
        S.dma("sp", lambda e: e.dma_start(out=d, in_=ap), reads=reads)

    bank_rr = {"gen": 0, "tr": 0}

    def gen_bank():
        b = bank_rr["gen"]
        bank_rr["gen"] ^= 1
        return b

    def tr_bank():
        b = 2 + bank_rr["tr"]
        bank_rr["tr"] ^= 1
        return b

    wlist = list(wlist_in) if wlist_in is not None else []
    wcollect = []
    wstate = {"issued": 0, "next": 0}

    def w_issue_upto(k):
        while wstate["issued"] <= k and wstate["issued"] < len(wlist):
            i = wstate["issued"]
            wap, c0, ncol = wlist[i]
            slot = i % NRING
            src = wap[:, c0:c0 + ncol].rearrange("(c p) n -> p c n", p=128)
            S.dma("pool", lambda e, slot=slot, src=src, ncol=ncol: e.dma_start(out=ring[:, slot, :, 0:ncol], in_=src),
                  writes=[("ring", slot)])
            wstate["issued"] += 1

    def w_get(wkey, c0, ahead=NRING - 1):
        k = wstate["next"]
        wstate["next"] += 1
        wcollect.append((wkey, c0))
        if wlist_in is not None:
            assert wlist_in[k][3] == (wkey, c0), ("weight stream order mismatch", k, wlist_in[k][3], (wkey, c0))
        w_issue_upto(k + ahead)
        return k % NRING

    WSRC = {}
    for l_ in range(4):
        WSRC[("ada", l_)] = ada_w[l_]
    for j_ in range(2):
        WSRC[("rin", j_)] = rec_w_in[j_]
        WSRC[("rout", j_)] = rec_w_out[j_]
        WSRC[("ain", j_)] = att_w_in[j_]
        WSRC[("aout", j_)] = att_w_out[j_]
    if wlist_in is not None:
        wlist = [(WSRC[wk], c0, 512) for (wk, c0) in wlist_in]
        wlist_in = [(WSRC[wk], c0, 512, (wk, c0)) for (wk, c0) in wlist_in]

    for t in range(12):
        S.dma("sp", lambda e, t=t: e.dma_start(out=X[:, t, :], in_=x_in[t]), writes=[("X", t)])
    S.dma("pool", lambda e: e.dma_start(out=idb[:], in_=ident_in), writes=["idb"])
    S.dma("pool", lambda e: e.dma_start(out=onesb[:], in_=ones_in), writes=["onesb"])
    S.dma("sp", lambda e: e.dma_start(out=onesf[:], in_=ones_in), writes=["onesf"])
    S.dma("pool", lambda e: e.dma_start(out=poolw[:], in_=pool_w.rearrange("j g c d -> c (j g) d")), writes=["poolw"])
    S.dma("sp", lambda e: e.dma_start(out=rmask[:], in_=rmask_in), writes=["rmask"])
    S.dma("sp", lambda e: e.dma_start(out=mkf[:], in_=mk_in), writes=["mkf"])
    S.dma("sp", lambda e: e.dma_start(out=cT[:], in_=cT_in), writes=["cT"])
    S.dma("sp", lambda e: e.dma_start(out=adabT[:], in_=ada_bT), writes=["adabT"])
    S.dma("sp", lambda e: e.dma_start(out=npre[:], in_=npreT), writes=["npre"])
    S.dma("sp", lambda e: e.dma_start(out=lbl[:], in_=lbT_in), writes=["lbl"])
    S.dma("sp", lambda e: e.dma_start(out=hn[:], in_=hnT_in), writes=["hn"])
    S.dma("sp", lambda e: e.dma_start(out=psc[:], in_=pscT_in), writes=["psc"])
    w_issue_upto(NRING - 2)

    act(scT[:], cT[:], AF.Silu, ["cT"], ["scT"])
    cp("dve", sc_rep[:], scT[:].unsqueeze(3).to_broadcast([128, 8, 2, 128]), ["scT"], ["sc_rep"])
    S.op("dve", lambda e: e.memset(lb[:, 0, :, :], 0.0), [], ["lb0"])
    tt("dve", small[:, 0:8], lbl[:, 1, :, :].rearrange("p a b -> p (a b)"),
       lbl[:, 0, :, :].rearrange("p a b -> p (a b)"), ALU.subtract, ["lbl"], ["small"])
    act(lb[:, 1, :, :].rearrange("p a b -> p (a b)"), small[:, 0:8], AF.Sigmoid, ["small"], ["lb1"])
    ts("dve", oml[:].rearrange("p j a b -> p (j a b)"), lb[:].rearrange("p j a b -> p (j a b)"),
       -1.0, 1.0, ALU.mult, ALU.add, ["lb0", "lb1"], ["oml"])
    LBK = ["lb0", "lb1", "oml"]

    deferred = []

    def run_deferred(bk, n=1):
        for _ in range(n):
            if deferred:
                deferred.pop(0)(bk)

    def queue_mod_ss(l):
        modT = modT2[:, l % 2]
        Gcol = Gcol2[:, l % 2]

        def blk(b):
            def f(bk):
                slot = w_get(("ada", l), b * 512)
                for n in range(4):
                    for kc in range(8):
                        mm(PS[:, bk, n * 2:n * 2 + 2], ring[:, slot, kc, n * 128:(n + 1) * 128], scT[:, kc, :],
                           kc == 0, kc == 7, [("ring", slot), "scT"], [("ps", bk)], signal=(kc == 7 and n == 3))
                tt("dve", modT[:, 4 * b:4 * b + 4, :], PS[:, bk, 0:8].rearrange("p (c v) -> p c v", v=2),
                   adabT[:, l, 4 * b:4 * b + 4].unsqueeze(2).to_broadcast([128, 4, 2]), ALU.add,
                   [("ps", bk), "adabT"], [("modT", l % 2, b)])
                if b == 3:
                    stt("dve", Gcol, modT[:, 8:16, :], 1.0, npre[:, l, :].unsqueeze(2).to_broadcast([128, 8, 2]),
                        ALU.add, ALU.mult, [("modT", l % 2, 2), ("modT", l % 2, 3), "npre"], [("Gcol", l % 2)])
            return f
        for b in range(4):
            deferred.append(blk(b))

    def queue_mod_gate(l, v):
        def blk(b):
            def f(bk):
                if b == 0:
                    S.dma("sp", lambda e: e.dma_start(out=sc2(2)[:, 0:1024],
                                                      in_=ada_bg[l:l + 1, :].partition_broadcast(128)),
                          writes=[("sc", 4), ("sc", 5)])
                    S.dma("sp", lambda e: e.dma_start(out=sc2(3)[:, 0:1024],
                                                      in_=npost[l:l + 1, :].partition_broadcast(128)),
                          writes=[("sc", 6), ("sc", 7)])
                slot = w_get(("ada", l), (4 + b) * 512)
                for kc in range(8):
                    mm(PS[:, bk, :], sc_rep[:, kc, v, :], ring[:, slot, kc, :], kc == 0, kc == 7,
                       [("ring", slot), "sc_rep"], [("ps", bk)], signal=(kc == 7))
                tt("dve", GT[:, v, b * 512:(b + 1) * 512], PS[:, bk, :], sc2(2)[:, b * 512:(b + 1) * 512], ALU.add,
                   [("ps", bk), ("sc", 4), ("sc", 5)], [("GT", v, b)])
                tt("dve", GT[:, v, b * 512:(b + 1) * 512], GT[:, v, b * 512:(b + 1) * 512],
                   sc2(3)[:, b * 512:(b + 1) * 512], ALU.mult,
                   [("GT", v, b), ("sc", 6), ("sc", 7)], [("GT", v, b)])
            return f
        for b in range(2):
            deferred.append(blk(b))

    def queue_pass_mod(l, v):
        if v == 1 and l + 1 < nlayers:
            queue_mod_ss(l + 1)
        queue_mod_gate(l, v)

    def prenorm(l, tile0, ngroups, v):
        modT = modT2[:, l % 2]
        Gcol = Gcol2[:, l % 2]
        MK = [("Gcol", l % 2), ("modT", l % 2, 0), ("modT", l % 2, 1)]
        for g in range(ngroups):
            for jt in range(4):
                t = tile0 + g * 4 + jt
                ssq = small[:, 16 + jt:17 + jt]
                rs = small[:, 20 + jt:21 + jt]
                act(scb(4 + jt), X[:, t, :], AF.Square, [("X", t)], [("sc", 4 + jt), ("ssq", jt)], accum_out=ssq)
                act(rs, ssq, AF.Sqrt, [("ssq", jt)], [("rs", jt)], scale=1.0 / 1024, bias=EPS)
                S.op("dve", lambda e, rs=rs: e.reciprocal(rs, rs), [("rs", jt)], [("rs", jt)])
                ts("dve", scb(jt), X[:, t, :], rs, None, ALU.mult, None, [("X", t), ("rs", jt)], [("sc", jt)])
            for cpair in range(4):
                bk = tr_bank()
                for cc in range(2):
                    c = 2 * cpair + cc
                    for jt in range(4):
                        tr(psb(bk)[:, cc * 512 + jt * 128: cc * 512 + (jt + 1) * 128],
                           scb(jt)[:, c * 128:(c + 1) * 128], [("sc", jt)], [("ps", bk)],
                           signal=(cc == 1 and jt == 3))
                for cc in range(2):
                    c = 2 * cpair + cc
                    dst = hT[:, c, g * 512:(g + 1) * 512]
                    src = psb(bk)[:, cc * 512:(cc + 1) * 512]
                    if cpair % 2 == 0:
                        act(dst, src, AF.Identity, [("ps", bk)] + MK, [("hT", g, c)],
                            scale=Gcol[:, c, v:v + 1], bias=modT[:, c, v:v + 1])
                    else:
                        ts("dve", dst, src, Gcol[:, c, v:v + 1], modT[:, c, v:v + 1], ALU.mult, ALU.add,
                           [("ps", bk)] + MK, [("hT", g, c)])

    def hT_keys(g):
        return [("hT", g, c) for c in range(8)]

    def proj_fm(slot, n, g, bk):
        for kc in range(8):
            mm(PS[:, bk, :], ring[:, slot, kc, n * 128:(n + 1) * 128], hT[:, kc, g * 512:(g + 1) * 512],
               kc == 0, kc == 7, [("ring", slot)] + hT_keys(g), [("ps", bk)], signal=(kc == 7))

    def proj_tm(slot, tok0, ntok, bk):
        g = tok0 // 512
        for kc in range(8):
            mm(PS[0:ntok, bk, :], hT[:, kc, tok0:tok0 + ntok], ring[:, slot, kc, :],
               kc == 0, kc == 7, [("ring", slot)] + hT_keys(g), [("ps", bk)], signal=(kc == 7))

    def out_proj_and_residual(l, tile0, ntiles, v, zin_fn, zin_keys_fn, wkey):
        while deferred:
            run_deferred(gen_bank())
        s0 = w_get(wkey, 0)
        s1 = w_get(wkey, 512, NRING - 2)
        slots = (s0, s1)
        for it in range(ntiles):
            t = tile0 + it
            b0 = 4 + 2 * (it % 2)
            for nb in range(2):
                for kc in range(8):
                    mm(PS[:, b0 + nb, :], zin_fn(kc, it), ring[:, slots[nb], kc, :], kc == 0, kc == 7,
                       [("ring", slots[nb])] + zin_keys_fn(kc, it), [("ps", b0 + nb)], signal=(kc == 7))
            po = PS[:, b0:b0 + 2, :].rearrange("p a b -> p (a b)")
            pk = [("ps", b0), ("ps", b0 + 1)]
            ssq = small[:, 24 + (it % 2):25 + (it % 2)]
            rs = small[:, 26 + (it % 2):27 + (it % 2)]
            jk = 6 + (it % 2)
            act(scb(jk), po, AF.Square, pk, [("sc", jk), ("ssq2", it % 2)], accum_out=ssq)
            act(rs, ssq, AF.Sqrt, [("ssq2", it % 2)], [("rs2", it % 2)], scale=1.0 / 1024, bias=EPS)
            S.op("dve", lambda e, rs=rs: e.reciprocal(rs, rs), [("rs2", it % 2)], [("rs2", it % 2)])
            tmpk = 2 * (it % 2)
            tmp = sc2(it % 2)[:, 0:1024]
            stt("dve", tmp, po, rs, GT[:, v, :], ALU.mult, ALU.mult,
                pk + [("rs2", it % 2), ("GT", v, 0), ("GT", v, 1)], [("sc", tmpk), ("sc", tmpk + 1)])
            tt("dve", X[:, t, :], X[:, t, :], tmp, ALU.add,
               [("X", t), ("sc", tmpk), ("sc", tmpk + 1)], [("X", t)])

    def rec_pass(l, j, tile0, ntiles, v, seqs, is_sample):
        L = ntiles * 128
        G = L // 512
        nch = L // CH
        arena_off[0] = 0
        qo = aalloc([128, 4, L], BF16)
        qp = [aalloc([128, 4, L], BF16) for r in range(2)]
        kp = [aalloc([128, 4, L], BF16) for r in range(2)]
        v_c = aalloc([64, nch, 512], BF16)
        ypool = aalloc([128, 4, L], BF16)
        tabS = aalloc([128, 2, 4, nch], F32)
        tabG = aalloc([128, 2, 4, nch], F32)
        tabA = aalloc([128, 2, 4, nch], F32)
        St = aalloc([128, 8, 128], F32)
        Stb = aalloc([128, 8, 128], BF16)
        kTt = aalloc([64, 8, 128], BF16)
        ATm = aalloc([64, 8, 64], BF16)
        Lseq = seqs[0][1]
        icnt = aalloc([128, Lseq], F32)

        if stop <= 1:
            return
        prenorm(l, tile0, G, v)
        src_ic = icnt_s if is_sample else icnt_p
        if stop <= 2:
            return

        queue_pass_mod(l, v)
        slot = w_get(("rin", j), 1024)
        for n in range(4):
            for g in range(G):
                bk = gen_bank()
                proj_fm(slot, n, g, bk)
                act(qo[:, n, g * 512:(g + 1) * 512], PS[:, bk, :], AF.Silu, [("ps", bk)], [("qo", n, g)])

        f_items = [(r, n, g) for r in range(2) for n in range(4) for g in range(G)]
        f_slots = {}

        def f_ctx(idx):
            r, n, g = f_items[idx]
            o4 = 4 * (idx % 2)
            Ts = [sc(o4 + i) for i in range(4)]
            Ks = [("sc", o4 + i) for i in range(4)]
            return r, n, g, Ts, Ks

        def f_s0(idx):
            r, n, g, (T1, T2, T3, T4), (K1, K2, K3, K4) = f_ctx(idx)
            if r not in f_slots:
                f_slots[r] = w_get(("rin", j), 1536 + 512 * r)
            bk = gen_bank()
            proj_fm(f_slots[r], n, g, bk)
            act(T1, PS[:, bk, :], AF.Exp, [("ps", bk)], [K1], scale=-1.0)

        def f_s1(idx):
            r, n, g, (T1, T2, T3, T4), (K1, K2, K3, K4) = f_ctx(idx)
            act(T1, T1, AF.Ln, [K1], [K1], bias=1.0)
            act(T1, T1, AF.Exp, [K1], [K1], scale=-1.0)
            ts("dve", T1, T1, oml[:, j, r, n:n + 1], lb[:, j, r, n:n + 1], ALU.mult, ALU.add, [K1] + LBK, [K1])
            ts("dve", T1, T1, F_MIN, 1.0, ALU.max, ALU.min, [K1], [K1])
            act(T2, T1, AF.Ln, [K1], [K2])
            ts("pool", T3, T1, -1.0, 1.0, ALU.mult, ALU.add, [K1], [K3])

        def f_s2(idx):
            r, n, g, (T1, T2, T3, T4), (K1, K2, K3, K4) = f_ctx(idx)
            S.op("dve", lambda e: e.tensor_tensor_scan(T4, rmask[:], T2, 0.0, ALU.mult, ALU.add),
                 [K2, "rmask"], [K4])
            b3 = T4.rearrange("p (c t) -> p c t", t=CH)
            c0 = g * 8
            tk = ("tab", r, n, g)
            smt = small[:, 32 + 8 * (idx % 2):40 + 8 * (idx % 2)]
            smk = ("smt", idx % 2)
            tt("dve", smt, b3[:, :, CH - 1], b3[:, :, CH // 2 - 1], ALU.subtract, [K4], [smk])
            e_mid = tabS if r == 0 else tabG
            e_dif = tabG if r == 0 else tabS
            act(e_mid[:, r, n, c0:c0 + 8], b3[:, :, CH // 2 - 1], AF.Exp, [K4], [tk + (0,)],
                bias=(-CSH if r == 0 else CSH))
            act(e_dif[:, r, n, c0:c0 + 8], smt, AF.Exp, [smk], [tk + (1,)], bias=(CSH if r == 0 else -CSH))
            act(tabA[:, r, n, c0:c0 + 8], b3[:, :, CH - 1], AF.Exp, [K4], [tk + (2,)])
            tt("pool", T1.rearrange("p (c t) -> p c t", t=CH), b3,
               b3[:, :, CH // 2 - 1:CH // 2].to_broadcast([128, 8, CH]), ALU.subtract, [K4, K1], [K1])
            if r == 1:
                tt("dve", T1, T1, T2, ALU.subtract, [K1, K2], [K1])

        def f_s3(idx):
            r, n, g, (T1, T2, T3, T4), (K1, K2, K3, K4) = f_ctx(idx)
            sg = 1.0 if r == 0 else -1.0
            act(T2, T1, AF.Exp, [K1], [K2], scale=sg, bias=-CSH)
            act(T4, T1, AF.Exp, [K1], [K4], scale=-sg, bias=-CSH)
            tt("dve", qp[r][:, n, g * 512:(g + 1) * 512], qo[:, n, g * 512:(g + 1) * 512], T2, ALU.mult,
               [("qo", n, g), K2], [("qp", r, n, g)])
            tt("pool", kp[r][:, n, g * 512:(g + 1) * 512], T3, T4, ALU.mult, [K3, K4], [("kp", r, n, g)])

        NF = len(f_items)
        for idx in range(NF + 1):
            if idx < NF:
                f_s0(idx)
            if idx >= 1:
                f_s2(idx - 1)
            if idx < NF:
                f_s1(idx)
            if idx >= 1:
                f_s3(idx - 1)

        tg = "_%d_%d" % (l, int(is_sample))
        allk = lambda nm, *pre: [(nm,) + pre + (n, g) for n in range(4) for g in range(G)]
        dbg("hT" + tg, hT[:, :, 0:L], [128, 8, L], [("hT", g, c) for g in range(G) for c in range(8)], BF16)
        dbg("qp0" + tg, qp[0], [128, 4, L], allk("qp", 0), BF16)
        dbg("kp0" + tg, kp[0], [128, 4, L], allk("kp", 0), BF16)
        dbg("qp1" + tg, qp[1], [128, 4, L], allk("qp", 1), BF16)
        dbg("kp1" + tg, kp[1], [128, 4, L], allk("kp", 1), BF16)
        dbg("tabS" + tg, tabS, [128, 2, 4, nch], [("tab", r, n, g, k) for r in range(2) for n in range(4) for g in range(G) for k in range(3)])
        dbg("tabG" + tg, tabG, [128, 2, 4, nch], [("tab", r, n, g, k) for r in range(2) for n in range(4) for g in range(G) for k in range(3)])
        dbg("tabA" + tg, tabA, [128, 2, 4, nch], [("tab", r, n, g, k) for r in range(2) for n in range(4) for g in range(G) for k in range(3)])
        if stop <= 3:
            return
        slot = w_get(("rin", j), 2560)
        for c in range(nch):
            bk = gen_bank()
            proj_tm(slot, c * CH, CH, bk)
            cp("act" if c % 2 == 0 else "dve", v_c[:, c, :], PS[0:64, bk, :], [("ps", bk)], [("v_c", c)])

        slot = w_get(("rin", j), 512)
        for n in range(4):
            for g in range(G):
                bk = gen_bank()
                proj_fm(slot, n, g, bk)
                act(ypool[:, n, g * 512:(g + 1) * 512], PS[:, bk, :], AF.Silu, [("ps", bk)], [("yp", n, g)])

        slot = w_get(("rin", j), 0)
        nseq = len(seqs)
        Wd = nseq * (Lseq + 16)
        for n in range(4):
            win = (2, 4, 8, 16)[n]
            UB, WA, WB = sc2(0), sc2(1), sc2(2)
            DB = sc2(3).bitcast(BF16)
            KU, KA, KB, KD = [[("sc", 2 * i), ("sc", 2 * i + 1)] for i in range(4)]
            S.dma("sp", lambda e, n=n: e.dma_start(out=icnt, in_=src_ic[n:n + 1, :].partition_broadcast(128)),
                  writes=["icnt"])
            S.op("pool", lambda e, UB=UB: e.memset(UB[:, 0:Wd], 0.0), [], KU)
            for g in range(G):
                bk = gen_bank()
                proj_fm(slot, n, g, bk)
                for k, (s0, Ls) in enumerate(seqs):
                    a = max(s0, g * 512)
                    b = min(s0 + Ls, (g + 1) * 512)
                    if a >= b:
                        continue
                    dst0 = k * (Ls + 16) + 8 + (a - s0)
                    cp("act", UB[:, dst0:dst0 + (b - a)], PS[:, bk, a - g * 512:b - g * 512], [("ps", bk)], KU)
            tt("dve", WA[:, 1:Wd], UB[:, 0:Wd - 1], UB[:, 1:Wd], ALU.add, KU, KA)
            cur, curk, oth, othk = WA, KA, WB, KB
            lo, hi = 1, Wd
            if win >= 4:
                tt("dve", oth[:, lo + 1:hi - 1], cur[:, lo:hi - 2], cur[:, lo + 2:hi], ALU.add, curk, othk)
                cur, curk, oth, othk = oth, othk, cur, curk
                lo, hi = lo + 1, hi - 1
            if win >= 8:
                tt("dve", oth[:, lo + 2:hi - 2], cur[:, lo:hi - 4], cur[:, lo + 4:hi], ALU.add, curk, othk)
                cur, curk, oth, othk = oth, othk, cur, curk
                lo, hi = lo + 2, hi - 2
            if win >= 16:
                tt("dve", oth[:, lo + 4:hi - 4], cur[:, lo:hi - 8], cur[:, lo + 8:hi], ALU.add, curk, othk)
                cur, curk, oth, othk = oth, othk, cur, curk
                lo, hi = lo + 4, hi - 4
            assert lo <= 8 and hi >= Wd - 8
            for k, (s0, Ls) in enumerate(seqs):
                base = k * (Ls + 16) + 8
                tt("dve", oth[:, base:base + Ls], cur[:, base:base + Ls], icnt, ALU.mult, curk + ["icnt"], othk)
                tt("dve", DB[:, s0:s0 + Ls], oth[:, base:base + Ls], UB[:, base:base + Ls], ALU.subtract,
                   othk + KU, KD)
            for g in range(G):
                bk = gen_bank()
                mm(PS[:, bk, :], poolw[:, j * 4 + n, :], DB[:, g * 512:(g + 1) * 512], True, True,
                   KD + ["poolw"], [("ps", bk)], signal=True)
                yv = ypool[:, n, g * 512:(g + 1) * 512]
                stt("dve", yv, PS[:, bk, :], psc[:, j, n:n + 1], yv, ALU.mult, ALU.mult,
                    [("ps", bk), "psc", ("yp", n, g)], [("yp", n, g)])

        dbg("vc" + tg, v_c, [64, nch, 512], [("v_c", c) for c in range(nch)], BF16)
        dbg("yp" + tg, ypool, [128, 4, L], allk("yp"), BF16)
        if stop <= 4:
            return
        if is_sample:
            ic_bf = icnt.bitcast(BF16)
            kT2 = [kTt, ic_bf[0:64, 0:1024].rearrange("p (a b) -> p a b", b=128)]
            AT2 = [ATm, ic_bf[0:64, 1024:1536].rearrange("p (a b) -> p a b", b=64)]
            alias_k = ["icnt"]
        else:
            kT2 = [kTt, aalloc([64, 8, 128], BF16)]
            AT2 = [ATm, aalloc([64, 8, 64], BF16)]
            alias_k = []
        for bsel in range(2):
            S.op("dve", lambda e, bsel=bsel: e.memset(AT2[bsel], 0.0), [],
                 [("ATm", bsel, 0), ("ATm", bsel, 1)] + alias_k)
        ctxs = []
        for si, (s0, Ls) in enumerate(seqs):
            if si == 0:
                St_s, Stb_s = St, Stb
            else:
                St_s, Stb_s = aalloc([128, 8, 128], F32), aalloc([128, 8, 128], BF16)
            ctxs.append(dict(si=si, s0=s0, nst=Ls // CH, cb=s0 // CH, St=St_s, Stb=Stb_s))
            if is_sample:
                S.dma("sp", lambda e: e.dma_start(out=St_s, in_=st_in[j]), writes=[("St", si)])
            else:
                S.op("dve", lambda e, St_s=St_s: e.memset(St_s, 0.0), [], [("St", si)])
        gsteps = []
        for i in range(max(c["nst"] for c in ctxs)):
            for c in ctxs:
                if i < c["nst"]:
                    gsteps.append((c, i))

        def geo(c, i):
            cr = (c["cb"] + i, c["cb"] + c["nst"] - 1 - i)
            t0 = (cr[0] * CH, cr[1] * CH)
            gq = (t0[0] // 512, t0[1] // 512)
            return cr, t0, gq

        def scan_p(g):
            c, i = gsteps[g]
            cr, t0, gq = geo(c, i)
            bsel = g % 2
            kTb, ATb = kT2[bsel], AT2[bsel]
            bT = 2 + bsel
            for r in range(2):
                for n in range(4):
                    tr(psb(bT)[0:64, (4 * r + n) * 128:(4 * r + n + 1) * 128], kp[r][:, n, t0[r]:t0[r] + CH],
                       [("kp", r, n, gq[r])], [("ps", bT)], signal=(r == 1 and n == 3))
            cp("act", kTb.rearrange("p a b -> p (a b)"), psb(bT)[0:64, :], [("ps", bT)],
               [("kTt", bsel)] + alias_k)
            bA = (0, 5)[bsel]
            for r in range(2):
                for n in range(4):
                    mm(PS[0:64, bA, (4 * r + n) * 64:(4 * r + n + 1) * 64], kp[r][:, n, t0[r]:t0[r] + CH],
                       qp[r][:, n, t0[r]:t0[r] + CH], True, True,
                       [("kp", r, n, gq[r]), ("qp", r, n, gq[r])], [("ps", bA)], signal=(r == 1 and n == 3))
            for r in range(2):
                mku = mkf[:, r, :].bitcast(mybir.dt.uint32)
                S.op("dve", lambda e, r=r, mku=mku, bA=bA, ATb=ATb: e.copy_predicated(
                    ATb[:, 4 * r:4 * r + 4, :].rearrange("p a b -> p (a b)"), mku,
                    PS[0:64, bA, 256 * r:256 * r + 256]),
                    [("ps", bA), "mkf"], [("ATm", bsel, r)] + alias_k)

        def scan_q(g):
            c, i = gsteps[g]
            cr, t0, gq = geo(c, i)
            bsel = g % 2
            kTb, ATb = kT2[bsel], AT2[bsel]
            St_s, Stb_s, si = c["St"], c["Stb"], c["si"]
            SK = ("St", si)
            first = i < c["nst"] // 2
            tabk = lambda r, kind: [("tab", r, n, gq[r], kind) for n in range(4)]
            if deferred:
                run_deferred(1)
            for r in range(2):
                for n in range(4):
                    mm(PS[:, 6 + r, n * 128:(n + 1) * 128], kTb[:, 4 * r + n, :],
                       v_c[:, cr[r], n * 128:(n + 1) * 128], True, True,
                       [("kTt", bsel), ("v_c", cr[r])], [("ps", 6 + r)], signal=(n == 3))
            ts0 = 2 * bsel
            tmpKV = SC[:, ts0:ts0 + 2, 0:512].rearrange("p a (h v) -> p a h v", v=128)
            for r in range(2):
                for n in range(4):
                    act(tmpKV[:, r, n, :], PS[:, 6 + r, n * 128:(n + 1) * 128], AF.Identity,
                        [("ps", 6 + r)] + tabk(r, 0) + tabk(r, 1), [("sc", ts0 + r, n)],
                        scale=tabG[:, r, n, cr[r]:cr[r] + 1])
            for r in range(2):
                tt("dve", Stb_s[:, 4 * r:4 * r + 4, :], St_s[:, 4 * r:4 * r + 4, :],
                   tabS[:, r, :, cr[r]:cr[r] + 1].to_broadcast([128, 4, 128]), ALU.mult,
                   [SK] + tabk(r, 0) + tabk(r, 1), [("Stb", si, r)])
            for r in range(2):
                tt("dve", St_s[:, 4 * r:4 * r + 4, :], St_s[:, 4 * r:4 * r + 4, :],
                   tabA[:, r, :, cr[r]:cr[r] + 1].to_broadcast([128, 4, 128]), ALU.mult,
                   [SK] + tabk(r, 2), [SK])
            St4 = St_s.rearrange("p (a h) v -> p a h v", a=2)
            tt("dve", St4, St4, tmpKV, ALU.add,
               [SK] + [("sc", ts0 + r, n) for r in range(2) for n in range(4)],
               [SK, ("sc", ts0), ("sc", ts0 + 1)])
            bO = 4
            for r in range(2):
                for n in range(4):
                    o_ap = PS[:, bO, (4 * r + n) * 64:(4 * r + n + 1) * 64]
                    mm(o_ap, v_c[:, cr[r], n * 128:(n + 1) * 128], ATb[:, 4 * r + n, :], True, False,
                       [("v_c", cr[r]), ("ATm", bsel, r)], [("ps", bO)], signal=False)
                    mm(o_ap, Stb_s[:, 4 * r + n, :], qp[r][:, n, t0[r]:t0[r] + CH], False, True,
                       [("Stb", si, r), ("qp", r, n, gq[r])], [("ps", bO)], signal=(r == 1 and n == 3))
            for r in range(2):
                dst = qo[:, :, t0[r]:t0[r] + CH]
                src = PS[:, bO, 256 * r:256 * r + 256].rearrange("p (a b) -> p a b", b=64)
                ok = [("qo", n, gq[r]) for n in range(4)]
                E2 = float(np.exp(2.0 * CSH))
                if first:
                    act(dst, src, AF.Identity, [("ps", bO)], ok, scale=E2)
                else:
                    stt("dve", dst, src, E2, dst, ALU.mult, ALU.add, [("ps", bO)] + ok, ok)

        S.op("dve", lambda e: e.memset(SC[:, 0:4, 0:8], 0.0), [], [("sc", q) for q in range(4)] +
             [("sc", q, n) for q in range(4) for n in range(4)])
        scan_p(0)
        for g in range(len(gsteps)):
            if g + 1 < len(gsteps):
                scan_p(g + 1)
            scan_q(g)
        if not is_sample:
            for c in ctxs:
                dst = st_out[c["si"], j].rearrange("r h d v -> d (r h) v")
                S.dma("sp", lambda e, dst=dst, St_s=c["St"]: e.dma_start(out=dst, in_=St_s), reads=[("St", c["si"])])

        dbg("o" + tg, qo, [128, 4, L], allk("qo"), BF16)
        if stop <= 5:
            return
        ctr = 0
        for n in range(4):
            for g in range(G):
                o4 = 4 * (ctr % 2)
                ctr += 1
                T1, T2 = sc(o4), sc(o4 + 1)
                K1, K2 = ("sc", o4), ("sc", o4 + 1)
                ov = qo[:, n, g * 512:(g + 1) * 512]
                sqb = scb(o4 + 2, 512)
                act(sqb, ov, AF.Square, [("qo", n, g)], [("sc", o4 + 2)])
                bk = gen_bank()
                mm(PS[:, bk, :], onesb[:], sqb, True, True, [("sc", o4 + 2), "onesb"], [("ps", bk)], signal=True)
                act(T1, PS[:, bk, :], AF.Sqrt, [("ps", bk)], [K1], scale=1.0 / 128, bias=EPS)
                S.op("dve", lambda e, T1=T1: e.reciprocal(T1, T1), [K1], [K1])
                stt("dve", qp[0][:, n, g * 512:(g + 1) * 512], ov, hn[:, j:j + 1], T1, ALU.mult, ALU.mult,
                    [("qo", n, g), "hn", K1, ("qp", 0, n, g)], [("qp", 0, n, g)])

        slot = w_get(("rin", j), 3072)
        ctr = 0
        for n in range(4):
            for g in range(G):
                bk = gen_bank()
                proj_fm(slot, n, g, bk)
                o4 = 4 * (ctr % 2) + 3
                ctr += 1
                sgt = scb(o4, 512)
                act(sgt, PS[:, bk, :], AF.Silu, [("ps", bk)], [("sc", o4)])
                zv = qp[0][:, n, g * 512:(g + 1) * 512]
                tt("dve", zv, zv, sgt, ALU.mult, [("qp", 0, n, g), ("sc", o4)], [("qp", 0, n, g)])
        dbg("z" + tg, qp[0], [128, 4, L], allk("qp", 0), BF16)
        dbg("zin%d_%d" % (l, int(is_sample)), ypool, [128, 4, L], [("yp", n, g) for n in range(4) for g in range(G)])

        def zin_fn(kc, it):
            buf = ypool if kc < 4 else qp[0]
            return buf[:, kc % 4, it * 128:(it + 1) * 128]

        def zin_keys(kc, it):
            g = (it * 128) // 512
            return [("yp", kc, g)] if kc < 4 else [("qp", 0, kc - 4, g)]

        out_proj_and_residual(l, tile0, ntiles, v, zin_fn, zin_keys, ("rout", j))
        S.barrier()

    def att_pass(l, j, tile0, ntiles, v, seqs, is_sample):
        L = ntiles * 128
        G = L // 512
        nkt_cache = 4 if is_sample else 0
        arena_off[0] = 0
        QT = aalloc([128, 8, L], BF16)
        KT = aalloc([128, 2, 512 + L], BF16)
        Vt = aalloc([128, nkt_cache + ntiles, 256], BF16)
        gT = aalloc([128, 8, L], BF16)
        pT = aalloc([128, 4, 512], BF16)
        gq = aalloc([128, 128], F32)
        gk = aalloc([128, 128], F32)
        stg = aalloc([128, 2, 512], F32)
        cosT = aalloc([128, 8, 128], F32)
        sinT = aalloc([128, 8, 128], F32)
        ckt = aalloc([128, 4, 256], BF16)
        if is_sample:
            CGq = aalloc([128, 8, 128], F32)
            SGq = aalloc([128, 8, 128], F32)
            CGk = aalloc([128, 8, 128], F32)
            SGk = aalloc([128, 8, 128], F32)

        S.dma("sp", lambda e: e.dma_start(out=gq, in_=qn_row[j:j + 1, :].partition_broadcast(128)), writes=["gq"])
        S.dma("sp", lambda e: e.dma_start(out=gk, in_=kn_row[j:j + 1, :].partition_broadcast(128)), writes=["gk"])
        if is_sample:
            S.dma("sp", lambda e: e.dma_start(out=cosT, in_=cos_in.rearrange("t p f -> p t f")), writes=["cosT"])
            S.dma("sp", lambda e: e.dma_start(out=sinT, in_=sin_in.rearrange("t p f -> p t f")), writes=["sinT"])
            S.dma("pool", lambda e: e.dma_start(out=ckt, in_=ck_in[j].rearrange("(t p) f -> p t f", p=128)),
                  writes=["ckt"])
            S.dma("pool", lambda e: e.dma_start(out=Vt[:, 0:4, :], in_=cv_in[j].rearrange("(t p) f -> p t f", p=128)),
                  writes=[("Vt", t) for t in range(4)])
        prenorm(l, tile0, G, v)
        if is_sample:
            for t in range(4):
                bk = tr_bank()
                for h in range(2):
                    tr(psb(bk)[:, h * 128:(h + 1) * 128], ckt[:, t, h * 128:(h + 1) * 128], ["ckt"], [("ps", bk)],
                       signal=(h == 1))
                cp("act", KT[:, :, t * 128:(t + 1) * 128], psb(bk)[:, 0:256].rearrange("p (h t) -> p h t", t=128),
                   [("ps", bk)], [("KT", t)])

        def normrope(src_ps, pskeys, nh, gain, it, dst_bf, dkeys, ctr):
            o4 = 4 * (ctr % 2)
            A, B, C = sc(o4, nh * 128), sc(o4 + 1, nh * 128), sc(o4 + 2, nh * 128)
            KA, KB, KC = ("sc", o4), ("sc", o4 + 1), ("sc", o4 + 2)
            ssq = small[:, 48 + 4 * (ctr % 2):48 + 4 * (ctr % 2) + nh]
            sk = ("ssq3", ctr % 2)
            v3 = lambda ap: ap.rearrange("p (h d) -> p h d", d=128)
            act(A, src_ps, AF.Square, pskeys, [KA])
            S.op("dve", lambda e: e.tensor_reduce(out=ssq, in_=v3(A), axis=AX.X, op=ALU.add), [KA], [sk])
            act(ssq, ssq, AF.Sqrt, [sk], [sk], scale=1.0 / 128, bias=EPS)
            S.op("dve", lambda e: e.reciprocal(ssq, ssq), [sk], [sk])
            tt("dve", v3(B), v3(src_ps), ssq.unsqueeze(2).to_broadcast([128, nh, 128]), ALU.mult,
               pskeys + [sk], [KB])
            gainb = gain.unsqueeze(1).to_broadcast([128, nh, 128])
            if not is_sample:
                tt("dve", v3(dst_bf), v3(B), gainb, ALU.mult, [KB, "gq", "gk"], dkeys)
                return B
            CG, SG = (CGq, SGq) if nh == 4 else (CGk, SGk)
            c2 = CG[:, it, :].unsqueeze(1).to_broadcast([128, nh, 128])
            tt("dve", v3(A), v3(B), c2, ALU.mult, [KB, "ropetab"], [KA])
            B4 = B.rearrange("p (h i two) -> p h i two", i=64, two=2)
            C4 = C.rearrange("p (h i two) -> p h i two", i=64, two=2)
            s4 = SG[:, it, :].rearrange("p (i two) -> p i two", two=2)
            for e_ in range(2):
                tt("pool" if e_ == 0 else "dve", C4[:, :, :, e_], B4[:, :, :, 1 - e_],
                   s4[:, :, e_].unsqueeze(1).to_broadcast([128, nh, 64]), ALU.mult, [KB, "ropetab", KC], [(KC[0], KC[1], e_)])
            tt("pool", dst_bf, A, C, ALU.add, [KA, (KC[0], KC[1], 0), (KC[0], KC[1], 1)], dkeys + [KC])
            return B

        if is_sample:
            for (CG, SG, gain) in ((CGq, SGq, gq), (CGk, SGk, gk)):
                gb8 = gain.unsqueeze(1).to_broadcast([128, 8, 128])
                tt("dve", CG, cosT, gb8, ALU.mult, ["cosT", "gq", "gk"], ["ropetab"])
                g2 = gain.rearrange("p (i two) -> p i two", two=2)
                S4 = SG.rearrange("p t (i two) -> p t i two", two=2)
                s8 = sinT.rearrange("p t (i two) -> p t i two", two=2)
                for e_ in range(2):
                    tt("dve", S4[:, :, :, e_], s8[:, :, :, e_],
                       g2[:, :, 1 - e_].unsqueeze(1).to_broadcast([128, 8, 64]), ALU.mult,
                       ["sinT", "gq", "gk"], ["ropetab"])
        queue_pass_mod(l, v)
        if stop == 11:
            return
        items = []
        for qb in range(2):
            for it in range(ntiles):
                items.append(("q", qb, it))
        for it in range(ntiles):
            items.append(("kv", 0, it))
        slots = {}
        pend = []

        def stage_a(ctr, item):
            kind, qb, it = item
            key = (kind, qb)
            if key not in slots:
                slots[key] = w_get(("ain", j), qb * 512 if kind == "q" else 1024)
            slot = slots[key]
            bk = gen_bank()
            proj_tm(slot, it * 128, 128, bk)
            pk = [("ps", bk)]
            dk = [("sc", 4 * (ctr % 2) + 3)]
            if kind == "q":
                qr = scb(4 * (ctr % 2) + 3, 512)
                normrope(PS[:, bk, :], pk, 4, gq, it, qr, dk, ctr)
                return (kind, qb, it, qr, dk, bk, None)
            kr = scb(4 * (ctr % 2) + 3, 256)
            Bn = normrope(PS[:, bk, 0:256], pk, 2, gk, it, kr, dk, ctr)
            kt_idx = nkt_cache + it
            cp("dve", Vt[:, kt_idx, :], PS[:, bk, 256:512], pk, [("Vt", kt_idx)])
            if not is_sample:
                sq = it % 2
                tt("dve", stg[:, sq, 0:256].rearrange("p (h d) -> p h d", d=128),
                   Bn.rearrange("p (h d) -> p h d", d=128), gk.unsqueeze(1).to_broadcast([128, 2, 128]), ALU.mult,
                   [("sc", 4 * (ctr % 2) + 1), "gk"], [("stg", sq, 0)])
                cp("dve", stg[:, sq, 256:512], PS[:, bk, 256:512], pk, [("stg", sq, 1)])
                si, tt0 = divmod(it * 128, 256)
                S.dma("sp", lambda e, si=si, tt0=tt0, sq=sq: e.dma_start(out=ck_out[si, j, tt0:tt0 + 128, :],
                                                                         in_=stg[:, sq, 0:256]), reads=[("stg", sq, 0)])
                S.dma("sp", lambda e, si=si, tt0=tt0, sq=sq: e.dma_start(out=cv_out[si, j, tt0:tt0 + 128, :],
                                                                         in_=stg[:, sq, 256:512]), reads=[("stg", sq, 1)])
            return (kind, qb, it, kr, dk, bk, kt_idx)

        def stage_b(st):
            kind, qb, it, rr, dk, bk, kt_idx = st
            bt = tr_bank()
            if kind == "q":
                for h in range(4):
                    tr(psb(bt)[:, h * 128:(h + 1) * 128], rr[:, h * 128:(h + 1) * 128], dk, [("ps", bt)],
                       signal=(h == 3))
                cp("act", QT[:, qb * 4:qb * 4 + 4, it * 128:(it + 1) * 128],
                   psb(bt)[:, 0:512].rearrange("p (h t) -> p h t", t=128), [("ps", bt)], [("QT", qb, it)])
            else:
                for h in range(2):
                    tr(psb(bt)[:, h * 128:(h + 1) * 128], rr[:, h * 128:(h + 1) * 128], dk, [("ps", bt)],
                       signal=(h == 1))
                cp("act", KT[:, :, kt_idx * 128:(kt_idx + 1) * 128],
                   psb(bt)[:, 0:256].rearrange("p (h t) -> p h t", t=128), [("ps", bt)], [("KT", kt_idx)])

        stop_items = len(items)
        if stop == 12:
            stop_items = 2 * ntiles
        for ctr, item in enumerate(items[:stop_items]):
            st = stage_a(ctr, item)
            if pend:
                stage_b(pend.pop(0))
            pend.append(st)
        while pend:
            stage_b(pend.pop(0))
        if stop in (12, 13):
            return
        for gb in range(2):
            slot = w_get(("ain", j), 1536 + gb * 512)
            for n in range(4):
                for g in range(G):
                    bk = gen_bank()
                    proj_fm(slot, n, g, bk)
                    act(gT[:, gb * 4 + n, g * 512:(g + 1) * 512], PS[:, bk, :], AF.Silu, [("ps", bk)],
                        [("gT", gb * 4 + n, g)])
        if stop == 14:
            return
        scale = 128.0 ** -0.5
        units = []
        itc = 0
        for (s0, Ls) in seqs:
            kt_lo = 0 if is_sample else s0 // 128
            nkt = (nkt_cache + ntiles) if is_sample else Ls // 128
            for h in range(8):
                for q0 in range(s0, s0 + Ls, 512):
                    nq = min(512, s0 + Ls - q0)
                    for ki in range(nkt):
                        units.append(dict(h=h, q0=q0, nq=nq, ki=ki, nkt=nkt, kt=kt_lo + ki, itc=itc))
                    itc += 1
        LA = 2

        def emit_S(idx, u):
            h, q0, nq, kt = u["h"], u["q0"], u["nq"], u["kt"]
            bS = idx % 3
            mm(PS[:, bS, 0:nq], KT[:, h // 4, kt * 128:(kt + 1) * 128], QT[:, h, q0:q0 + nq], True, True,
               [("KT", kt)] + [("QT", h // 4, t) for t in range(q0 // 128, (q0 + nq) // 128)],
               [("ps", bS)], signal=True)

        def emit_rest(idx, u):
            h, q0, nq, kt, ki, nkt, ic = u["h"], u["q0"], u["nq"], u["kt"], u["ki"], u["nkt"], u["itc"]
            kvh = h // 4
            bS = idx % 3
            pb = idx % 4
            bO = 4 + (ic % 2)
            bD = 6 + (ic % 2)
            act(pT[:, pb, 0:nq], PS[:, bS, 0:nq], AF.Exp, [("ps", bS)], [("pT", pb)], scale=scale)
            mm(PS[:, bO, 0:nq], Vt[:, kt, kvh * 128:(kvh + 1) * 128], pT[:, pb, 0:nq], ki == 0,
               ki == nkt - 1, [("Vt", kt), ("pT", pb)], [("ps", bO)], signal=False)
            mm(PS[:, bD, 0:nq], onesb[:], pT[:, pb, 0:nq], ki == 0, ki == nkt - 1,
               [("pT", pb), "onesb"], [("ps", bD)], signal=True)
            if ki == nkt - 1:
                gg = q0 // 512
                o4 = 2 * (ic % 2)
                R1, R2 = sc(o4, nq), sc(o4 + 1, nq)
                S.op("dve", lambda e, R1=R1, bD=bD, nq=nq: e.reciprocal(R1, PS[:, bD, 0:nq]),
                     [("ps", bD)], [("sc", o4)])
                tt("dve", R2, PS[:, bO, 0:nq], R1, ALU.mult, [("ps", bO), ("sc", o4)], [("sc", o4 + 1)])
                gv = gT[:, h, q0:q0 + nq]
                tt("pool", gv, gv, R2, ALU.mult, [("gT", h, gg), ("sc", o4 + 1)], [("gT", h, gg)])

        dstep = max(1, len(units) // 10)
        for idx in range(len(units) + LA):
            if deferred and idx % dstep == dstep - 1:
                run_deferred(3)
            if idx < len(units):
                emit_S(idx, units[idx])
            if idx >= LA:
                emit_rest(idx - LA, units[idx - LA])

        if stop == 15:
            return

        def zin_fn(kc, it):
            return gT[:, kc, it * 128:(it + 1) * 128]

        def zin_keys(kc, it):
            return [("gT", kc, (it * 128) // 512)]

        out_proj_and_residual(l, tile0, ntiles, v, zin_fn, zin_keys, ("aout", j))
        S.barrier()

    for l in range(nlayers):
        j = l // 2
        if l == 0:
            queue_mod_ss(0)
            while deferred:
                run_deferred(6)
        if l % 2 == 0:
            rec_pass(l, j, 0, 4, 0, [(0, 256), (256, 256)], False)
            rec_pass(l, j, 4, 8, 1, [(0, 1024)], True)
        else:
            att_pass(l, j, 0, 4, 0, [(0, 256), (256, 256)], False)
            att_pass(l, j, 4, 8, 1, [(0, 1024)], True)

    for t in range(12):
        S.dma("sp", lambda e, t=t: e.dma_start(out=y_out[t], in_=X[:, t, :]), reads=[("X", t)])

    if wlist_in is not None:
        S.emit_all()
    es.close()
    return nc, dbg_out, wcollect


def _consts():
    ident = np.eye(128, dtype=np.float32)
    ones = np.ones((128, 128), np.float32)
    s = np.arange(64)[:, None]
    t = np.arange(64)[None, :]
    mk = np.stack([(s <= t), (s >= t)], axis=1).astype(np.float32)
    mk = np.ascontiguousarray(np.tile(mk, (1, 1, 4)))
    rmask = np.ones((128, 512), np.float32)
    rmask[:, ::CH] = 0.0

    def icnt(L):
        tt_ = np.arange(L)
        out = []
        for win in (2, 4, 8, 16):
            lo = np.clip(tt_ - win // 2, 0, L)
            hi = np.clip(tt_ + win // 2, 0, L)
            out.append(1.0 / (hi - lo).astype(np.float32))
        return np.stack(out).astype(np.float32)

    tpos = np.arange(1024)
    row = (tpos // 64).astype(np.float32)
    col = (tpos % 64).astype(np.float32)
    inv = (10000.0 ** (-np.arange(0, 64, 2, dtype=np.float32) / 64)).astype(np.float32)
    ang = np.concatenate([row[:, None] * inv[None, :], col[:, None] * inv[None, :]], axis=-1).astype(np.float32)
    cos_t = np.repeat(np.cos(ang).astype(np.float32), 2, axis=-1).reshape(8, 128, 128)
    sn = np.sin(ang).astype(np.float32)
    sin_t = np.stack([-sn, sn], axis=-1).reshape(8, 128, 128)
    return dict(ident=ident, ones=ones, mk=mk, rmask=rmask, icnt_s=icnt(1024), icnt_p=icnt(256),
                cos_t=cos_t, sin_t=sin_t)


def _prep_inputs(x_prompt, x_sample, c, state_hgrn, cache_k, cache_v, c_ctx, ada_w, ada_b,
                 norm_pre, norm_post, rec_w_in, rec_lb_logits, rec_head_norm, pool_w, pool_scale,
                 rec_w_out, att_w_in, att_q_norm, att_k_norm, att_w_out):
    f = lambda a: np.ascontiguousarray(np.asarray(a, dtype=np.float32))
    shared = dict(
        ada_w=f(ada_w),
        ada_bT=f(np.asarray(ada_b).reshape(4, 24, 128).transpose(2, 0, 1)),
        ada_bg=f(np.asarray(ada_b)[:, 2048:3072]),
        npreT=f(np.asarray(norm_pre).reshape(4, 8, 128).transpose(2, 0, 1)),
        npost=f(norm_post),
        rec_w_in=f(rec_w_in), rec_w_out=f(rec_w_out), att_w_in=f(att_w_in), att_w_out=f(att_w_out),
        lbT=f(np.asarray(rec_lb_logits).reshape(2, 2, 4, 128).transpose(3, 0, 1, 2)),
        hnT=f(np.asarray(rec_head_norm).T),
        pool_w=f(pool_w),
        pscT=f(np.asarray(pool_scale).reshape(2, 4, 128).transpose(2, 0, 1)),
        qn_row=f(att_q_norm), kn_row=f(att_k_norm),
    )
    shared.update(_consts())
    maps = []
    for i in range(NCORES):
        xin = np.concatenate([np.asarray(x_prompt[2 * i]).reshape(2, 128, 1024),
                              np.asarray(x_prompt[2 * i + 1]).reshape(2, 128, 1024),
                              np.asarray(x_sample[i]).reshape(8, 128, 1024)], axis=0)
        cvec = np.stack([np.asarray(c_ctx), np.asarray(c[i])], axis=0)
        cT = cvec.reshape(2, 8, 128).transpose(2, 1, 0)
        st = np.asarray(state_hgrn[i]).transpose(0, 3, 1, 2, 4).reshape(2, 128, 8, 128)
        m = dict(shared)
        m.update(x_in=f(xin), cT=f(cT), st_in=f(st),
                 ck_in=f(np.asarray(cache_k[i]).reshape(2, 512, 256)),
                 cv_in=f(np.asarray(cache_v[i]).reshape(2, 512, 256)))
        maps.append(m)
    return maps


_CACHE = {}


def kernel(**inputs):
    maps = _prep_inputs(**inputs)
    if "nc" not in _CACHE:
        _CACHE["nc"] = build_program()[0]
    nc = _CACHE["nc"]
    res = run_bass_kernel_spmd(nc, maps, core_ids=list(range(NCORES)))
    R = res.results
    y_prompt = np.zeros((16, 256, 1024), np.float32)
    y_sample = np.zeros((8, 1024, 1024), np.float32)
    new_state = np.zeros((16, 2, 2, 4, 128, 128), np.float32)
    new_k = np.zeros((16, 2, 256, 2, 128), np.float32)
    new_v = np.zeros((16, 2, 256, 2, 128), np.float32)
    for i in range(NCORES):
        y = np.asarray(R[i]["y_out"])
        y_prompt[2 * i] = y[0:2].reshape(256, 1024)
        y_prompt[2 * i + 1] = y[2:4].reshape(256, 1024)
        y_sample[i] = y[4:12].reshape(1024, 1024)
        new_state[2 * i:2 * i + 2] = np.asarray(R[i]["st_out"])
        new_k[2 * i:2 * i + 2] = np.asarray(R[i]["ck_out"]).reshape(2, 2, 256, 2, 128)
        new_v[2 * i:2 * i + 2] = np.asarray(R[i]["cv_out"]).reshape(2, 2, 256, 2, 128)
    return (y_prompt, y_sample, new_state, new_k, new_v)
```
